# Optimizing a Trainium2 kernel written in Bass

```python
import math
import jax, jax.numpy as jnp
from jax import lax
import numpy as np

D_MODEL = 1024
BATCH = 8
SEQ = 2048
DEPTH = 2

MLA_HEADS = 8
QK_NOPE_DIM = 64
QK_ROPE_DIM = 32
QK_HEAD_DIM = QK_NOPE_DIM + QK_ROPE_DIM
V_HEAD_DIM = 64
Q_LORA_RANK = 384
KV_LORA_RANK = 256
ROPE_THETA = 10000.0
ATTN_BLOCK = 128
MLA_OUT = MLA_HEADS * V_HEAD_DIM
POOL_WINDOWS = (2, 4, 8, 16)
POOL_GROUP_DIM = 128
POOL_GROUPS = len(POOL_WINDOWS)
POOL_DIM = POOL_GROUPS * POOL_GROUP_DIM
SSD_HEADS = 16
SSD_HEAD_DIM = 64
SSD_INNER = SSD_HEADS * SSD_HEAD_DIM
SSD_GROUPS = 2
SSD_STATE = 128
SSD_CONV = 4
SSD_CHUNK = 128
SSD_CONV_DIM = SSD_INNER + 2 * SSD_GROUPS * SSD_STATE
FFN_DIM = 2816
FFN_CONV = 3
N_BRANCH = 3
BRANCH_DIM = MLA_OUT + POOL_DIM + SSD_INNER
IN_SIZES = (Q_LORA_RANK, KV_LORA_RANK + QK_ROPE_DIM, POOL_DIM, SSD_INNER, SSD_CONV_DIM, SSD_HEADS, N_BRANCH * D_MODEL)
IN_DIM = sum(IN_SIZES)
EPS = 1e-6

kernel_name = 'hybrid_mla_pool_ssd_gated_block'


def _split(x, sizes):
    outs, start = [], 0
    for size in sizes:
        outs.append(x[..., start:start + size])
        start += size
    return outs


def rms_norm(x, w):
    xf = x.astype(jnp.float32)
    y = xf * lax.rsqrt(jnp.mean(xf * xf, axis=-1, keepdims=True) + EPS)
    return (y * w.astype(jnp.float32)).astype(x.dtype)


def modulate(h, shift, scale):
    return h * (1 + scale[:, None, :]) + shift[:, None, :]


def causal_dwconv(x, w, b):
    k = w.shape[0]
    y = lax.conv_general_dilated(x, w[:, None, :].astype(x.dtype), window_strides=(1,), padding=[(k - 1, 0)],
                                 dimension_numbers=('NWC', 'WIO', 'NWC'), feature_group_count=x.shape[-1])
    return y + b.astype(x.dtype)


def rope_tables(positions):
    inv_freq = ROPE_THETA ** (-jnp.arange(0, QK_ROPE_DIM, 2, dtype=jnp.float32) / QK_ROPE_DIM)
    ang = positions.astype(jnp.float32)[..., None] * inv_freq
    return jnp.cos(ang), jnp.sin(ang)


def apply_rope(x, cos, sin):
    half = x.shape[-1] // 2
    x1, x2 = x[..., :half], x[..., half:]
    return jnp.concatenate([x1 * cos - x2 * sin, x2 * cos + x1 * sin], axis=-1).astype(x.dtype)


def mla_branch(q_lat, kv_lat, cos, sin, q_a_norm, w_q_b, kv_a_norm, w_kv_b, q_norm, k_norm):
    b, s, _ = q_lat.shape
    q = (rms_norm(q_lat, q_a_norm) @ w_q_b).reshape(b, s, MLA_HEADS, QK_HEAD_DIM)
    q_nope, q_rope = q[..., :QK_NOPE_DIM], q[..., QK_NOPE_DIM:]
    c_kv, k_rope = kv_lat[..., :KV_LORA_RANK], kv_lat[..., KV_LORA_RANK:]
    kv = (rms_norm(c_kv, kv_a_norm) @ w_kv_b).reshape(b, s, MLA_HEADS, QK_NOPE_DIM + V_HEAD_DIM)
    k_nope, v = kv[..., :QK_NOPE_DIM], kv[..., QK_NOPE_DIM:]
    q_nope = rms_norm(q_nope, q_norm[:QK_NOPE_DIM])
    q_rope = apply_rope(rms_norm(q_rope, q_norm[QK_NOPE_DIM:]), cos[:, :, None], sin[:, :, None])
    k_nope = rms_norm(k_nope, k_norm[:QK_NOPE_DIM])
    k_rope = apply_rope(rms_norm(k_rope, k_norm[QK_NOPE_DIM:]), cos, sin)
    n_blk = s // ATTN_BLOCK
    qn = q_nope.reshape(b, n_blk, ATTN_BLOCK, MLA_HEADS, QK_NOPE_DIM).swapaxes(0, 1)
    qr = q_rope.reshape(b, n_blk, ATTN_BLOCK, MLA_HEADS, QK_ROPE_DIM).swapaxes(0, 1)
    key_pos = jnp.arange(s)
    scale = QK_HEAD_DIM ** -0.5

    def attend(args):
        qn_b, qr_b, blk = args
        sc = jnp.einsum('bqhd,bkhd->bhqk', qn_b, k_nope) + jnp.einsum('bqhd,bkd->bhqk', qr_b, k_rope)
        sc = sc.astype(jnp.float32) * scale
        q_pos = blk * ATTN_BLOCK + jnp.arange(ATTN_BLOCK)
        sc = jnp.where(key_pos[None, :] <= q_pos[:, None], sc, -jnp.inf)
        p = jax.nn.softmax(sc, axis=-1).astype(v.dtype)
        return jnp.einsum('bhqk,bkhd->bqhd', p, v)

    o = lax.map(attend, (qn, qr, jnp.arange(n_blk)))
    return o.swapaxes(0, 1).reshape(b, s, MLA_OUT)


def pool_branch(u, pool_w, pool_scale):
    b, s, _ = u.shape
    uf = u.astype(jnp.float32).reshape(b, s, POOL_GROUPS, POOL_GROUP_DIM)
    cs = jnp.cumsum(uf, axis=1)
    t = jnp.arange(s)
    outs = []
    for g, w in enumerate(POOL_WINDOWS):
        csg = cs[:, :, g]
        lagged = jnp.pad(csg, ((0, 0), (w, 0), (0, 0)))[:, :s]
        count = jnp.minimum(t + 1, w).astype(jnp.float32)[None, :, None]
        outs.append((csg - lagged) / count - uf[:, :, g])
    pooled = jnp.stack(outs, axis=2).astype(u.dtype)
    mixed = jnp.einsum('bsgc,gcd->bsgd', pooled, pool_w)
    return mixed.reshape(b, s, POOL_DIM) * pool_scale


def ssd_branch(z, xbc, dt_raw, conv_w, conv_b, dt_bias, a_log, d_skip, norm_w):
    b, s, _ = z.shape
    nc, lc = s // SSD_CHUNK, SSD_CHUNK
    r = SSD_HEADS // SSD_GROUPS
    xbc = jax.nn.silu(causal_dwconv(xbc, conv_w, conv_b))
    xs, bm, cm = _split(xbc, (SSD_INNER, SSD_GROUPS * SSD_STATE, SSD_GROUPS * SSD_STATE))
    dt = jax.nn.softplus(dt_raw.astype(jnp.float32) + dt_bias.astype(jnp.float32))
    a = -jnp.exp(a_log.astype(jnp.float32))
    x_h = xs.reshape(b, s, SSD_HEADS, SSD_HEAD_DIM)
    xdt = (x_h.astype(jnp.float32) * dt[..., None]).reshape(b, nc, lc, SSD_GROUPS, r, SSD_HEAD_DIM)
    bc = bm.astype(jnp.float32).reshape(b, nc, lc, SSD_GROUPS, SSD_STATE)
    cc = cm.astype(jnp.float32).reshape(b, nc, lc, SSD_GROUPS, SSD_STATE)
    da = (dt * a).reshape(b, nc, lc, SSD_GROUPS, r).transpose(0, 3, 4, 1, 2)
    a_cs = jnp.cumsum(da, axis=-1)
    causal = jnp.tril(jnp.ones((lc, lc), dtype=bool))
    seg = a_cs[..., :, None] - a_cs[..., None, :]
    decay = jnp.exp(jnp.where(causal, seg, -jnp.inf))
    cb = jnp.einsum('bclgn,bcsgn->bcgls', cc, bc)
    y_diag = jnp.einsum('bcgls,bgrcls,bcsgrp->bclgrp', cb, decay, xdt)
    decay_states = jnp.exp(a_cs[..., -1:] - a_cs)
    states = jnp.einsum('bclgn,bgrcl,bclgrp->bcgrpn', bc, decay_states, xdt)
    chunk_decay = jnp.exp(a_cs[..., -1])

    def step(h, inp):
        st, dc = inp
        return h * dc[..., None, None] + st, h

    h0 = jnp.zeros((b, SSD_GROUPS, r, SSD_HEAD_DIM, SSD_STATE), jnp.float32)
    _, prev = lax.scan(step, h0, (states.transpose(1, 0, 2, 3, 4, 5), chunk_decay.transpose(3, 0, 1, 2)))
    y_off = jnp.einsum('bclgn,cbgrpn,bgrcl->bclgrp', cc, prev, jnp.exp(a_cs))
    y = (y_diag + y_off).reshape(b, s, SSD_HEADS, SSD_HEAD_DIM) + x_h * d_skip[:, None]
    gated = y.reshape(b, s, SSD_INNER) * jax.nn.silu(z.astype(jnp.float32))
    gated = gated.reshape(b, s, SSD_GROUPS, SSD_INNER // SSD_GROUPS)
    gated = gated * lax.rsqrt(jnp.mean(gated * gated, axis=-1, keepdims=True) + EPS)
    return (gated.reshape(b, s, SSD_INNER) * norm_w.astype(jnp.float32)).astype(z.dtype)


def setup_inputs(seed: int = 0) -> dict:
    key = jax.random.key(seed)
    ks = iter(jax.random.split(key, 40))
    L, D = DEPTH, D_MODEL

    def nrm(shape, scale):
        return jax.random.normal(next(ks), shape, jnp.float32) * scale

    x = nrm((BATCH, SEQ, D), 1.0)
    c = nrm((BATCH, D), 1.0)
    offset = jax.random.randint(next(ks), (BATCH, 1), 0, 4096, dtype=jnp.int32)
    positions = offset + jnp.arange(SEQ, dtype=jnp.int32)[None, :]
    ada_w = nrm((L, D, 6 * D), D ** -0.5)
    ada_b = nrm((L, 6 * D), 0.02)
    norm1_w = 1.0 + nrm((L, D), 0.02)
    w_in = nrm((L, D, IN_DIM), D ** -0.5)
    q_a_norm = 1.0 + nrm((L, Q_LORA_RANK), 0.02)
    w_q_b = nrm((L, Q_LORA_RANK, MLA_HEADS * QK_HEAD_DIM), Q_LORA_RANK ** -0.5)
    kv_a_norm = 1.0 + nrm((L, KV_LORA_RANK), 0.02)
    w_kv_b = nrm((L, KV_LORA_RANK, MLA_HEADS * (QK_NOPE_DIM + V_HEAD_DIM)), KV_LORA_RANK ** -0.5)
    q_norm = 1.0 + nrm((L, QK_HEAD_DIM), 0.02)
    k_norm = 1.0 + nrm((L, QK_HEAD_DIM), 0.02)
    pool_w = nrm((L, POOL_GROUPS, POOL_GROUP_DIM, POOL_GROUP_DIM), POOL_GROUP_DIM ** -0.5)
    pool_scale = 1.0 + nrm((L, POOL_DIM), 0.1)
    ssd_conv_w = nrm((L, SSD_CONV, SSD_CONV_DIM), SSD_CONV ** -0.5)
    ssd_conv_b = nrm((L, SSD_CONV_DIM), 0.02)
    dt0 = jnp.exp(jax.random.uniform(next(ks), (L, SSD_HEADS), jnp.float32, math.log(1e-3), math.log(1e-1)))
    ssd_dt_bias = dt0 + jnp.log(-jnp.expm1(-dt0))
    ssd_a_log = jnp.log(jax.random.uniform(next(ks), (L, SSD_HEADS), jnp.float32, 1.0, 16.0))
    ssd_d = 1.0 + nrm((L, SSD_HEADS), 0.1)
    ssd_norm_w = 1.0 + nrm((L, SSD_INNER), 0.02)
    w_branch = nrm((L, BRANCH_DIM, D), MLA_OUT ** -0.5)
    w_out = nrm((L, D, D), D ** -0.5)
    norm2_w = 1.0 + nrm((L, D), 0.02)
    ffn_up = nrm((L, D, 2 * FFN_DIM), D ** -0.5)
    ffn_conv_w = nrm((L, FFN_CONV, 2 * FFN_DIM), FFN_CONV ** -0.5)
    ffn_conv_b = nrm((L, 2 * FFN_DIM), 0.02)
    ffn_down = nrm((L, FFN_DIM, D), FFN_DIM ** -0.5)
    return {'x': x, 'c': c, 'positions': positions, 'ada_w': ada_w, 'ada_b': ada_b, 'norm1_w': norm1_w,
            'w_in': w_in, 'q_a_norm': q_a_norm, 'w_q_b': w_q_b, 'kv_a_norm': kv_a_norm, 'w_kv_b': w_kv_b,
            'q_norm': q_norm, 'k_norm': k_norm, 'pool_w': pool_w, 'pool_scale': pool_scale,
            'ssd_conv_w': ssd_conv_w, 'ssd_conv_b': ssd_conv_b, 'ssd_dt_bias': ssd_dt_bias,
            'ssd_a_log': ssd_a_log, 'ssd_d': ssd_d, 'ssd_norm_w': ssd_norm_w, 'w_branch': w_branch,
            'w_out': w_out, 'norm2_w': norm2_w, 'ffn_up': ffn_up, 'ffn_conv_w': ffn_conv_w,
            'ffn_conv_b': ffn_conv_b, 'ffn_down': ffn_down}


def reference(x, c, positions, ada_w, ada_b, norm1_w, w_in, q_a_norm, w_q_b, kv_a_norm, w_kv_b, q_norm, k_norm,
              pool_w, pool_scale, ssd_conv_w, ssd_conv_b, ssd_dt_bias, ssd_a_log, ssd_d, ssd_norm_w, w_branch,
              w_out, norm2_w, ffn_up, ffn_conv_w, ffn_conv_b, ffn_down):
    cos, sin = rope_tables(positions)
    c_act = jax.nn.silu(c)
    for l in range(DEPTH):
        mod = c_act @ ada_w[l] + ada_b[l]
        sh1, sc1, g1, sh2, sc2, g2 = jnp.split(mod, 6, axis=-1)
        h = modulate(rms_norm(x, norm1_w[l]), sh1, sc1)
        proj = h @ w_in[l]
        q_lat, kv_lat, u_pool, z, xbc, dt_raw, gate_logits = _split(proj, IN_SIZES)
        o_a = mla_branch(q_lat, kv_lat, cos, sin, q_a_norm[l], w_q_b[l], kv_a_norm[l], w_kv_b[l],
                         q_norm[l], k_norm[l])
        o_b = pool_branch(u_pool, pool_w[l], pool_scale[l])
        o_c = ssd_branch(z, xbc, dt_raw, ssd_conv_w[l], ssd_conv_b[l], ssd_dt_bias[l], ssd_a_log[l],
                         ssd_d[l], ssd_norm_w[l])
        wb = w_branch[l]
        y_a = o_a @ wb[:MLA_OUT]
        y_b = o_b @ wb[MLA_OUT:MLA_OUT + POOL_DIM]
        y_c = o_c @ wb[MLA_OUT + POOL_DIM:]
        gate_a, gate_b, gate_c = jnp.split(jax.nn.sigmoid(gate_logits), N_BRANCH, axis=-1)
        merged = gate_a * y_a + gate_b * y_b + gate_c * y_c
        x = x + g1[:, None, :] * (merged @ w_out[l])
        h = modulate(rms_norm(x, norm2_w[l]), sh2, sc2)
        up = causal_dwconv(h @ ffn_up[l], ffn_conv_w[l], ffn_conv_b[l])
        u_gate, u_val = jnp.split(up, 2, axis=-1)
        x = x + g2[:, None, :] * ((jax.nn.silu(u_gate) * u_val) @ ffn_down[l])
    return x
```

```python
import math
from contextlib import ExitStack

import numpy as np
import concourse.bass as bass
import concourse.mybir as mybir
from concourse.bass_utils import run_bass_kernel_spmd

F32 = mybir.dt.float32
BF16 = mybir.dt.bfloat16
I32 = mybir.dt.int32
AF = mybir.ActivationFunctionType
ALU = mybir.AluOpType
AX = mybir.AxisListType

S = 2048
D = 1024
NT = 16
NB = 4
L = 2
NH = 8
EPS = 1e-6
FF = 2816
NJ = 22
IN_DIM = 6832
SM_SCALE = 96 ** -0.5
TWO_PI = 2.0 * math.pi


class KB:
    NDSEM = 24

    def __init__(self, nc, stack):
        self.nc = nc
        self.eng = {"pe": nc.tensor, "act": nc.scalar, "dve": nc.vector,
                    "pool": nc.gpsimd, "sp": nc.sync}
        self.sem = {}
        self.cnt = {}
        for e in self.eng:
            self.sem[e] = stack.enter_context(nc.semaphore("s_" + e))
            self.cnt[e] = 0
        self.dsem = [stack.enter_context(nc.semaphore("d%d" % i)) for i in range(self.NDSEM)]
        self.dcnt = [0] * self.NDSEM
        self.dnext = 0
        self.semobj = dict(self.sem)
        for i, s in enumerate(self.dsem):
            self.semobj[("d", i)] = s
        self.waited = {}
        self.last_w = {}
        self.readers = {}
        self.nwaits = 0
        self.nops = 0
        self.npe = 0
        self.marks = []

    def mark(self, name):
        self.marks.append((name, self.npe))

    def _deps(self, reads, writes):
        deps = {}

        def add(d):
            if d is None:
                return
            sk, v = d
            if deps.get(sk, 0) < v:
                deps[sk] = v
        for r in reads:
            add(self.last_w.get(r))
        for w in writes:
            add(self.last_w.get(w))
            for d in self.readers.get(w, ()):
                add(d)
        return deps

    def _emit_waits(self, e, deps, skip_self=False):
        for sk, v in deps.items():
            if skip_self and sk == e:
                continue
            if self.waited.get((e, sk), 0) >= v:
                continue
            self.eng[e].wait_ge(self.semobj[sk], v)
            self.waited[(e, sk)] = v
            self.nwaits += 1

    def _record(self, mark, reads, writes):
        for w in writes:
            self.last_w[w] = mark
            self.readers[w] = []
        for r in reads:
            self.readers.setdefault(r, []).append(mark)

    def group(self, e, fns, reads=(), writes=()):
        deps = self._deps(reads, writes)
        self._emit_waits(e, deps, skip_self=(e == "pe"))
        ins = None
        for fn in fns:
            ins = fn()
        if e == "pe":
            self.npe += len(fns)
        self.cnt[e] += 1
        ins.then_inc(self.sem[e], 1)
        self._record((e, self.cnt[e]), reads, writes)
        self.nops += len(fns)
        return ins

    def op(self, e, fn, reads=(), writes=()):
        return self.group(e, [fn], reads, writes)

    def dma(self, q, out, in_, reads=(), writes=(), **kw):
        deps = self._deps(reads, writes)
        j = self.dnext
        self.dnext = (self.dnext + 1) % self.NDSEM
        if self.dcnt[j] > 0:
            deps[("d", j)] = max(deps.get(("d", j), 0), self.dcnt[j])
        self._emit_waits(q, deps)
        ins = self.eng[q].dma_start(out=out, in_=in_, **kw)
        self.dcnt[j] += 16
        ins.then_inc(self.dsem[j], 16)
        self._record((("d", j), self.dcnt[j]), reads, writes)
        self.nops += 1
        return ins

    def barrier(self, name=None):
        self.mark("barrier")
        for e in self.eng:
            deps = {}
            for f in self.eng:
                if f != e and self.cnt[f] > 0:
                    deps[f] = self.cnt[f]
            for j in range(self.NDSEM):
                if self.dcnt[j] > 0:
                    deps[("d", j)] = self.dcnt[j]
            self._emit_waits(e, deps)
        self.last_w.clear()
        self.readers.clear()

    def finish(self):
        deps = {}
        for f in self.eng:
            if f != "sp" and self.cnt[f] > 0:
                deps[f] = self.cnt[f]
        for j in range(self.NDSEM):
            if self.dcnt[j] > 0:
                deps[("d", j)] = self.dcnt[j]
        self._emit_waits("sp", deps)


VP = {}
_off = 0
for _n, _w in [("ada_b", 48), ("norm1_w", 8), ("norm2_w", 8), ("q_a_norm", 3), ("kv_a_norm", 2),
               ("qwA", 1), ("qwB", 1), ("kwA", 1), ("kwB", 1), ("pool_scale", 4),
               ("ssd_conv_w", 48), ("ssd_conv_b", 12), ("ssd_norm_w", 8),
               ("ffn_conv_w", 132), ("ffn_conv_b", 44)]:
    VP[_n] = (_off, _w)
    _off += _w
NV = _off

C16 = {}
_off = 0
for _n, _w in [("ident", 128), ("ones", 128), ("bd", 128), ("maskneg", 128), ("mask01", 128),
               ("bcur", 512), ("bprev", 512), ("bcur0", 512), ("sel16", 1024), ("maskneg4", 512)]:
    C16[_n] = (_off, _w)
    _off += _w
N16 = _off
C32 = {}
_off = 0
for _n, _w in [("U", 128), ("ones", 128), ("sel", 1024), ("invf", 1), ("sgn", 1), ("ident", 128), ("invf2", 1), ("sgn2", 1)]:
    C32[_n] = (_off, _w)
    _off += _w
N32 = _off


def _kc(w):
    k, n = w.shape
    return np.ascontiguousarray(w.reshape(k // 128, 128, n).transpose(1, 0, 2)).reshape(128, (k // 128) * n)


def _col(v):
    return np.ascontiguousarray(v.reshape(-1, 128).T)


def host_constants():
    c16 = np.zeros((128, N16), np.float32)
    c32 = np.zeros((128, N32), np.float32)
    i = np.arange(128)
    o, w = C16["ident"]; c16[:, o:o + w] = np.eye(128)
    o, w = C16["ones"]; c16[:, o:o + w] = 1.0
    bd = np.zeros((128, 128), np.float32)
    bd[0:64, 0:64] = 1.0 / 64
    bd[64:96, 64:96] = 1.0 / 32
    o, w = C16["bd"]; c16[:, o:o + w] = bd
    o, w = C16["maskneg"]; c16[:, o:o + w] = np.where(i[:, None] > i[None, :], -30000.0, 0.0)
    o, w = C16["mask01"]; c16[:, o:o + w] = (i[:, None] <= i[None, :]).astype(np.float32)
    o, w = C16["maskneg4"]; c16[:, o:o + w] = np.tile(np.where(i[:, None] > i[None, :], -30000.0, 0.0), (1, 4))
    for g, win in enumerate((2, 4, 8, 16)):
        s = i[:, None]
        t = i[None, :]
        cur = np.where((t - s >= 0) & (t - s < win), 1.0 / win, 0.0) - (s == t)
        prev = np.where((t + 128 - s) < win, 1.0 / win, 0.0)
        cnt = np.minimum(t + 1, win).astype(np.float32)
        cur0 = np.where((t - s >= 0) & (t - s < win), 1.0 / cnt, 0.0) - (s == t)
        o, _ = C16["bcur"]; c16[:, o + g * 128:o + (g + 1) * 128] = cur
        o, _ = C16["bprev"]; c16[:, o + g * 128:o + (g + 1) * 128] = prev
        o, _ = C16["bcur0"]; c16[:, o + g * 128:o + (g + 1) * 128] = cur0
    sel16 = np.zeros((128, 8, 128), np.float32)
    for r in range(8):
        sel16[r, r, :] = 1.0
        sel16[32 + r, r, :] = 1.0
    o, w = C16["sel16"]; c16[:, o:o + w] = sel16.reshape(128, 1024)
    o, w = C32["U"]; c32[:, o:o + w] = (i[:, None] <= i[None, :]).astype(np.float32)
    o, w = C32["ones"]; c32[:, o:o + w] = 1.0
    o, w = C32["ident"]; c32[:, o:o + w] = np.eye(128)
    sel = np.zeros((128, 8, 128), np.float32)
    for r in range(8):
        sel[r, r, :] = 1.0
    o, w = C32["sel"]; c32[:, o:o + w] = sel.reshape(128, 1024)
    inv_freq = (10000.0 ** (-np.arange(0, 32, 2, dtype=np.float32) / np.float32(32))).astype(np.float32)
    invf = np.zeros(128, np.float32)
    invf[64:80] = inv_freq
    invf[80:96] = inv_freq
    sgn = np.zeros(128, np.float32)
    sgn[64:80] = -1.0
    sgn[80:96] = 1.0
    c32[:, C32["invf"][0]] = invf
    c32[:, C32["sgn"][0]] = sgn
    c32[:, C32["invf2"][0]] = np.tile(np.concatenate([inv_freq, inv_freq]), 4)
    c32[:, C32["sgn2"][0]] = np.tile(np.concatenate([-np.ones(16, np.float32), np.ones(16, np.float32)]), 4)
    return c16, c32


def host_layout(inp):
    f = lambda a: np.asarray(a, dtype=np.float32)
    w_in = f(inp["w_in"]); w_q_b = f(inp["w_q_b"]); w_kv_b = f(inp["w_kv_b"])
    out = {}
    z64 = np.zeros((1024, 64), np.float32)
    w_attn, w_qb, w_kn, w_v, w_u, w_pool, w_ssd, w_gate, w_br, w_o, w_up, w_dn, w_ada, vp = ([] for _ in range(14))
    for l in range(L):
        wi = w_in[l]
        krA = np.concatenate([z64, wi[:, 640:672]], axis=1)
        krB = np.concatenate([z64, wi[:, 656:672], wi[:, 640:656]], axis=1)
        w_attn.append(_kc(np.concatenate([wi[:, 0:640], krA, krB], axis=1)))
        qb = w_q_b[l]
        z = np.zeros((384, 64), np.float32)
        cols = []
        for h in range(NH):
            cols.append(qb[:, 96 * h:96 * h + 96])
            cols.append(np.concatenate([z, qb[:, 96 * h + 80:96 * h + 96], qb[:, 96 * h + 64:96 * h + 80]], axis=1))
        w_qb.append(_kc(np.concatenate(cols, axis=1)))
        kvb = w_kv_b[l].reshape(256, NH, 128)
        w_kn.append(_kc(np.ascontiguousarray(kvb[:, :, 0:64]).reshape(256, 512)))
        w_v.append(_kc(np.ascontiguousarray(kvb[:, :, 64:128]).reshape(256, 512)))
        w_u.append(_kc(wi[:, 672:1184]))
        w_pool.append(np.ascontiguousarray(f(inp["pool_w"])[l].transpose(1, 0, 2)).reshape(128, 512))
        gs = []
        for g in range(2):
            xb = 2208
            parts = [wi[:, xb + 512 * g + 128 * j: xb + 512 * g + 128 * (j + 1)] for j in range(4)]
            parts.append(wi[:, xb + 1024 + 128 * g: xb + 1024 + 128 * (g + 1)])
            parts.append(wi[:, xb + 1280 + 128 * g: xb + 1280 + 128 * (g + 1)])
            parts.append(wi[:, 1184 + 512 * g:1184 + 512 * (g + 1)])
            parts.append(wi[:, 3744 + 8 * g:3744 + 8 * (g + 1)])
            gs.append(_kc(np.concatenate(parts, axis=1)))
        w_ssd.append(np.stack(gs))
        gm = []
        bm = []
        for m in range(8):
            gm.append(_kc(np.concatenate([wi[:, 3760 + x * 1024 + m * 128:3760 + x * 1024 + (m + 1) * 128] for x in range(3)], axis=1)))
            bm.append(_kc(f(inp["w_branch"])[l][:, m * 128:(m + 1) * 128]))
        w_gate.append(np.stack(gm)); w_br.append(np.stack(bm))
        w_o.append(_kc(f(inp["w_out"])[l]))
        fu = f(inp["ffn_up"])[l]
        w_up.append(np.stack([_kc(np.concatenate([fu[:, j * 128:(j + 1) * 128], fu[:, FF + j * 128:FF + (j + 1) * 128]], axis=1)) for j in range(NJ)]))
        w_dn.append(_kc(f(inp["ffn_down"])[l]))
        aw = f(inp["ada_w"])[l]
        w_ada.append(np.stack([_kc(aw[:, i * 1024:(i + 1) * 1024]) for i in range(6)]))
        v = np.zeros((128, NV), np.float32)

        def put(name, arr):
            o, w = VP[name]
            assert arr.shape == (128, w), (name, arr.shape)
            v[:, o:o + w] = arr
        put("ada_b", _col(f(inp["ada_b"])[l]))
        put("norm1_w", _col(f(inp["norm1_w"])[l]))
        put("norm2_w", _col(f(inp["norm2_w"])[l]))
        put("q_a_norm", _col(f(inp["q_a_norm"])[l]))
        put("kv_a_norm", _col(f(inp["kv_a_norm"])[l]))
        for nm, src in (("q", f(inp["q_norm"])[l]), ("k", f(inp["k_norm"])[l])):
            a = np.zeros((128, 1), np.float32)
            b = np.zeros((128, 1), np.float32)
            a[0:96, 0] = src
            b[64:80, 0] = src[80:96]
            b[80:96, 0] = src[64:80]
            put(nm + "wA", a); put(nm + "wB", b)
        put("pool_scale", _col(f(inp["pool_scale"])[l]))
        cw = f(inp["ssd_conv_w"])[l]
        put("ssd_conv_w", np.concatenate([_col(cw[k]) for k in range(4)], axis=1))
        put("ssd_conv_b", _col(f(inp["ssd_conv_b"])[l]))
        put("ssd_norm_w", _col(f(inp["ssd_norm_w"])[l]))
        fw_ = f(inp["ffn_conv_w"])[l]
        put("ffn_conv_w", np.concatenate([_col(fw_[k]) for k in range(3)], axis=1))
        put("ffn_conv_b", _col(f(inp["ffn_conv_b"])[l]))
        vp.append(v)
    st = lambda xs: np.ascontiguousarray(np.stack(xs))
    out["w_attn"] = st(w_attn); out["w_qb"] = st(w_qb); out["w_kn"] = st(w_kn); out["w_v"] = st(w_v)
    out["w_u"] = st(w_u); out["w_pool"] = st(w_pool); out["w_ssd"] = st(w_ssd); out["w_gate"] = st(w_gate)
    out["w_br"] = st(w_br); out["w_o"] = st(w_o); out["w_up"] = st(w_up); out["w_dn"] = st(w_dn)
    out["w_ada"] = st(w_ada); out["vp"] = st(vp)
    out["ada_b"] = np.ascontiguousarray(f(inp["ada_b"]))
    out["ssd_small"] = np.ascontiguousarray(np.concatenate([f(inp["ssd_d"]), f(inp["ssd_dt_bias"]), f(inp["ssd_a_log"])], axis=1))
    out["ssd_nw"] = np.ascontiguousarray(f(inp["ssd_norm_w"]))
    c16, c32 = host_constants()
    out["c16"] = c16; out["c32"] = c32
    return out


SHARED_SHAPES = {
    "w_attn": [L, 128, 8 * 832], "w_qb": [L, 128, 3 * 8 * 192], "w_kn": [L, 128, 1024], "w_v": [L, 128, 1024],
    "w_u": [L, 128, 8 * 512], "w_pool": [L, 128, 512], "w_ssd": [L, 2, 128, 8 * 1288],
    "w_gate": [L, 8, 128, 8 * 384], "w_br": [L, 8, 128, 16 * 128], "w_o": [L, 128, 8 * 1024],
    "w_up": [L, NJ, 128, 8 * 256], "w_dn": [L, 128, NJ * 1024], "w_ada": [L, 6, 128, 8 * 1024],
    "vp": [L, 128, NV], "ada_b": [L, 6144], "ssd_small": [L, 48], "ssd_nw": [L, 1024], "c16": [128, N16], "c32": [128, N32],
}


class Prog:
    def __init__(self, nlayers=L, stop_after=None, dumps=()):
        self.nlayers = nlayers
        self.stop_after = stop_after
        self.dumps = set(dumps)
        self.dump_specs = {}
        nc = bass.Bass("TRN2", target_bir_lowering=False)
        self.nc = nc
        self.d = {}
        for n, shp in SHARED_SHAPES.items():
            self.d[n] = nc.dram_tensor(n, shp, F32, kind="ExternalInput").ap()
        self.d["x"] = nc.dram_tensor("x", [S, D], F32, kind="ExternalInput").ap()
        self.d["cT"] = nc.dram_tensor("cT", [128, 8], F32, kind="ExternalInput").ap()
        self.d["pos"] = nc.dram_tensor("pos", [1, S], I32, kind="ExternalInput").ap()
        self.d["out"] = nc.dram_tensor("out", [S, D], F32, kind="ExternalOutput").ap()
        self.xs = [nc.dram_tensor("xs%d" % i, [S, D], F32).ap() for i in range(3)]
        self.bk = 0
        self.bk5 = 0
        self.uid = 0

    def nb(self):
        b = self.bk
        self.bk = (b + 1) % 7
        return b

    def nb5(self):
        b = self.bk5
        self.bk5 = (b + 1) % 5
        return b

    def sbt(self, st, name, shape, dt):
        self.uid += 1
        return st.enter_context(self.nc.sbuf_tensor("%s_%d" % (name, self.uid), shape, dt))

    def dump(self, name, ap, shape, key, dt=F32):
        if name not in self.dumps:
            return
        dr = self.nc.dram_tensor("dbg_" + name, shape, dt, kind="ExternalOutput").ap()
        self.dump_specs[name] = shape
        self.kb.dma("sp", dr, ap, reads=[key] if not isinstance(key, list) else key)

    def mm(self, out, pairs, bank, reads, start=True):
        nc = self.nc
        n = len(pairs)
        fns = []
        for i, (l_, r_) in enumerate(pairs):
            fns.append(lambda i=i, l_=l_, r_=r_: nc.tensor.matmul(out, lhsT=l_, rhs=r_, start=(start and i == 0), stop=(i == n - 1)))
        self.kb.group("pe", fns, reads=reads, writes=[("ps", bank)])

    def act(self, out, in_, func, reads, writes, bias=0.0, scale=1.0, accum=None):
        nc = self.nc
        if accum is None:
            fn = lambda: nc.scalar.activation(out=out, in_=in_, func=func, bias=bias, scale=scale)
        else:
            fn = lambda: nc.scalar.activation(out=out, in_=in_, func=func, bias=bias, scale=scale, accum_out=accum)
        self.kb.op("act", fn, reads=reads, writes=writes)

    def rstd(self, out, in_, scale, bias, in_keys, wkey):
        self.act(out, in_, AF.Ln, reads=[], writes=list(in_keys) + [wkey], bias=bias, scale=scale)
        self.act(out, out, AF.Exp, reads=[], writes=[wkey], scale=-0.5)

    def run_chains(self, factories, width):
        todo = list(factories)
        free = list(range(width))
        active = []

        def refill():
            while todo and free:
                sl = free.pop(0)
                active.append((todo.pop(0)(sl), sl))
        refill()
        while active:
            for ent in list(active):
                try:
                    next(ent[0])
                except StopIteration:
                    active.remove(ent)
                    free.append(ent[1])
            refill()

    def dve(self, fn, reads, writes):
        self.kb.op("dve", fn, reads=reads, writes=writes)

    def wload(self, dst, src, key):
        self.kb.dma("pool", dst, src, writes=[key])

    def build(self):
        nc = self.nc
        with ExitStack() as st:
            self.kb = KB(nc, st)
            kb = self.kb
            ps_all = st.enter_context(nc.psum_tensor("ps_all", [128, 7 * 512], F32))
            self.ps = [ps_all[:, i * 512:(i + 1) * 512] for i in range(7)]
            self.psb = st.enter_context(nc.psum_tensor("psb", [128, 1024], BF16))
            self.c16 = self.sbt(st, "c16", [128, N16], BF16)
            self.c32 = self.sbt(st, "c32", [128, N32], F32)
            kb.dma("pool", self.c16[:], self.d["c16"], writes=["c16"])
            kb.dma("sp", self.c32[:], self.d["c32"], writes=["c32"])
            self.k16 = lambda n, j=0: self.c16[:, C16[n][0] + j * 128:C16[n][0] + (j + 1) * 128]
            self.k32 = lambda n, j=0: self.c32[:, C32[n][0] + j * 128:C32[n][0] + (j + 1) * 128]
            self.rope_tables(st)
            self.cact(st)
            kb.barrier()
            xin = self.d["x"]
            for l in range(self.nlayers):
                xout1 = self.xs[2 * l % 3]
                xout2 = self.d["out"] if l == self.nlayers - 1 else self.xs[(2 * l + 1) % 3]
                with ExitStack() as lst:
                    self.layer(lst, l, xin, xout1, xout2)
                    kb.barrier()
                xin = xout2
                if self.stop_after is not None and self.stop_after[0] == l:
                    break
            kb.finish()
        return nc

    def rope_tables(self, st):
        nc = self.nc
        kb = self.kb
        self.rope_c = nc.dram_tensor("rope_c", [128, S], F32).ap()
        self.rope_s = nc.dram_tensor("rope_s", [128, S], F32).ap()
        with ExitStack() as t:
            Cs = self.sbt(t, "Ctab", [128, 512], F32)
            Ss = self.sbt(t, "Stab", [128, 512], F32)
            pi_ = self.sbt(t, "posi", [128, 512], I32)
            v = self.sbt(t, "ropev", [128, 512], F32)
            ki = self.sbt(t, "ropeki", [128, 512], I32)
            kf = self.sbt(t, "ropekf", [128, 512], F32)
            w = self.sbt(t, "ropew", [128, 512], F32)
            fill = self.sbt(t, "ropefill", [64, S], F32)
            for q in range(4):
                kb.dma("sp", pi_[q * 32:(q + 1) * 32, :], self.d["pos"][0:1, q * 512:(q + 1) * 512].partition_broadcast(32), writes=[("posi", q)])
            invf = self.c32[:, C32["invf2"][0]:C32["invf2"][0] + 1]
            sgn = self.c32[:, C32["sgn2"][0]:C32["sgn2"][0] + 1]
            self.dve(lambda: nc.vector.memset(fill[:], 0.0), [], ["ropefill"])
            kb.dma("sp", self.rope_s[0:64, :], fill[:], reads=["ropefill"], writes=["fillz0"])
            kb.dma("sp", self.rope_s[96:128, :], fill[0:32, :], reads=["ropefill"], writes=["fillz1"])
            self.dve(lambda: nc.vector.memset(fill[:], 1.0), ["fillz0", "fillz1"], ["ropefill"])
            kb.dma("sp", self.rope_c[0:64, :], fill[:], reads=["ropefill"])
            kb.dma("sp", self.rope_c[96:128, :], fill[0:32, :], reads=["ropefill"])
            self.dve(lambda: nc.vector.tensor_copy(out=v[:], in_=pi_[:]), [("posi", q) for q in range(4)], ["ropev"])
            self.dve(lambda: nc.vector.tensor_scalar(out=v[:], in0=v[:], scalar1=invf, scalar2=1.0 / TWO_PI, op0=ALU.mult, op1=ALU.mult), ["ropev", "c32"], ["ropev"])
            for which, shift, dst in (("c", 0.25, Cs), ("s", 0.0, Ss)):
                self.dve(lambda shift=shift: nc.vector.tensor_scalar(out=w[:], in0=v[:], scalar1=shift, scalar2=None, op0=ALU.add), ["ropev"], ["ropew"])
                self.dve(lambda: nc.vector.tensor_copy(out=ki[:], in_=w[:]), ["ropew"], ["ropeki"])
                self.dve(lambda: nc.vector.tensor_copy(out=kf[:], in_=ki[:]), ["ropeki"], ["ropekf"])
                self.dve(lambda: nc.vector.tensor_sub(out=w[:], in0=w[:], in1=kf[:]), ["ropekf", "ropew"], ["ropew"])
                self.dve(lambda: nc.vector.tensor_single_scalar(out=kf[:], in_=w[:], scalar=0.5, op=ALU.is_gt), ["ropew"], ["ropekf"])
                self.dve(lambda: nc.vector.tensor_sub(out=w[:], in0=w[:], in1=kf[:]), ["ropekf", "ropew"], ["ropew"])
                self.dve(lambda: nc.vector.tensor_single_scalar(out=kf[:], in_=w[:], scalar=-0.5, op=ALU.is_lt), ["ropew"], ["ropekf"])
                self.dve(lambda: nc.vector.tensor_add(out=w[:], in0=w[:], in1=kf[:]), ["ropekf", "ropew"], ["ropew"])
                self.act(dst[:], w[:], AF.Sin, reads=["ropew"], writes=["tab" + which], scale=TWO_PI)
            self.dve(lambda: nc.vector.tensor_scalar(out=Ss[:], in0=Ss[:], scalar1=sgn, scalar2=None, op0=ALU.mult), ["tabs", "c32"], ["tabs"])
            for q in range(4):
                kb.dma("sp", self.rope_c[64:96, q * 512:(q + 1) * 512], Cs[q * 32:(q + 1) * 32, :], reads=["tabc"])
                kb.dma("sp", self.rope_s[64:96, q * 512:(q + 1) * 512], Ss[q * 32:(q + 1) * 32, :], reads=["tabs"])
            kb.barrier()

    def cact(self, st):
        nc = self.nc
        kb = self.kb
        cT = self.sbt(st, "cT", [128, 8], F32)
        ca = self.sbt(st, "cact", [128, 8], BF16)
        self.cbc = self.sbt(st, "cbc", [128, 8, 128], BF16)
        kb.dma("sp", cT[:], self.d["cT"], writes=["cT"])
        self.act(ca[:], cT[:], AF.Silu, reads=["cT"], writes=["cact"])
        self.cactb = ca
        self.dve(lambda: nc.vector.tensor_copy(out=self.cbc[:], in_=ca[:].unsqueeze(2).to_broadcast([128, 8, 128])), ["cact"], ["cbc"])

    def layer(self, st, l, xin, xout1, xout2):
        nc = self.nc
        kb = self.kb
        self.l = l
        self.vp = self.sbt(st, "vp", [128, NV], F32)
        kb.dma("sp", self.vp[:], self.d["vp"][l], writes=["vp"])
        self.vpc = lambda n, j=0, w=1: self.vp[:, VP[n][0] + j:VP[n][0] + j + w]
        self.ada(st, l)
        if self.stop_after == (l, "ada"):
            return
        self.hT = self.sbt(st, "hT", [128, 8, S], BF16)
        with ExitStack() as ph:
            self.norm_a(ph, self.load_x(ph, xin, 0), 0)
            for t in range(NT):
                if t + 1 < NT:
                    self.norm_a(ph, self.load_x(ph, xin, t + 1), t + 1)
                self.norm_b(self.s1, self.b1, t, mixed=True)
            kb.barrier()
        self.dump("hT%d" % l, self.hT[:], [128, 8, S], [("hT", b) for b in range(NB)], BF16)
        if self.stop_after == (l, "norm1"):
            return
        with ExitStack() as mix:
            self.mix(mix, l, xin, xout1)
            kb.barrier()
        if self.stop_after is not None and self.stop_after[0] == l and self.stop_after[1] != "ffn":
            return
        with ExitStack() as ph:
            self.ffn(ph, l, xout1, xout2)
            kb.barrier()

    def mix(self, st, l, xin, xout1):
        kb = self.kb
        self.o_a = self.sbt(st, "o_a", [128, 4, S], BF16)
        with ExitStack() as ph:
            self.attention(ph, l)
            kb.barrier()
        self.dump("o_a%d" % l, self.o_a[:], [128, 4, S], "o_a", BF16)
        if self.stop_after == (l, "attn"):
            return
        self.o_c = self.sbt(st, "o_c", [128, 8, S], BF16)
        for g in range(2):
            with ExitStack() as ph:
                self.ssd(ph, l, g)
                kb.barrier()
        self.dump("o_c%d" % l, self.o_c[:], [128, 8, S], "o_c", BF16)
        if self.stop_after == (l, "ssd"):
            return
        self.o_b = self.sbt(st, "o_b", [128, 4, S], BF16)
        self.mergedT = self.sbt(st, "mergedT", [128, 8, S], BF16)
        self.wg_pre = self.sbt(st, "wgate", [128, 8, 3, 128], BF16)
        self.wb_pre = self.sbt(st, "wbr", [128, 16, 128], BF16)
        with ExitStack() as ph:
            self.pool_branch(ph, l)
            kb.barrier()
        self.dump("o_b%d" % l, self.o_b[:], [128, 4, S], "o_b", BF16)
        if self.stop_after == (l, "pool"):
            return
        self.wo = self.sbt(st, "wo", [128, 8, D], BF16)
        with ExitStack() as ph:
            self.merge(ph, l)
            kb.barrier()
        self.dump("merged%d" % l, self.mergedT[:], [128, 8, S], "merged", BF16)
        if self.stop_after == (l, "merge"):
            return
        with ExitStack() as ph:
            self.wout_phase(ph, l, xin, xout1)
            kb.barrier()
        self.dump("h2T%d" % l, self.hT[:], [128, 8, S], [("hT", b) for b in range(NB)], BF16)

    def ada(self, st, l):
        nc = self.nc
        kb = self.kb
        kb.mark("ada")
        self.modp = self.sbt(st, "modp", [128, 6, 8], F32)
        self.gbc = self.sbt(st, "gbc", [128, 2, D], F32)
        self.s1 = self.sbt(st, "s1", [128, 8], F32)
        self.s2 = self.sbt(st, "s2", [128, 8], F32)
        with ExitStack() as ph:
            slots = [self.sbt(ph, "adaw", [128, 8, 1024], BF16) for _ in range(2)]
            abc = self.sbt(ph, "adab_bc", [128, 2, D], F32)
            for gi, pc in enumerate((2, 5)):
                kb.dma("sp", abc[:, gi, :], self.d["ada_b"][l:l + 1, pc * 1024:(pc + 1) * 1024].partition_broadcast(128), writes=[("abc", gi)])
            for i in range(6):
                sl = slots[i % 2]
                key = ("adaw", i % 2)
                self.wload(sl[:], self.d["w_ada"][l, i].rearrange("p (k n) -> p k n", k=8), key)
                if i in (2, 5):
                    gi = 0 if i == 2 else 1
                    for half in range(2):
                        b = self.nb()
                        self.mm(self.ps[b], [(self.cbc[:, kc, :], sl[:, kc, half * 512:(half + 1) * 512]) for kc in range(8)], b, reads=[key, "cbc"])
                        self.dve(lambda b=b, gi=gi, half=half: nc.vector.tensor_add(out=self.gbc[:, gi, half * 512:(half + 1) * 512], in0=self.ps[b], in1=abc[:, gi, half * 512:(half + 1) * 512]),
                                 [("abc", gi)], [("ps", b), ("gbc", gi)])
                else:
                    b = self.nb()
                    fns = []
                    for j in range(8):
                        for kc in range(8):
                            fns.append(lambda j=j, kc=kc, b=b, sl=sl: nc.tensor.matmul(self.ps[b][:, j:j + 1], lhsT=sl[:, kc, j * 128:(j + 1) * 128], rhs=self.cactb[:, kc:kc + 1], start=(kc == 0), stop=(kc == 7)))
                    kb.group("pe", fns, reads=[key, "cact"], writes=[("ps", b)])
                    self.dve(lambda b=b, i=i: nc.vector.tensor_add(out=self.modp[:, i, :], in0=self.ps[b][:, 0:8], in1=self.vpc("ada_b", i * 8, 8)),
                             ["vp"], [("ps", b), ("modp", i)])
            self.dve(lambda: nc.vector.scalar_tensor_tensor(out=self.s1[:], in0=self.modp[:, 1, :], scalar=1.0, in1=self.vpc("norm1_w", 0, 8), op0=ALU.add, op1=ALU.mult), [("modp", 1), "vp"], ["s1"])
            self.dve(lambda: nc.vector.scalar_tensor_tensor(out=self.s2[:], in0=self.modp[:, 4, :], scalar=1.0, in1=self.vpc("norm2_w", 0, 8), op0=ALU.add, op1=ALU.mult), [("modp", 4), "vp"], ["s2"])
            self.b1 = self.modp[:, 0, :]
            self.b2 = self.modp[:, 3, :]
            self.dump("modp%d" % l, self.modp[:], [128, 6, 8], [("modp", i) for i in (0, 1, 3, 4)])
            self.dump("gbc%d" % l, self.gbc[:], [128, 2, D], [("gbc", 0), ("gbc", 1)])
            kb.barrier()

    def load_x(self, ph, xsrc, t):
        if not hasattr(self, "_xbufs") or self._xbufs_ph is not ph:
            self._xbufs = [self.sbt(ph, "xt", [128, D], F32) for _ in range(2)]
            self._xbufs_ph = ph
            self._xi = 0
        i = self._xi
        self._xi = (i + 1) % 2
        xt = self._xbufs[i]
        self.kb.dma("sp", xt[:], xsrc[t * 128:(t + 1) * 128, :], writes=[("xt", i)])
        return (xt, [("xt", i)])

    def norm_a(self, ph, xtk, t):
        nc = self.nc
        xt, xkeys = xtk
        if not hasattr(self, "_nb") or self._nb_ph is not ph:
            self._nb = dict(junk=self.sbt(ph, "njunk", [128, D], BF16),
                            ss=[self.sbt(ph, "nss", [128, 1], F32) for _ in range(2)],
                            xn=[self.sbt(ph, "nxn", [128, D], BF16) for _ in range(2)],
                            tmp=self.sbt(ph, "ntmp", [128, 4, 128], F32))
            self._nb_ph = ph
        i = t % 2
        junk = self._nb["junk"]
        ss = self._nb["ss"][i]
        xn = self._nb["xn"][i]
        self.act(junk[:], xt[:], AF.Square, reads=list(xkeys), writes=["njunk", ("nss", i)], accum=ss[:])
        self.rstd(ss[:], ss[:], 1.0 / D, EPS, [], ("nss", i))
        self.dve(lambda: nc.vector.tensor_scalar(out=xn[:], in0=xt[:], scalar1=ss[:, 0:1], scalar2=None, op0=ALU.mult), list(xkeys) + [("nss", i)], [("nxn", i)])

    def norm_b(self, s_ap, b_ap, t, skey=("s1",), bkey=(("modp", 0),), mixed=False):
        nc = self.nc
        i = t % 2
        xn = self._nb["xn"][i]
        self.kb.group("pe", [lambda kc=kc: nc.tensor.transpose(self.psb[:, kc * 128:(kc + 1) * 128], xn[:, kc * 128:(kc + 1) * 128], self.k16("ident")) for kc in range(8)],
                      reads=[("nxn", i), "c16"], writes=[("ps", 7)])
        nact = 4 if mixed else 8
        for kc in range(nact):
            self.act(self.hT[:, kc, t * 128:(t + 1) * 128], self.psb[:, kc * 128:(kc + 1) * 128], AF.Identity,
                     reads=list(skey) + list(bkey), writes=[("ps", 7), ("hT", t // 4)], scale=s_ap[:, kc:kc + 1], bias=b_ap[:, kc:kc + 1])
        if mixed:
            tmp = self._nb["tmp"]
            pin = self.psb[:, 512:1024].rearrange("p (c n) -> p c n", c=4)
            self.dve(lambda: nc.vector.tensor_tensor(out=tmp[:], in0=pin, in1=s_ap[:, 4:8].unsqueeze(2).to_broadcast([128, 4, 128]), op=ALU.mult),
                     list(skey), [("ps", 7), "ntmp"])
            self.dve(lambda: nc.vector.tensor_tensor(out=self.hT[:, 4:8, t * 128:(t + 1) * 128], in0=tmp[:], in1=b_ap[:, 4:8].unsqueeze(2).to_broadcast([128, 4, 128]), op=ALU.add),
                     list(bkey) + ["ntmp"], [("hT", t // 4)])

    def attention(self, ph, l):
        nc = self.nc
        kb = self.kb
        kb.mark("attention")
        wa = self.sbt(ph, "wattn", [128, 8, 832], BF16)
        wasrc = self.d["w_attn"][l].rearrange("p (k n) -> p k n", k=8)
        self.wload(wa[:, :, 384:640], wasrc[:, :, 384:640], ("wattn", "kv"))
        self.wload(wa[:, :, 640:832], wasrc[:, :, 640:832], ("wattn", "kr"))
        wqb = self.sbt(ph, "wqb", [128, 3, NH, 192], BF16)
        self._late_attn_loads = lambda: (
            self.kb.dma("pool", wa[:, :, 0:384], wasrc[:, :, 0:384], reads=[("ckvn", 0)], writes=[("wattn", "q")]),
            self.kb.dma("pool", wqb[:], self.d["w_qb"][l].rearrange("p (k h n) -> p k h n", k=3, h=NH), reads=[("ckvn", 0)], writes=["wqb"]))
        kT = self.sbt(ph, "kT", [128, NH, S], BF16)
        vext = self.sbt(ph, "vext", [128, NT, NH, 65], BF16)
        self.Ctab = self.sbt(ph, "Ctab", [128, S], F32)
        self.Stab = self.sbt(ph, "Stab", [128, S], F32)
        kb.dma("sp", self.Ctab[:], self.rope_c, writes=["tabc"])
        kb.dma("sp", self.Stab[:], self.rope_s, writes=["tabs"])
        self.dve(lambda: nc.vector.memset(vext[:], 1.0), [], ["vext"])
        ones16 = self.k16("ones")
        bd = self.k16("bd")

        def mkscr(stk, n):
            scr = [self.sbt(stk, "ascr", [128, 512], F32) for _ in range(n)]
            state = {"i": 0}

            def nscr():
                i = state["i"]
                state["i"] = (i + 1) % n
                return scr[i], ("ascr", i)
            return nscr

        with ExitStack() as sa:
            wkn = self.sbt(sa, "wkn", [128, 2, 512], BF16)
            wv = self.sbt(sa, "wv", [128, 2, 512], BF16)
            self.wload(wkn[:], self.d["w_kn"][l].rearrange("p (k n) -> p k n", k=2), "wkn")
            self.wload(wv[:], self.d["w_v"][l].rearrange("p (k n) -> p k n", k=2), "wv")
            ckvn = self.sbt(sa, "ckvn", [128, 2, S], BF16)
            ksq = [self.sbt(sa, "ksq", [64, 512], BF16) for _ in range(3)]
            krs = [self.sbt(sa, "krs", [64, 512], F32) for _ in range(3)]
            sqb = [self.sbt(sa, "asq", [128, 3, 512], BF16) for _ in range(2)]
            nscr = mkscr(sa, 4)
            for b in range(NB):
                bs = slice(b * 512, (b + 1) * 512)
                hk = ("hT", b)
                sq, sqk = sqb[b % 2], ("asq", b % 2)
                banks = []
                for c in range(2):
                    bk = self.nb()
                    banks.append(bk)
                    self.mm(self.ps[bk], [(wa[:, kc, 384 + c * 128:384 + (c + 1) * 128], self.hT[:, kc, bs]) for kc in range(8)], bk, reads=[("wattn", "kv"), hk])
                    self.act(sq[:, c, :], self.ps[bk], AF.Square, reads=[], writes=[("ps", bk), (sqk, c)])
                bss = self.nb()
                self.mm(self.ps[bss], [(ones16, sq[:, c, :]) for c in range(2)], bss, reads=["c16", (sqk, 0), (sqk, 1)])
                rs, rsk = nscr()
                self.rstd(rs[:], self.ps[bss], 1.0 / 256, EPS, [("ps", bss)], rsk)
                for c in range(2):
                    self.dve(lambda c=c, rs=rs, bk=banks[c]: nc.vector.scalar_tensor_tensor(out=ckvn[:, c, bs], in0=self.ps[bk], scalar=self.vpc("kv_a_norm", c), in1=rs[:], op0=ALU.mult, op1=ALU.mult),
                             [rsk, "vp"], [("ps", banks[c]), ("ckvn", b)])
                if b == 0:
                    self._late_attn_loads()
                bA = self.nb()
                self.mm(self.ps[bA][0:96, :], [(wa[:, kc, 640:736], self.hT[:, kc, bs]) for kc in range(8)], bA, reads=[("wattn", "kr"), hk])
                bB = self.nb()
                self.mm(self.ps[bB][0:96, :], [(wa[:, kc, 736:832], self.hT[:, kc, bs]) for kc in range(8)], bB, reads=[("wattn", "kr"), hk])
                self.act(sq[0:96, 2, :], self.ps[bA][0:96, :], AF.Square, reads=[], writes=[("ps", bA), (sqk, 2)])
                bm = self.nb()
                self.mm(self.ps[bm][0:96, :], [(bd[0:96, 0:96], sq[0:96, 2, :])], bm, reads=["c16", (sqk, 2)])
                rs, rsk = nscr()
                self.rstd(rs[0:96, :], self.ps[bm][0:96, :], 1.0, EPS, [("ps", bm)], rsk)
                t1, t1k = nscr()
                t2, t2k = nscr()
                self.dve(lambda t1=t1, bA=bA: nc.vector.scalar_tensor_tensor(out=t1[64:96, :], in0=self.ps[bA][64:96, :], scalar=self.vpc("kwA")[64:96, :], in1=self.Ctab[64:96, bs], op0=ALU.mult, op1=ALU.mult),
                         ["vp", "tabc"], [("ps", bA), t1k])
                self.dve(lambda t2=t2, bB=bB: nc.vector.scalar_tensor_tensor(out=t2[64:96, :], in0=self.ps[bB][64:96, :], scalar=self.vpc("kwB")[64:96, :], in1=self.Stab[64:96, bs], op0=ALU.mult, op1=ALU.mult),
                         ["vp", "tabs"], [("ps", bB), t2k])
                self.dve(lambda t1=t1, t2=t2: nc.vector.tensor_add(out=t1[64:96, :], in0=t1[64:96, :], in1=t2[64:96, :]), [t2k], [t1k])
                self.dve(lambda t1=t1, rs=rs: nc.vector.tensor_mul(out=t1[64:96, :], in0=t1[64:96, :], in1=rs[64:96, :]), [rsk], [t1k])
                self.dve(lambda t1=t1: nc.vector.tensor_copy(out=kT[64:96, :, bs], in_=t1[64:96, :].unsqueeze(1).to_broadcast([32, NH, 512])), [t1k], [("kTr", b)])
                def kchain(h, b=b, bs=bs):
                    def gen(slot):
                        bk, bm = slot, 3 + slot
                        sqh, sqhk = ksq[slot], ("ksq", slot)
                        rs, rsk = krs[slot], ("krs", slot)
                        self.mm(self.ps[bk][0:64, :], [(wkn[:, c, h * 64:(h + 1) * 64], ckvn[:, c, bs]) for c in range(2)], bk, reads=["wkn", ("ckvn", b)])
                        yield
                        self.act(sqh[0:64, :], self.ps[bk][0:64, :], AF.Square, reads=[], writes=[("ps", bk), sqhk])
                        yield
                        self.mm(self.ps[bm][0:64, :], [(bd[0:64, 0:64], sqh[0:64, :])], bm, reads=["c16", sqhk])
                        yield
                        self.act(rs[0:64, :], self.ps[bm][0:64, :], AF.Ln, reads=[], writes=[("ps", bm), rsk], bias=EPS, scale=1.0)
                        yield
                        self.act(rs[0:64, :], rs[0:64, :], AF.Exp, reads=[], writes=[rsk], scale=-0.5)
                        yield
                        self.dve(lambda: nc.vector.scalar_tensor_tensor(out=kT[0:64, h, bs], in0=self.ps[bk][0:64, :], scalar=self.vpc("kwA")[0:64, :], in1=rs[0:64, :], op0=ALU.mult, op1=ALU.mult),
                                 [rsk, "vp"], [("ps", bk), ("kTn", b, h)])
                        yield
                    return gen
                self.run_chains([kchain(h) for h in range(NH)], 3)
                for tt in range(4):
                    t = b * 4 + tt
                    bk = self.nb()
                    self.mm(self.ps[bk], [(ckvn[:, c, t * 128:(t + 1) * 128], wv[:, c, :]) for c in range(2)], bk, reads=["wv", ("ckvn", b)])
                    self.act(vext[:, t, :, 0:64], self.ps[bk].rearrange("p (h d) -> p h d", h=NH), AF.Identity, reads=[], writes=[("ps", bk), "vext"])
            kb.barrier()
        self.dump("kT%d" % l, kT[:], [128, NH, S], [], BF16)
        self.dump("vext%d" % l, vext[:], [128, NT, NH, 65], [], BF16)

        with ExitStack() as sq_:
            sqb = [self.sbt(sq_, "asq", [128, 3, 512], BF16)] * 2
            nscr = mkscr(sq_, 2)
            ql = self.sbt(sq_, "qln", [128, 3, 512], BF16)
            qsq = [self.sbt(sq_, "qsq", [128, 512], BF16) for _ in range(2)]
            qsc = [self.sbt(sq_, "qsc", [128, 512], F32) for _ in range(6)]
            qT = [self.sbt(sq_, "qT", [128, NH, 512], BF16) for _ in range(2)]
            Eb = [self.sbt(sq_, "E", [128, 512], BF16) for _ in range(3)]
            ot = self.sbt(sq_, "otok", [128, 4, 512], BF16)
            rcp = [self.sbt(sq_, "rcp", [128, 4], F32) for _ in range(2)]
            mask01 = self.k16("mask01")
            ei = 0
            for qb_ in range(NB):
                bs = slice(qb_ * 512, (qb_ + 1) * 512)
                hk = ("hT", qb_)
                qlk = "qln"
                qt_, qtk = qT[qb_ % 2], ("qT", qb_ % 2)
                sq, sqk = sqb[0], ("asq", 0)
                banks = []
                for c in range(3):
                    bk = self.nb()
                    banks.append(bk)
                    self.mm(self.ps[bk], [(wa[:, kc, c * 128:(c + 1) * 128], self.hT[:, kc, bs]) for kc in range(8)], bk, reads=[("wattn", "q"), hk])
                    self.act(sq[:, c, :], self.ps[bk], AF.Square, reads=[], writes=[("ps", bk), (sqk, c)])
                bss = self.nb()
                self.mm(self.ps[bss], [(ones16, sq[:, c, :]) for c in range(3)], bss, reads=["c16"] + [(sqk, c) for c in range(3)])
                rs, rsk = nscr()
                self.rstd(rs[:], self.ps[bss], 1.0 / 384, EPS, [("ps", bss)], rsk)
                for c in range(3):
                    self.dve(lambda c=c, rs=rs, bk=banks[c]: nc.vector.scalar_tensor_tensor(out=ql[:, c, :], in0=self.ps[bk], scalar=self.vpc("q_a_norm", c), in1=rs[:], op0=ALU.mult, op1=ALU.mult),
                             [rsk, "vp"], [("ps", banks[c]), (qlk, c)])
                def qchain(h, qb_=qb_, bs=bs, qt_=qt_, qtk=qtk):
                    def gen(slot):
                        bA, bB, bm = 3 * slot, 3 * slot + 1, 3 * slot + 2
                        sqh, sqhk = qsq[slot], ("qsq", slot)
                        rs, rsk = qsc[3 * slot], ("qsc", 3 * slot)
                        t1, t1k = qsc[3 * slot + 1], ("qsc", 3 * slot + 1)
                        t2, t2k = qsc[3 * slot + 2], ("qsc", 3 * slot + 2)
                        self.mm(self.ps[bA][0:96, :], [(wqb[:, c, h, 0:96], ql[:, c, :]) for c in range(3)], bA, reads=["wqb"] + [(qlk, c) for c in range(3)])
                        self.mm(self.ps[bB][0:96, :], [(wqb[:, c, h, 96:192], ql[:, c, :]) for c in range(3)], bB, reads=["wqb"] + [(qlk, c) for c in range(3)])
                        yield
                        self.act(sqh[0:96, :], self.ps[bA][0:96, :], AF.Square, reads=[], writes=[("ps", bA), sqhk])
                        yield
                        self.mm(self.ps[bm][0:96, :], [(bd[0:96, 0:96], sqh[0:96, :])], bm, reads=["c16", sqhk])
                        self.dve(lambda: nc.vector.scalar_tensor_tensor(out=t1[0:96, :], in0=self.ps[bA][0:96, :], scalar=self.vpc("qwA")[0:96, :], in1=self.Ctab[0:96, bs], op0=ALU.mult, op1=ALU.mult),
                                 ["vp", "tabc"], [("ps", bA), t1k])
                        yield
                        self.act(rs[0:96, :], self.ps[bm][0:96, :], AF.Ln, reads=[], writes=[("ps", bm), rsk], bias=EPS / (SM_SCALE ** 2), scale=1.0 / (SM_SCALE ** 2))
                        self.dve(lambda: nc.vector.scalar_tensor_tensor(out=t2[0:96, :], in0=self.ps[bB][0:96, :], scalar=self.vpc("qwB")[0:96, :], in1=self.Stab[0:96, bs], op0=ALU.mult, op1=ALU.mult),
                                 ["vp", "tabs"], [("ps", bB), t2k])
                        yield
                        self.act(rs[0:96, :], rs[0:96, :], AF.Exp, reads=[], writes=[rsk], scale=-0.5)
                        self.dve(lambda: nc.vector.tensor_add(out=t1[0:96, :], in0=t1[0:96, :], in1=t2[0:96, :]), [t2k], [t1k])
                        yield
                        self.dve(lambda: nc.vector.tensor_mul(out=qt_[0:96, h, :], in0=t1[0:96, :], in1=rs[0:96, :]), [rsk, t1k], [(qtk, h)])
                        yield
                    return gen
                self.run_chains([qchain(h) for h in range(NH)], 2)
                if qb_ == 0:
                    self.dump("qT%d" % l, qt_[:], [128, NH, 512], [(qtk, h) for h in range(NH)], BF16)
                otk = "otok"
                LA = 2
                for h in range(NH):
                    bo = 5 + (h % 2)
                    nkt = 4 * qb_ + 4
                    st = {"first": True}
                    pend = {}

                    def score(kt, h=h, qb_=qb_):
                        j0 = max(0, kt - 4 * qb_)
                        cs = slice(j0 * 128, 512)
                        bsT = self.nb5()
                        self.mm(self.ps[bsT][:, cs], [(kT[0:96, h, kt * 128:(kt + 1) * 128], qt_[0:96, h, cs])], bsT, reads=[(qtk, h)])
                        pend[kt] = (bsT, j0, cs)

                    def finish(kt, h=h, qb_=qb_, bo=bo, st=st):
                        nonlocal ei
                        bsT, j0, cs = pend.pop(kt)
                        E, Ek = Eb[ei % 3], ("E", ei % 3)
                        ei += 1
                        self.act(E[:, cs], self.ps[bsT][:, cs], AF.Exp, reads=[], writes=[("ps", bsT), Ek])
                        if kt >= 4 * qb_:
                            self.dve(lambda E=E, j0=j0: nc.vector.tensor_mul(out=E[:, j0 * 128:(j0 + 1) * 128], in0=E[:, j0 * 128:(j0 + 1) * 128], in1=mask01), ["c16"], [Ek])
                        fns = []
                        for j in range(j0, 4):
                            qtile = 4 * qb_ + j
                            st_flag = st["first"]
                            st["first"] = False
                            fns.append(lambda j=j, E=E, kt=kt, st_flag=st_flag, qtile=qtile: nc.tensor.matmul(
                                self.ps[bo][:, j * 65:(j + 1) * 65], lhsT=E[:, j * 128:(j + 1) * 128], rhs=vext[:, kt, h, :],
                                start=st_flag, stop=(kt == qtile), skip_group_check=True))
                        kb.group("pe", fns, reads=[Ek], writes=[("ps", bo)])
                    for kt in range(nkt):
                        score(kt)
                        if kt >= LA:
                            finish(kt - LA)
                    for kt in range(max(0, nkt - LA), nkt):
                        finish(kt)
                    rc, rck = rcp[h % 2], ("rcp", h % 2)
                    pview = self.ps[bo][:, 0:260].rearrange("p (j e) -> p j e", j=4)
                    self.dve(lambda rc=rc, pview=pview: nc.vector.reciprocal(out=rc[:].unsqueeze(2), in_=pview[:, :, 64:65]), [], [("ps", bo), rck])
                    self.dve(lambda rc=rc, pview=pview, h=h: nc.vector.tensor_tensor(out=ot[:, :, h * 64:(h + 1) * 64], in0=pview[:, :, 0:64], in1=rc[:].unsqueeze(2).to_broadcast([128, 4, 64]), op=ALU.mult),
                             [rck], [("ps", bo), (otk, h)])
                for j in range(4):
                    t = 4 * qb_ + j
                    kb.group("pe", [lambda c=c, j=j: nc.tensor.transpose(self.psb[:, c * 128:(c + 1) * 128], ot[:, j, c * 128:(c + 1) * 128], self.k16("ident")) for c in range(4)],
                             reads=[(otk, h) for h in range(NH)] + ["c16"], writes=[("ps", 7)])
                    self.act(self.o_a[:, :, t * 128:(t + 1) * 128], self.psb[:, 0:512].rearrange("p (c n) -> p c n", c=4), AF.Identity, reads=[], writes=[("ps", 7), "o_a"])
            kb.barrier()

    def pool_branch(self, ph, l):
        nc = self.nc
        kb = self.kb
        kb.mark("pool_branch")
        wu = self.sbt(ph, "wu", [128, 8, 512], BF16)
        wp = self.sbt(ph, "wpool", [128, 4, 128], BF16)
        self.wload(wu[:], self.d["w_u"][l].rearrange("p (k n) -> p k n", k=8), "wu")
        self.wload(wp[:], self.d["w_pool"][l].rearrange("p (g n) -> p g n", g=4), "wpool")
        self.wload(self.wg_pre[:], self.d["w_gate"][l, 0].rearrange("p (k x n) -> p k x n", k=8, x=3), ("wgate", 0))
        self.wload(self.wb_pre[:], self.d["w_br"][l, 0].rearrange("p (k n) -> p k n", k=16), ("wbr", 0))
        utok = self.sbt(ph, "utok", [128, NT, 512], BF16)
        pooled = [self.sbt(ph, "pooled", [128, 512], BF16) for _ in range(3)]
        for t in range(NT):
            bk = self.nb()
            self.mm(self.ps[bk], [(self.hT[:, kc, t * 128:(t + 1) * 128], wu[:, kc, :]) for kc in range(8)], bk, reads=["wu", ("hT", t // 4)])
            self.act(utok[:, t, :], self.ps[bk], AF.Identity, reads=[], writes=[("ps", bk), ("utok", t)])
        def pchain(b, g):
            def gen(slot):
                bk, b2 = 2 * slot, 2 * slot + 1
                pl, plk = pooled[slot], ("pooled", slot)
                fns = []
                for tt in range(4):
                    t = 4 * b + tt
                    cur = self.k16("bcur0" if t == 0 else "bcur", g)
                    o_ = self.ps[bk][:, tt * 128:(tt + 1) * 128]
                    if t == 0:
                        fns.append(lambda o_=o_, cur=cur, t=t: nc.tensor.matmul(o_, lhsT=utok[:, t, g * 128:(g + 1) * 128], rhs=cur, start=True, stop=True))
                    else:
                        fns.append(lambda o_=o_, cur=cur, t=t: nc.tensor.matmul(o_, lhsT=utok[:, t, g * 128:(g + 1) * 128], rhs=cur, start=True, stop=False))
                        fns.append(lambda o_=o_, t=t: nc.tensor.matmul(o_, lhsT=utok[:, t - 1, g * 128:(g + 1) * 128], rhs=self.k16("bprev", g), start=False, stop=True))
                kb.group("pe", fns, reads=["c16"] + [("utok", t) for t in range(max(0, 4 * b - 1), 4 * b + 4)], writes=[("ps", bk)])
                yield
                self.dve(lambda: nc.vector.tensor_copy(out=pl[:], in_=self.ps[bk]), [], [("ps", bk), plk])
                yield
                self.mm(self.ps[b2], [(wp[:, g, :], pl[:])], b2, reads=["wpool", plk])
                yield
                self.act(self.o_b[:, g, b * 512:(b + 1) * 512], self.ps[b2], AF.Identity, reads=["vp"], writes=[("ps", b2), "o_b"], scale=self.vpc("pool_scale", g))
                yield
            return gen
        self.run_chains([pchain(b, g) for b in range(NB) for g in range(4)], 3)

    def ssd(self, ph, l, g):
        nc = self.nc
        kb = self.kb
        kb.mark("ssd")
        ws = self.sbt(ph, "wssd", [128, 8, 1288], BF16)
        wssrc = self.d["w_ssd"][l, g].rearrange("p (k n) -> p k n", k=8)
        for j in range(6):
            self.wload(ws[:, :, j * 128:(j + 1) * 128], wssrc[:, :, j * 128:(j + 1) * 128], ("wssd", j))
        self.wload(ws[:, :, 768:1280], wssrc[:, :, 768:1280], ("wssd", "z"))
        self.wload(ws[:, :, 1280:1288], wssrc[:, :, 1280:1288], ("wssd", "dt"))
        sm = self.sbt(ph, "ssdsm", [128, 48], F32)
        kb.dma("sp", sm[:], self.d["ssd_small"][l:l + 1, :].partition_broadcast(128), writes=["ssdsm"])
        abc_ = self.sbt(ph, "ssda", [128, 8], F32)
        self.act(abc_[:], sm[:, 32 + 8 * g:32 + 8 * g + 8], AF.Exp, reads=["ssdsm"], writes=["ssda"])
        self.dve(lambda: nc.vector.tensor_scalar(out=abc_[:], in0=abc_[:], scalar1=-1.0, scalar2=None, op0=ALU.mult), [], ["ssda"])
        Dg = sm[:, 8 * g:8 * g + 8]
        dtb = sm[:, 16 + 8 * g:16 + 8 * g + 8]
        xbc = self.sbt(ph, "xbc", [128, 6, S], BF16)
        zs_all = self.sbt(ph, "zsall", [128, NT, 512], BF16)
        with ExitStack() as s1_:
            raw = self.sbt(s1_, "sraw", [128, 6, 3 + S], BF16)
            cacc = [self.sbt(s1_, "scacc", [128, 512], F32) for _ in range(2)]
            self.dve(lambda: nc.vector.memset(raw[:, :, 0:3], 0.0), [], ["rawpad"])
            for b in range(NB):
                bs = slice(b * 512, (b + 1) * 512)
                for j in range(6):
                    bk = self.nb()
                    self.mm(self.ps[bk], [(ws[:, kc, j * 128:(j + 1) * 128], self.hT[:, kc, bs]) for kc in range(8)], bk, reads=[("wssd", j), ("hT", b)])
                    self.act(raw[:, j, 3 + b * 512:3 + (b + 1) * 512], self.ps[bk], AF.Identity, reads=[], writes=[("ps", bk), ("sraw", j, b)])
                for j in range(6):
                    ch = (4 * g + j) if j < 4 else (8 + g if j == 4 else 10 + g)
                    rd = [("sraw", j, b), "rawpad", "vp"] + ([("sraw", j, b - 1)] if b > 0 else [])
                    ca, ck = cacc[(b * 6 + j) % 2], ("scacc", (b * 6 + j) % 2)
                    rv = lambda k, j=j, b=b: raw[:, j, b * 512 + k:b * 512 + k + 512]
                    self.dve(lambda ca=ca, rv=rv, ch=ch: nc.vector.tensor_scalar(out=ca[:], in0=rv(0), scalar1=self.vpc("ssd_conv_w", 0 * 12 + ch), scalar2=self.vpc("ssd_conv_b", ch), op0=ALU.mult, op1=ALU.add), rd, [ck])
                    for k in range(1, 4):
                        self.dve(lambda ca=ca, rv=rv, ch=ch, k=k: nc.vector.scalar_tensor_tensor(out=ca[:], in0=rv(k), scalar=self.vpc("ssd_conv_w", k * 12 + ch), in1=ca[:], op0=ALU.mult, op1=ALU.add), rd, [ck])
                    self.act(xbc[:, j, bs], ca[:], AF.Silu, reads=[ck], writes=[("xbc", j, b)])
                for tt in range(4):
                    t = 4 * b + tt
                    bk = self.nb()
                    self.mm(self.ps[bk], [(self.hT[:, kc, t * 128:(t + 1) * 128], ws[:, kc, 768:1280]) for kc in range(8)], bk, reads=[("wssd", "z"), ("hT", b)])
                    self.act(zs_all[:, t, :], self.ps[bk], AF.Silu, reads=[], writes=[("ps", bk), ("zsall", t)])

            kb.barrier()
        self.dump("xbc%d_%d" % (l, g), xbc[:], [128, 6, S], [("xbc", j, b) for j in range(6) for b in range(NB)], BF16)
        U32 = self.k32("U")
        ones32 = self.k32("ones")
        ident16 = self.k16("ident")
        maskneg4 = self.c16[:, C16["maskneg4"][0]:C16["maskneg4"][0] + 512]
        ones40 = self.k16("ones")[0:40, :]
        sel16 = self.c16[0:64, C16["sel16"][0]:C16["sel16"][0] + 1024].rearrange("p (r m) -> p r m", r=8)
        prev = self.sbt(ph, "sprev", [128, 512], F32)
        prevb = self.sbt(ph, "sprevb", [128, 512], BF16)
        self.dve(lambda: nc.vector.memset(prev[:], 0.0), [], ["sprev"])
        self.dve(lambda: nc.vector.memset(prevb[:], 0.0), [], ["sprevb"])
        Dm = self.sbt(ph, "sDm", [128, 8, 128], BF16)
        for r in range(8):
            self.dve(lambda r=r: nc.vector.tensor_scalar(out=Dm[:, r, :], in0=ident16, scalar1=Dg[:, r:r + 1], scalar2=None, op0=ALU.mult), ["ssdsm", "c16"], ["sDm"])
        dt_all = self.sbt(ph, "sdtall", [128, NT, 8], F32)
        da_all = self.sbt(ph, "sdaall", [128, NT, 8], F32)
        dae_all = self.sbt(ph, "sdaeall", [128, NT, 2, 32], F32)
        acs_all = self.sbt(ph, "sacsall", [128, NT, 8], F32)
        ea_all = self.sbt(ph, "seaall", [128, NT, 8], F32)
        cd_all = self.sbt(ph, "scdall", [128, NT, 8], F32)
        dsd_all = self.sbt(ph, "sdsdall", [128, NT, 8], F32)
        hl_all = self.sbt(ph, "shlall", [64, NT, 128], BF16)
        nhl_all = self.sbt(ph, "snhlall", [64, NT, 128], BF16)
        hc_q = self.sbt(ph, "shcq", [64, 4, 128], BF16)
        self.dve(lambda: nc.vector.memset(dae_all[:], 0.0), [], ["sdae"])
        self.dve(lambda: nc.vector.memset(hl_all[:], 0.0), [], ["shl"])
        bdt = self.nb()
        fns = []
        for t in range(NT):
            for kc in range(8):
                fns.append(lambda t=t, kc=kc: nc.tensor.matmul(self.ps[bdt][:, t * 8:(t + 1) * 8], lhsT=self.hT[:, kc, t * 128:(t + 1) * 128], rhs=ws[:, kc, 1280:1288], start=(kc == 0), stop=(kc == 7)))
        kb.group("pe", fns, reads=[("wssd", "dt")] + [("hT", b) for b in range(NB)], writes=[("ps", bdt)])
        f2 = lambda a: a.rearrange("p t r -> p (t r)")
        self.dve(lambda: nc.vector.tensor_tensor(out=dt_all[:], in0=self.ps[bdt][:, 0:128].rearrange("p (t r) -> p t r", r=8), in1=dtb.unsqueeze(1).to_broadcast([128, NT, 8]), op=ALU.add), ["ssdsm"], [("ps", bdt), "sdt"])
        self.act(f2(dt_all[:]), f2(dt_all[:]), AF.Exp, reads=[], writes=["sdt"])
        self.act(f2(dt_all[:]), f2(dt_all[:]), AF.Ln, reads=[], writes=["sdt"], bias=1.0)
        self.dve(lambda: nc.vector.tensor_tensor(out=da_all[:], in0=dt_all[:], in1=abc_[:].unsqueeze(1).to_broadcast([128, NT, 8]), op=ALU.mult), ["sdt", "ssda"], ["sda"])
        for hh in range(2):
            self.dve(lambda hh=hh: nc.vector.tensor_copy(out=dae_all[:, :, hh, 0:8], in_=da_all[:]), ["sda"], ["sdae"])
        bcs = self.nb()
        kb.group("pe", [lambda: nc.tensor.matmul(self.ps[bcs][:, 0:128], lhsT=U32, rhs=f2(da_all[:]), start=True, stop=True),
                        lambda: nc.tensor.matmul(self.ps[bcs][:, 128:256], lhsT=ones32, rhs=f2(da_all[:]), start=True, stop=True)],
                 reads=["sda", "c32"], writes=[("ps", bcs)])
        self.act(f2(acs_all[:]), self.ps[bcs][:, 0:128], AF.Identity, reads=[], writes=[("ps", bcs), "sacs"])
        self.act(f2(ea_all[:]), self.ps[bcs][:, 0:128], AF.Exp, reads=[], writes=[("ps", bcs), "sea"])
        self.act(f2(cd_all[:]), self.ps[bcs][:, 128:256], AF.Exp, reads=[], writes=[("ps", bcs), "scd"])
        self.dve(lambda: nc.vector.tensor_sub(out=f2(dsd_all[:]), in0=self.ps[bcs][:, 128:256], in1=f2(acs_all[:])), ["sacs"], [("ps", bcs), "sdsd"])
        self.act(f2(dsd_all[:]), f2(dsd_all[:]), AF.Exp, reads=[], writes=["sdsd"])
        for q4 in range(4):
            bq = self.nb()
            fns = []
            for tt in range(4):
                t = 4 * q4 + tt
                lhs = dae_all[:, t].rearrange("p a b -> p (a b)")[:, 0:40]
                fns.append(lambda tt=tt, lhs=lhs, bq=bq: nc.tensor.matmul(self.ps[bq][0:40, tt * 128:(tt + 1) * 128], lhsT=lhs, rhs=U32, start=True, stop=True))
            kb.group("pe", fns, reads=["sdae", "c32"], writes=[("ps", bq)])
            tsl = slice(4 * q4, 4 * q4 + 4)
            pv = lambda lo, hi, bq=bq: self.ps[bq][lo:hi, :].rearrange("p (t m) -> p t m", t=4)
            self.act(hl_all[0:8, tsl, :], pv(0, 8), AF.Identity, reads=[], writes=[("ps", bq), "shl"])
            self.act(hc_q[32:40, :, :], pv(32, 40), AF.Identity, reads=[], writes=[("ps", bq), "shc"])
            self.dve(lambda tsl=tsl, pv=pv: nc.vector.tensor_sub(out=hl_all[32:40, tsl, :], in0=pv(32, 40), in1=hc_q[32:40, :, :]), ["shc"], [("ps", bq), "shl"])
        self.dve(lambda: nc.vector.tensor_scalar(out=nhl_all[0:40], in0=hl_all[0:40], scalar1=-1.0, scalar2=None, op0=ALU.mult), ["shl"], ["snhl"])

        R = 2

        def rot(name, shape, dt):
            return [self.sbt(ph, name, shape, dt) for _ in range(R)]
        one = lambda name, shape, dt: [self.sbt(ph, name, shape, dt)] * R
        bsel = one("sbsel", [64, 8, 128], BF16)
        xsb = one("sxsb", [128, 512], BF16)
        Eexp = one("sE", [128, 8, 128], BF16)
        Mt = Eexp
        cbT = one("scbT", [128, 128], BF16)
        xdt = one("sxdt", [128, 512], BF16)
        xdt2 = rot("sxdt2", [128, 512], BF16)
        Btok = rot("sBtok", [128, 128], BF16)
        yb = one("sy", [128, 512], F32)
        gt = one("sgt", [128, 512], BF16)
        junk = self.sbt(ph, "sjunk", [128, 512], BF16)
        ssq = one("sssq", [128, 1], F32)
        x3 = lambda a: a.rearrange("p (r d) -> p r d", r=8)
        bc8 = lambda a: a.unsqueeze(2).to_broadcast([128, 8, 64])
        nwbc = self.sbt(ph, "snwbc", [128, 512], F32)
        kb.dma("sp", nwbc[:], self.d["ssd_nw"][l:l + 1, 512 * g:512 * (g + 1)].partition_broadcast(128), writes=["snwbc"])
        psbA = self.ps[6].bitcast(BF16)
        poolA = {"i": 0}
        poolB = {"i": 0}

        def nbA():
            poolA["i"] ^= 1
            return poolA["i"]

        def nbB():
            poolB["i"] ^= 1
            return 2 + poolB["i"]

        SINGLE = {"sbsel", "sxsb", "sE", "sM", "scbT", "sxdt", "sy", "sgt", "sssq"}

        def stageA(t):
            i = t % R
            ts_ = slice(t * 128, (t + 1) * 128)
            b = t // 4
            K = lambda n: ("sE", 0) if n == "sM" else ((n, 0) if n in SINGLE else (n, i))
            by = 4 + i
            kb.group("pe", [lambda j=j: nc.tensor.transpose(psbA[:, j * 128:(j + 1) * 128], xbc[:, j, ts_], ident16) for j in range(5)],
                     reads=[("xbc", j, b) for j in range(5)] + ["c16"], writes=[("ps", 6)])
            bcb = nbA()
            self.mm(self.ps[bcb][:, 0:128], [(xbc[:, 4, ts_], xbc[:, 5, ts_])], bcb, reads=[("xbc", 4, b), ("xbc", 5, b)])
            yield
            self.dve(lambda: nc.vector.tensor_tensor(out=bsel[i][0:40], in0=sel16[0:40], in1=hl_all[0:40, t, :].unsqueeze(1).to_broadcast([40, 8, 128]), op=ALU.mult), ["shl", "c16"], [K("sbsel")])
            yield
            self.act(cbT[i][:], self.ps[bcb][:, 0:128], AF.Identity, reads=[], writes=[("ps", bcb), K("scbT")])
            self.act(xsb[i][:], psbA[:, 0:512], AF.Identity, reads=[], writes=[("ps", 6), K("sxsb")])
            self.act(Btok[i][:], psbA[:, 512:640], AF.Identity, reads=[], writes=[("ps", 6), K("sBtok")])
            yield
            bp = [nbA(), nbA()]
            for half in range(2):
                hsl = slice(half * 4, (half + 1) * 4)
                fns = [lambda hsl=hsl, half=half: nc.tensor.matmul(self.ps[bp[half]], lhsT=ones40, rhs=bsel[i][0:40, hsl, :].rearrange("p r m -> p (r m)"), start=True, stop=False),
                       lambda hsl=hsl, half=half: nc.tensor.matmul(self.ps[bp[half]], lhsT=nhl_all[0:40, t, :], rhs=sel16[0:40, hsl, :].rearrange("p r m -> p (r m)"), start=False, stop=False),
                       lambda half=half: nc.tensor.matmul(self.ps[bp[half]], lhsT=ident16, rhs=maskneg4, start=False, stop=True)]
                kb.group("pe", fns, reads=[K("sbsel"), "snhl", "c16"], writes=[("ps", bp[half])])
                yield
            self.dve(lambda: nc.vector.tensor_tensor(out=x3(xdt[i][:]), in0=x3(xsb[i][:]), in1=bc8(dt_all[:, t, :]), op=ALU.mult), [K("sxsb"), "sdt"], [K("sxdt")])
            yield
            for half in range(2):
                hsl = slice(half * 4, (half + 1) * 4)
                ek = ("sE", half)
                self.act(Eexp[i][:, hsl, :], self.ps[bp[half]].rearrange("p (r m) -> p r m", r=4), AF.Exp, reads=[], writes=[("ps", bp[half]), ek])
                yield
                self.dve(lambda hsl=hsl: nc.vector.tensor_mul(out=Mt[i][:, hsl, :], in0=Eexp[i][:, hsl, :], in1=cbT[i][:].unsqueeze(1).to_broadcast([128, 4, 128])), [K("scbT")], [ek])
                yield
                fns = []
                for r in range(half * 4, half * 4 + 4):
                    fns.append(lambda r=r: nc.tensor.matmul(self.ps[by][:, r * 64:(r + 1) * 64], lhsT=Mt[i][:, r, :], rhs=xdt[i][:, r * 64:(r + 1) * 64], start=True, stop=False))
                    fns.append(lambda r=r: nc.tensor.matmul(self.ps[by][:, r * 64:(r + 1) * 64], lhsT=Dm[:, r, :], rhs=xsb[i][:, r * 64:(r + 1) * 64], start=False, stop=True))
                kb.group("pe", fns, reads=[ek, K("sxdt"), K("sxsb"), "sDm"], writes=[("ps", by)])
                yield
            self.dve(lambda: nc.vector.tensor_tensor(out=x3(xdt2[i][:]), in0=x3(xdt[i][:]), in1=bc8(dsd_all[:, t, :]), op=ALU.mult), [K("sxdt"), "sdsd"], [K("sxdt2")])
            yield

        def stageB(t):
            i = t % R
            ts_ = slice(t * 128, (t + 1) * 128)
            b = t // 4
            K = lambda n: ("sE", 0) if n == "sM" else ((n, 0) if n in SINGLE else (n, i))
            by = 4 + i
            bo = nbB()
            self.mm(self.ps[bo], [(xbc[:, 5, ts_], prevb[:])], bo, reads=[("xbc", 5, b), "sprevb"])
            bst = nbB()
            self.mm(self.ps[bst], [(Btok[i][:], xdt2[i][:])], bst, reads=[K("sBtok"), K("sxdt2")])
            yield
            self.dve(lambda: nc.vector.tensor_tensor(out=x3(prev[:]), in0=x3(prev[:]), in1=bc8(cd_all[:, t, :]), op=ALU.mult), ["scd"], ["sprev"])
            yield
            self.dve(lambda: nc.vector.tensor_add(out=prev[:], in0=prev[:], in1=self.ps[bst]), [], [("ps", bst), "sprev"])
            yield
            self.dve(lambda: nc.vector.tensor_copy(out=prevb[:], in_=prev[:]), ["sprev"], ["sprevb"])
            yield
            self.dve(lambda: nc.vector.tensor_tensor(out=x3(yb[i][:]), in0=x3(self.ps[bo]), in1=bc8(ea_all[:, t, :]), op=ALU.mult), ["sea"], [("ps", bo), K("sy")])
            yield
            self.dve(lambda: nc.vector.tensor_add(out=yb[i][:], in0=yb[i][:], in1=self.ps[by]), [], [("ps", by), K("sy")])
            yield
            self.dve(lambda: nc.vector.tensor_mul(out=yb[i][:], in0=yb[i][:], in1=zs_all[:, t, :]), [], [K("sy")])
            yield
            self.act(junk[:], yb[i][:], AF.Square, reads=[K("sy")], writes=["sjunk", K("sssq")], accum=ssq[i][:])
            self.rstd(ssq[i][:], ssq[i][:], 1.0 / 512, EPS, [], K("sssq"))
            yield
            self.dve(lambda: nc.vector.scalar_tensor_tensor(out=gt[i][:], in0=yb[i][:], scalar=ssq[i][:, 0:1], in1=nwbc[:], op0=ALU.mult, op1=ALU.mult), [K("sy"), K("sssq"), "snwbc"], [K("sgt")])
            yield
            kb.group("pe", [lambda j=j: nc.tensor.transpose(self.psb[:, j * 128:(j + 1) * 128], gt[i][:, j * 128:(j + 1) * 128], ident16) for j in range(4)],
                     reads=[K("sgt"), "c16"], writes=[("ps", 7)])
            yield
            self.act(self.o_c[:, 4 * g:4 * g + 4, ts_], self.psb[:, 0:512].rearrange("p (c n) -> p c n", c=4), AF.Identity, reads=[], writes=[("ps", 7), "o_c"])
            yield

        def interleave(ga, gb, ra=1, rb=1):
            alive = [ga, gb]
            while alive:
                for g_, n_ in ((ga, ra), (gb, rb)):
                    if g_ in alive:
                        for _ in range(n_):
                            try:
                                next(g_)
                            except StopIteration:
                                alive.remove(g_)
                                break

        for _ in stageA(0):
            pass
        for t in range(NT):
            if t + 1 < NT:
                interleave(stageA(t + 1), stageB(t), ra=4, rb=3)
            else:
                for _ in stageB(t):
                    pass

    def merge(self, ph, l):
        nc = self.nc
        kb = self.kb
        kb.mark("merge")
        wg = [self.wg_pre, self.sbt(ph, "wgate", [128, 8, 3, 128], BF16)]
        wb = [self.wb_pre, self.sbt(ph, "wbr", [128, 16, 128], BF16)]
        sig = [self.sbt(ph, "msig", [128, 512], F32) for _ in range(6)]
        acc = [self.sbt(ph, "macc", [128, 512], F32) for _ in range(2)]
        srcs = [(self.o_a, 4, 0, "o_a"), (self.o_b, 4, 4, "o_b"), (self.o_c, 8, 8, "o_c")]
        wo = self.wo
        wosrc = self.d["w_o"][l].rearrange("p (k n) -> p k n", k=8)
        si = 0
        ai = 0
        for m in range(8):
            g_, gk = wg[m % 2], ("wgate", m % 2)
            b_, bk_ = wb[m % 2], ("wbr", m % 2)
            if m > 0:
                self.wload(g_[:], self.d["w_gate"][l, m].rearrange("p (k x n) -> p k x n", k=8, x=3), gk)
                self.wload(b_[:], self.d["w_br"][l, m].rearrange("p (k n) -> p k n", k=16), bk_)
            if m == 1:
                for half in range(2):
                    self.wload(wo[:, :, half * 512:(half + 1) * 512], wosrc[:, :, half * 512:(half + 1) * 512], ("wo", half))
            if m == 6:
                for half in range(2):
                    for kc in range(8):
                        self.dve(lambda kc=kc, half=half: nc.vector.tensor_tensor(out=wo[:, kc, half * 512:(half + 1) * 512], in0=wo[:, kc, half * 512:(half + 1) * 512], in1=self.gbc[:, 0, half * 512:(half + 1) * 512], op=ALU.mult),
                                 [("gbc", 0)], [("wo", half)])
            for b in range(NB):
                bs = slice(b * 512, (b + 1) * 512)
                ac, ack = acc[ai % 2], ("macc", ai % 2)
                ai += 1
                for x, (src, nch, off, skey) in enumerate(srcs):
                    bg = self.nb()
                    self.mm(self.ps[bg], [(g_[:, kc, x, :], self.hT[:, kc, bs]) for kc in range(8)], bg, reads=[gk, ("hT", b)])
                    sg, sgk = sig[si % 6], ("msig", si % 6)
                    si += 1
                    self.act(sg[:], self.ps[bg], AF.Sigmoid, reads=[], writes=[("ps", bg), sgk])
                    by = self.nb()
                    self.mm(self.ps[by], [(b_[:, off + c, :], src[:, c, bs]) for c in range(nch)], by, reads=[bk_, skey])
                    if x == 0:
                        self.dve(lambda ac=ac, sg=sg, by=by: nc.vector.tensor_mul(out=ac[:], in0=sg[:], in1=self.ps[by]), [sgk], [("ps", by), ack])
                    else:
                        self.dve(lambda sg=sg, by=by: nc.vector.tensor_mul(out=sg[:], in0=sg[:], in1=self.ps[by]), [], [("ps", by), sgk])
                        if x == 1:
                            self.dve(lambda ac=ac, sg=sg: nc.vector.tensor_add(out=ac[:], in0=ac[:], in1=sg[:]), [sgk], [ack])
                        else:
                            self.dve(lambda ac=ac, sg=sg, m=m, bs=bs: nc.vector.tensor_add(out=self.mergedT[:, m, bs], in0=ac[:], in1=sg[:]), [sgk, ack], ["merged"])

    def wout_phase(self, ph, l, xin, xout1):
        nc = self.nc
        kb = self.kb
        kb.mark("wout_phase")
        wo = self.wo
        xn = [self.sbt(ph, "xnew", [128, D], F32) for _ in range(2)]

        def compute(t):
            ts_ = slice(t * 128, (t + 1) * 128)
            xt, xkeys = self.load_x(ph, xin, t)
            xo, xok = xn[t % 2], ("xnew", t % 2)
            for half in range(2):
                hs = slice(half * 512, (half + 1) * 512)
                bk = self.nb()
                self.mm(self.ps[bk], [(self.mergedT[:, kc, ts_], wo[:, kc, hs]) for kc in range(8)], bk, reads=[("wo", half), "merged"])
                self.dve(lambda xo=xo, xt=xt, bk=bk, hs=hs: nc.vector.tensor_add(out=xo[:, hs], in0=self.ps[bk], in1=xt[:, hs]), list(xkeys), [("ps", bk), (xok, half)])
            kb.dma("sp", xout1[ts_, :], xo[:], reads=[(xok, 0), (xok, 1)], writes=[("xo1", t)])
            if t == 0:
                self.dump("xo0_%d" % l, xo[:], [128, D], [(xok, 0), (xok, 1)])
                self.dump("wo_%d" % l, wo[:], [128, 8, D], [("wo", 0), ("wo", 1)], BF16)
            self.norm_a(ph, (xo, [(xok, 0), (xok, 1)]), t)
        compute(0)
        for t in range(NT):
            if t + 1 < NT:
                compute(t + 1)
            self.norm_b(self.s2, self.b2, t, skey=("s2",), bkey=(("modp", 3),), mixed=True)

    def ffn(self, ph, l, xmid, xout2):
        nc = self.nc
        kb = self.kb
        kb.mark("ffn")
        actT = self.sbt(ph, "actT", [128, NJ, S], BF16)
        wd0 = self.sbt(ph, "wdn", [128, NJ, 512], BF16)
        wdsrc = self.d["w_dn"][l].rearrange("p (k n) -> p k n", k=NJ)
        with ExitStack() as up:
            wup = [self.sbt(up, "wup", [128, 8, 256], BF16) for _ in range(2)]
            raw = [self.sbt(up, "fraw", [128, 2, 2 + S], BF16) for _ in range(2)]
            dg = [self.sbt(up, "fdiag", [128, 3, 2, 128], BF16) for _ in range(2)]
            sg = [self.sbt(up, "fsil", [128, 512], F32) for _ in range(3)]
            vacc = [self.sbt(up, "fvacc", [128, 512], F32) for _ in range(2)]
            si = 0
            for p_ in range(2):
                self.dve(lambda p_=p_: nc.vector.memset(raw[p_][:, :, 0:2], 0.0), [], [("frawpad", p_)])
            def prep(j):
                p_ = j % 2
                self.wload(wup[p_][:], self.d["w_up"][l, j].rearrange("p (k n) -> p k n", k=8), ("wup", p_))
                if j == 2:
                    self.wload(wd0[:], wdsrc[:, :, 0:512], ("wdn", 0))
                for k in range(3):
                    self.dve(lambda k=k, p_=p_: nc.vector.tensor_scalar(out=dg[p_][:, k, 0, :], in0=self.k16("ident"), scalar1=self.vpc("ffn_conv_w", k * 44 + j), scalar2=None, op0=ALU.mult),
                             ["c16", "vp"], [("fdiag", p_)])

            def proj(step):
                j, b = divmod(step, NB)
                p_ = j % 2
                w_, wk = wup[p_], ("wup", p_)
                bs = slice(b * 512, (b + 1) * 512)
                for x in range(2):
                    bk = 2 * (step % 2) + x
                    self.mm(self.ps[bk], [(w_[:, kc, x * 128:(x + 1) * 128], self.hT[:, kc, bs]) for kc in range(8)], bk, reads=[wk, ("hT", b)])
                    if x == 0:
                        self.act(raw[p_][:, x, 2 + b * 512:2 + (b + 1) * 512], self.ps[bk], AF.Identity, reads=[], writes=[("ps", bk), ("fraw", p_, x, b)])
                    else:
                        self.dve(lambda p_=p_, x=x, b=b, bk=bk: nc.vector.tensor_copy(out=raw[p_][:, x, 2 + b * 512:2 + (b + 1) * 512], in_=self.ps[bk]), [], [("ps", bk), ("fraw", p_, x, b)])

            def conv(step):
                nonlocal si
                j, b = divmod(step, NB)
                p_ = j % 2
                bs = slice(b * 512, (b + 1) * 512)
                bk = 4 + (step % 2)
                rd = [("fdiag", p_), ("fraw", p_, 0, b), ("frawpad", p_)] + ([("fraw", p_, 0, b - 1)] if b > 0 else [])
                self.mm(self.ps[bk], [(dg[p_][:, k, 0, :], raw[p_][:, 0, b * 512 + k:b * 512 + k + 512]) for k in range(3)], bk, reads=rd)
                s_, sk = sg[si % 3], ("fsil", si % 3)
                va, vk = vacc[si % 2], ("fvacc", si % 2)
                si += 1
                self.act(s_[:], self.ps[bk], AF.Silu, reads=["vp"], writes=[("ps", bk), sk], bias=self.vpc("ffn_conv_b", j))
                rdv = [("fraw", p_, 1, b), ("frawpad", p_), "vp"] + ([("fraw", p_, 1, b - 1)] if b > 0 else [])
                rv = lambda k: raw[p_][:, 1, b * 512 + k:b * 512 + k + 512]
                chv = NJ + j
                self.dve(lambda: nc.vector.tensor_scalar(out=va[:], in0=rv(0), scalar1=self.vpc("ffn_conv_w", 0 * 44 + chv), scalar2=self.vpc("ffn_conv_b", chv), op0=ALU.mult, op1=ALU.add), rdv, [vk])
                self.dve(lambda: nc.vector.scalar_tensor_tensor(out=va[:], in0=rv(1), scalar=self.vpc("ffn_conv_w", 1 * 44 + chv), in1=va[:], op0=ALU.mult, op1=ALU.add), rdv, [vk])
                self.dve(lambda: nc.vector.scalar_tensor_tensor(out=va[:], in0=rv(2), scalar=self.vpc("ffn_conv_w", 2 * 44 + chv), in1=va[:], op0=ALU.mult, op1=ALU.add), rdv, [vk])
                self.dve(lambda: nc.vector.tensor_mul(out=actT[:, j, bs], in0=va[:], in1=s_[:]), [sk, vk], [("actT", b)])
            nsteps = NJ * NB
            prep(0)
            proj(0)
            for step in range(nsteps):
                if step + 1 < nsteps:
                    if (step + 1) % NB == 0:
                        prep((step + 1) // NB)
                    proj(step + 1)
                conv(step)
            kb.barrier()
        self.dump("actT%d" % l, actT[:], [128, NJ, S], [("actT", b) for b in range(NB)], BF16)
        with ExitStack() as dn:
            wd = [wd0, self.sbt(dn, "wdn", [128, NJ, 512], BF16)]
            xh = [self.sbt(dn, "xh", [128, 512], F32) for _ in range(2)]
            xn = [self.sbt(dn, "xnew2", [128, 512], F32) for _ in range(2)]
            wsrc = self.d["w_dn"][l].rearrange("p (k n) -> p k n", k=NJ)
            self.wload(wd[1][:], wsrc[:, :, 512:1024], ("wdn", 1))
            ci = 0
            for half in range(2):
                hs = slice(half * 512, (half + 1) * 512)
                for t in range(NT):
                    ts_ = slice(t * 128, (t + 1) * 128)
                    i = ci % 2
                    ci += 1
                    kb.dma("sp", xh[i][:], xmid[ts_, hs], reads=[("xo1", t)] if False else [], writes=[("xh", i)])
                    bk = self.nb()
                    self.mm(self.ps[bk], [(actT[:, j, ts_], wd[half][:, j, :]) for j in range(NJ)], bk, reads=[("wdn", half), ("actT", t // 4)])
                    self.dve(lambda i=i, bk=bk, hs=hs: nc.vector.tensor_mul(out=xn[i][:], in0=self.ps[bk], in1=self.gbc[:, 1, hs]), [("gbc", 1)], [("ps", bk), ("xnew2", i)])
                    self.dve(lambda i=i: nc.vector.tensor_add(out=xn[i][:], in0=xn[i][:], in1=xh[i][:]), [("xh", i)], [("xnew2", i)])
                    kb.dma("sp", xout2[ts_, hs], xn[i][:], reads=[("xnew2", i)], writes=[("xo2", t, half)])
            kb.barrier()


_CACHE = {}


def make_in_maps(inputs, n_cores=8):
    shared = host_layout(inputs)
    x = np.asarray(inputs["x"], np.float32)
    c = np.asarray(inputs["c"], np.float32)
    pos = np.asarray(inputs["positions"], np.int32)
    maps = []
    for b in range(n_cores):
        m = dict(shared)
        m["x"] = np.ascontiguousarray(x[b])
        m["cT"] = np.ascontiguousarray(c[b].reshape(8, 128).T)
        m["pos"] = np.ascontiguousarray(pos[b:b + 1])
        maps.append(m)
    return maps


def kernel(**inputs):
    maps = make_in_maps(inputs, 8)
    prog = Prog()
    nc = prog.build()
    res = run_bass_kernel_spmd(nc, maps, core_ids=list(range(8)))
    out = np.stack([np.asarray(res.results[b]["out"], np.float32).reshape(S, D) for b in range(8)], axis=0)
    return out
```

```python
import math
from contextlib import ExitStack

import numpy as np
import concourse.bass as bass
import concourse.mybir as mybir
from concourse.bass_utils import run_bass_kernel_spmd

F32 = mybir.dt.float32
BF16 = mybir.dt.bfloat16
I32 = mybir.dt.int32
AF = mybir.ActivationFunctionType
ALU = mybir.AluOpType
AX = mybir.AxisListType

S = 2048
D = 1024
NT = 16
NB = 4
L = 2
NH = 8
EPS = 1e-6
FF = 2816
NJ = 22
IN_DIM = 6832
SM_SCALE = 96 ** -0.5
TWO_PI = 2.0 * math.pi


class KB:
    NDSEM = 24

    def __init__(self, nc, stack):
        self.nc = nc
        self.eng = {"pe": nc.tensor, "act": nc.scalar, "dve": nc.vector,
                    "pool": nc.gpsimd, "sp": nc.sync}
        self.sem = {}
        self.cnt = {}
        for e in self.eng:
            self.sem[e] = stack.enter_context(nc.semaphore("s_" + e))
            self.cnt[e] = 0
        self.dsem = [stack.enter_context(nc.semaphore("d%d" % i)) for i in range(self.NDSEM)]
        self.dcnt = [0] * self.NDSEM
        self.dnext = 0
        self.semobj = dict(self.sem)
        for i, s in enumerate(self.dsem):
            self.semobj[("d", i)] = s
        self.waited = {}
        self.last_w = {}
        self.readers = {}
        self.nwaits = 0
        self.nops = 0
        self.npe = 0
        self.marks = []

    def mark(self, name):
        self.marks.append((name, self.npe))

    def _deps(self, reads, writes):
        deps = {}

        def add(d):
            if d is None:
                return
            sk, v = d
            if deps.get(sk, 0) < v:
                deps[sk] = v
        for r in reads:
            add(self.last_w.get(r))
        for w in writes:
            add(self.last_w.get(w))
            for d in self.readers.get(w, ()):
                add(d)
        return deps

    def _emit_waits(self, e, deps, skip_self=False):
        for sk, v in deps.items():
            if skip_self and sk == e:
                continue
            if self.waited.get((e, sk), 0) >= v:
                continue
            self.eng[e].wait_ge(self.semobj[sk], v)
            self.waited[(e, sk)] = v
            self.nwaits += 1

    def _record(self, mark, reads, writes):
        for w in writes:
            self.last_w[w] = mark
            self.readers[w] = []
        for r in reads:
            self.readers.setdefault(r, []).append(mark)

    def group(self, e, fns, reads=(), writes=()):
        deps = self._deps(reads, writes)
        self._emit_waits(e, deps, skip_self=(e == "pe"))
        ins = None
        for fn in fns:
            ins = fn()
        if e == "pe":
            self.npe += len(fns)
        self.cnt[e] += 1
        ins.then_inc(self.sem[e], 1)
        self._record((e, self.cnt[e]), reads, writes)
        self.nops += len(fns)
        return ins

    def op(self, e, fn, reads=(), writes=()):
        return self.group(e, [fn], reads, writes)

    def dma(self, q, out, in_, reads=(), writes=(), **kw):
        deps = self._deps(reads, writes)
        j = self.dnext
        self.dnext = (self.dnext + 1) % self.NDSEM
        if self.dcnt[j] > 0:
            deps[("d", j)] = max(deps.get(("d", j), 0), self.dcnt[j])
        self._emit_waits(q, deps)
        ins = self.eng[q].dma_start(out=out, in_=in_, **kw)
        self.dcnt[j] += 16
        ins.then_inc(self.dsem[j], 16)
        self._record((("d", j), self.dcnt[j]), reads, writes)
        self.nops += 1
        return ins

    def barrier(self, name=None):
        self.mark("barrier")
        for e in self.eng:
            deps = {}
            for f in self.eng:
                if f != e and self.cnt[f] > 0:
                    deps[f] = self.cnt[f]
            for j in range(self.NDSEM):
                if self.dcnt[j] > 0:
                    deps[("d", j)] = self.dcnt[j]
            self._emit_waits(e, deps)
        self.last_w.clear()
        self.readers.clear()

    def finish(self):
        deps = {}
        for f in self.eng:
            if f != "sp" and self.cnt[f] > 0:
                deps[f] = self.cnt[f]
        for j in range(self.NDSEM):
            if self.dcnt[j] > 0:
                deps[("d", j)] = self.dcnt[j]
        self._emit_waits("sp", deps)


VP = {}
_off = 0
for _n, _w in [("ada_b", 48), ("norm1_w", 8), ("norm2_w", 8), ("q_a_norm", 3), ("kv_a_norm", 2),
               ("qwA", 1), ("qwB", 1), ("kwA", 1), ("kwB", 1), ("pool_scale", 4),
               ("ssd_conv_w", 48), ("ssd_conv_b", 12), ("ssd_norm_w", 8),
               ("ffn_conv_w", 132), ("ffn_conv_b", 44)]:
    VP[_n] = (_off, _w)
    _off += _w
NV = _off

C16 = {}
_off = 0
for _n, _w in [("ident", 128), ("ones", 128), ("bd", 128), ("maskneg", 128), ("mask01", 128),
               ("bcur", 512), ("bprev", 512), ("bcur0", 512), ("sel16", 1024), ("maskneg4", 512)]:
    C16[_n] = (_off, _w)
    _off += _w
N16 = _off
C32 = {}
_off = 0
for _n, _w in [("U", 128), ("ones", 128), ("sel", 1024), ("invf", 1), ("sgn", 1), ("ident", 128), ("invf2", 1), ("sgn2", 1)]:
    C32[_n] = (_off, _w)
    _off += _w
N32 = _off


def _kc(w):
    k, n = w.shape
    return np.ascontiguousarray(w.reshape(k // 128, 128, n).transpose(1, 0, 2)).reshape(128, (k // 128) * n)


def _col(v):
    return np.ascontiguousarray(v.reshape(-1, 128).T)


def host_constants():
    c16 = np.zeros((128, N16), np.float32)
    c32 = np.zeros((128, N32), np.float32)
    i = np.arange(128)
    o, w = C16["ident"]; c16[:, o:o + w] = np.eye(128)
    o, w = C16["ones"]; c16[:, o:o + w] = 1.0
    bd = np.zeros((128, 128), np.float32)
    bd[0:64, 0:64] = 1.0 / 64
    bd[64:96, 64:96] = 1.0 / 32
    o, w = C16["bd"]; c16[:, o:o + w] = bd
    o, w = C16["maskneg"]; c16[:, o:o + w] = np.where(i[:, None] > i[None, :], -30000.0, 0.0)
    o, w = C16["mask01"]; c16[:, o:o + w] = (i[:, None] <= i[None, :]).astype(np.float32)
    o, w = C16["maskneg4"]; c16[:, o:o + w] = np.tile(np.where(i[:, None] > i[None, :], -30000.0, 0.0), (1, 4))
    for g, win in enumerate((2, 4, 8, 16)):
        s = i[:, None]
        t = i[None, :]
        cur = np.where((t - s >= 0) & (t - s < win), 1.0 / win, 0.0) - (s == t)
        prev = np.where((t + 128 - s) < win, 1.0 / win, 0.0)
        cnt = np.minimum(t + 1, win).astype(np.float32)
        cur0 = np.where((t - s >= 0) & (t - s < win), 1.0 / cnt, 0.0) - (s == t)
        o, _ = C16["bcur"]; c16[:, o + g * 128:o + (g + 1) * 128] = cur
        o, _ = C16["bprev"]; c16[:, o + g * 128:o + (g + 1) * 128] = prev
        o, _ = C16["bcur0"]; c16[:, o + g * 128:o + (g + 1) * 128] = cur0
    sel16 = np.zeros((128, 8, 128), np.float32)
    for r in range(8):
        sel16[r, r, :] = 1.0
        sel16[32 + r, r, :] = 1.0
    o, w = C16["sel16"]; c16[:, o:o + w] = sel16.reshape(128, 1024)
    o, w = C32["U"]; c32[:, o:o + w] = (i[:, None] <= i[None, :]).astype(np.float32)
    o, w = C32["ones"]; c32[:, o:o + w] = 1.0
    o, w = C32["ident"]; c32[:, o:o + w] = np.eye(128)
    sel = np.zeros((128, 8, 128), np.float32)
    for r in range(8):
        sel[r, r, :] = 1.0
    o, w = C32["sel"]; c32[:, o:o + w] = sel.reshape(128, 1024)
    inv_freq = (10000.0 ** (-np.arange(0, 32, 2, dtype=np.float32) / np.float32(32))).astype(np.float32)
    invf = np.zeros(128, np.float32)
    invf[64:80] = inv_freq
    invf[80:96] = inv_freq
    sgn = np.zeros(128, np.float32)
    sgn[64:80] = -1.0
    sgn[80:96] = 1.0
    c32[:, C32["invf"][0]] = invf
    c32[:, C32["sgn"][0]] = sgn
    c32[:, C32["invf2"][0]] = np.tile(np.concatenate([inv_freq, inv_freq]), 4)
    c32[:, C32["sgn2"][0]] = np.tile(np.concatenate([-np.ones(16, np.float32), np.ones(16, np.float32)]), 4)
    return c16, c32


def host_layout(inp):
    f = lambda a: np.asarray(a, dtype=np.float32)
    w_in = f(inp["w_in"]); w_q_b = f(inp["w_q_b"]); w_kv_b = f(inp["w_kv_b"])
    out = {}
    z64 = np.zeros((1024, 64), np.float32)
    w_attn, w_qb, w_kn, w_v, w_u, w_pool, w_ssd, w_gate, w_br, w_o, w_up, w_dn, w_ada, vp = ([] for _ in range(14))
    for l in range(L):
        wi = w_in[l]
        krA = np.concatenate([z64, wi[:, 640:672]], axis=1)
        krB = np.concatenate([z64, wi[:, 656:672], wi[:, 640:656]], axis=1)
        w_attn.append(_kc(np.concatenate([wi[:, 0:640], krA, krB], axis=1)))
        qb = w_q_b[l]
        z = np.zeros((384, 64), np.float32)
        cols = []
        for h in range(NH):
            cols.append(qb[:, 96 * h:96 * h + 96])
            cols.append(np.concatenate([z, qb[:, 96 * h + 80:96 * h + 96], qb[:, 96 * h + 64:96 * h + 80]], axis=1))
        w_qb.append(_kc(np.concatenate(cols, axis=1)))
        kvb = w_kv_b[l].reshape(256, NH, 128)
        w_kn.append(_kc(np.ascontiguousarray(kvb[:, :, 0:64]).reshape(256, 512)))
        w_v.append(_kc(np.ascontiguousarray(kvb[:, :, 64:128]).reshape(256, 512)))
        w_u.append(_kc(wi[:, 672:1184]))
        w_pool.append(np.ascontiguousarray(f(inp["pool_w"])[l].transpose(1, 0, 2)).reshape(128, 512))
        gs = []
        for g in range(2):
            xb = 2208
            parts = [wi[:, xb + 512 * g + 128 * j: xb + 512 * g + 128 * (j + 1)] for j in range(4)]
            parts.append(wi[:, xb + 1024 + 128 * g: xb + 1024 + 128 * (g + 1)])
            parts.append(wi[:, xb + 1280 + 128 * g: xb + 1280 + 128 * (g + 1)])
            parts.append(wi[:, 1184 + 512 * g:1184 + 512 * (g + 1)])
            parts.append(wi[:, 3744 + 8 * g:3744 + 8 * (g + 1)])
            gs.append(_kc(np.concatenate(parts, axis=1)))
        w_ssd.append(np.stack(gs))
        gm = []
        bm = []
        for m in range(8):
            gm.append(_kc(np.concatenate([wi[:, 3760 + x * 1024 + m * 128:3760 + x * 1024 + (m + 1) * 128] for x in range(3)], axis=1)))
            bm.append(_kc(f(inp["w_branch"])[l][:, m * 128:(m + 1) * 128]))
        w_gate.append(np.stack(gm)); w_br.append(np.stack(bm))
        w_o.append(_kc(f(inp["w_out"])[l]))
        fu = f(inp["ffn_up"])[l]
        w_up.append(np.stack([_kc(np.concatenate([fu[:, j * 128:(j + 1) * 128], fu[:, FF + j * 128:FF + (j + 1) * 128]], axis=1)) for j in range(NJ)]))
        w_dn.append(_kc(f(inp["ffn_down"])[l]))
        aw = f(inp["ada_w"])[l]
        w_ada.append(np.stack([_kc(aw[:, i * 1024:(i + 1) * 1024]) for i in range(6)]))
        v = np.zeros((128, NV), np.float32)

        def put(name, arr):
            o, w = VP[name]
            assert arr.shape == (128, w), (name, arr.shape)
            v[:, o:o + w] = arr
        put("ada_b", _col(f(inp["ada_b"])[l]))
        put("norm1_w", _col(f(inp["norm1_w"])[l]))
        put("norm2_w", _col(f(inp["norm2_w"])[l]))
        put("q_a_norm", _col(f(inp["q_a_norm"])[l]))
        put("kv_a_norm", _col(f(inp["kv_a_norm"])[l]))
        for nm, src in (("q", f(inp["q_norm"])[l]), ("k", f(inp["k_norm"])[l])):
            a = np.zeros((128, 1), np.float32)
            b = np.zeros((128, 1), np.float32)
            a[0:96, 0] = src
            b[64:80, 0] = src[80:96]
            b[80:96, 0] = src[64:80]
            put(nm + "wA", a); put(nm + "wB", b)
        put("pool_scale", _col(f(inp["pool_scale"])[l]))
        cw = f(inp["ssd_conv_w"])[l]
        put("ssd_conv_w", np.concatenate([_col(cw[k]) for k in range(4)], axis=1))
        put("ssd_conv_b", _col(f(inp["ssd_conv_b"])[l]))
        put("ssd_norm_w", _col(f(inp["ssd_norm_w"])[l]))
        fw_ = f(inp["ffn_conv_w"])[l]
        put("ffn_conv_w", np.concatenate([_col(fw_[k]) for k in range(3)], axis=1))
        put("ffn_conv_b", _col(f(inp["ffn_conv_b"])[l]))
        vp.append(v)
    st = lambda xs: np.ascontiguousarray(np.stack(xs))
    out["w_attn"] = st(w_attn); out["w_qb"] = st(w_qb); out["w_kn"] = st(w_kn); out["w_v"] = st(w_v)
    out["w_u"] = st(w_u); out["w_pool"] = st(w_pool); out["w_ssd"] = st(w_ssd); out["w_gate"] = st(w_gate)
    out["w_br"] = st(w_br); out["w_o"] = st(w_o); out["w_up"] = st(w_up); out["w_dn"] = st(w_dn)
    out["w_ada"] = st(w_ada); out["vp"] = st(vp)
    out["ada_b"] = np.ascontiguousarray(f(inp["ada_b"]))
    out["ssd_small"] = np.ascontiguousarray(np.concatenate([f(inp["ssd_d"]), f(inp["ssd_dt_bias"]), f(inp["ssd_a_log"])], axis=1))
    out["ssd_nw"] = np.ascontiguousarray(f(inp["ssd_norm_w"]))
    c16, c32 = host_constants()
    out["c16"] = c16; out["c32"] = c32
    return out


SHARED_SHAPES = {
    "w_attn": [L, 128, 8 * 832], "w_qb": [L, 128, 3 * 8 * 192], "w_kn": [L, 128, 1024], "w_v": [L, 128, 1024],
    "w_u": [L, 128, 8 * 512], "w_pool": [L, 128, 512], "w_ssd": [L, 2, 128, 8 * 1288],
    "w_gate": [L, 8, 128, 8 * 384], "w_br": [L, 8, 128, 16 * 128], "w_o": [L, 128, 8 * 1024],
    "w_up": [L, NJ, 128, 8 * 256], "w_dn": [L, 128, NJ * 1024], "w_ada": [L, 6, 128, 8 * 1024],
    "vp": [L, 128, NV], "ada_b": [L, 6144], "ssd_small": [L, 48], "ssd_nw": [L, 1024], "c16": [128, N16], "c32": [128, N32],
}


class Prog:
    def __init__(self, nlayers=L, stop_after=None, dumps=()):
        self.nlayers = nlayers
        self.stop_after = stop_after
        self.dumps = set(dumps)
        self.dump_specs = {}
        nc = bass.Bass("TRN2", target_bir_lowering=False)
        self.nc = nc
        self.d = {}
        for n, shp in SHARED_SHAPES.items():
            self.d[n] = nc.dram_tensor(n, shp, F32, kind="ExternalInput").ap()
        self.d["x"] = nc.dram_tensor("x", [S, D], F32, kind="ExternalInput").ap()
        self.d["cT"] = nc.dram_tensor("cT", [128, 8], F32, kind="ExternalInput").ap()
        self.d["pos"] = nc.dram_tensor("pos", [1, S], I32, kind="ExternalInput").ap()
        self.d["out"] = nc.dram_tensor("out", [S, D], F32, kind="ExternalOutput").ap()
        self.xs = [nc.dram_tensor("xs%d" % i, [S, D], F32).ap() for i in range(3)]
        self.bk = 0
        self.bk5 = 0
        self.uid = 0

    def nb(self):
        b = self.bk
        self.bk = (b + 1) % 7
        return b

    def nb5(self):
        b = self.bk5
        self.bk5 = (b + 1) % 5
        return b

    def sbt(self, st, name, shape, dt):
        self.uid += 1
        return st.enter_context(self.nc.sbuf_tensor("%s_%d" % (name, self.uid), shape, dt))

    def dump(self, name, ap, shape, key, dt=F32):
        if name not in self.dumps:
            return
        dr = self.nc.dram_tensor("dbg_" + name, shape, dt, kind="ExternalOutput").ap()
        self.dump_specs[name] = shape
        self.kb.dma("sp", dr, ap, reads=[key] if not isinstance(key, list) else key)

    def mm(self, out, pairs, bank, reads, start=True):
        nc = self.nc
        n = len(pairs)
        fns = []
        for i, (l_, r_) in enumerate(pairs):
            fns.append(lambda i=i, l_=l_, r_=r_: nc.tensor.matmul(out, lhsT=l_, rhs=r_, start=(start and i == 0), stop=(i == n - 1)))
        self.kb.group("pe", fns, reads=reads, writes=[("ps", bank)])

    def act(self, out, in_, func, reads, writes, bias=0.0, scale=1.0, accum=None):
        nc = self.nc
        if accum is None:
            fn = lambda: nc.scalar.activation(out=out, in_=in_, func=func, bias=bias, scale=scale)
        else:
            fn = lambda: nc.scalar.activation(out=out, in_=in_, func=func, bias=bias, scale=scale, accum_out=accum)
        self.kb.op("act", fn, reads=reads, writes=writes)

    def rstd(self, out, in_, scale, bias, in_keys, wkey):
        self.act(out, in_, AF.Ln, reads=[], writes=list(in_keys) + [wkey], bias=bias, scale=scale)
        self.act(out, out, AF.Exp, reads=[], writes=[wkey], scale=-0.5)

    def run_chains(self, factories, width):
        todo = list(factories)
        free = list(range(width))
        active = []

        def refill():
            while todo and free:
                sl = free.pop(0)
                active.append((todo.pop(0)(sl), sl))
        refill()
        while active:
            for ent in list(active):
                try:
                    next(ent[0])
                except StopIteration:
                    active.remove(ent)
                    free.append(ent[1])
            refill()

    def dve(self, fn, reads, writes):
        self.kb.op("dve", fn, reads=reads, writes=writes)

    def wload(self, dst, src, key):
        self.kb.dma("pool", dst, src, writes=[key])

    def build(self):
        nc = self.nc
        with ExitStack() as st:
            self.kb = KB(nc, st)
            kb = self.kb
            ps_all = st.enter_context(nc.psum_tensor("ps_all", [128, 7 * 512], F32))
            self.ps = [ps_all[:, i * 512:(i + 1) * 512] for i in range(7)]
            self.psb = st.enter_context(nc.psum_tensor("psb", [128, 1024], BF16))
            self.c16 = self.sbt(st, "c16", [128, N16], BF16)
            self.c32 = self.sbt(st, "c32", [128, N32], F32)
            kb.dma("pool", self.c16[:], self.d["c16"], writes=["c16"])
            kb.dma("sp", self.c32[:], self.d["c32"], writes=["c32"])
            self.k16 = lambda n, j=0: self.c16[:, C16[n][0] + j * 128:C16[n][0] + (j + 1) * 128]
            self.k32 = lambda n, j=0: self.c32[:, C32[n][0] + j * 128:C32[n][0] + (j + 1) * 128]
            self.rope_tables(st)
            self.cact(st)
            kb.barrier()
            xin = self.d["x"]
            for l in range(self.nlayers):
                xout1 = self.xs[2 * l % 3]
                xout2 = self.d["out"] if l == self.nlayers - 1 else self.xs[(2 * l + 1) % 3]
                with ExitStack() as lst:
                    self.layer(lst, l, xin, xout1, xout2)
                    kb.barrier()
                xin = xout2
                if self.stop_after is not None and self.stop_after[0] == l:
                    break
            kb.finish()
        return nc

    def rope_tables(self, st):
        nc = self.nc
        kb = self.kb
        self.rope_c = nc.dram_tensor("rope_c", [128, S], F32).ap()
        self.rope_s = nc.dram_tensor("rope_s", [128, S], F32).ap()
        with ExitStack() as t:
            Cs = self.sbt(t, "Ctab", [128, 512], F32)
            Ss = self.sbt(t, "Stab", [128, 512], F32)
            pi_ = self.sbt(t, "posi", [128, 512], I32)
            v = self.sbt(t, "ropev", [128, 512], F32)
            ki = self.sbt(t, "ropeki", [128, 512], I32)
            kf = self.sbt(t, "ropekf", [128, 512], F32)
            w = self.sbt(t, "ropew", [128, 512], F32)
            fill = self.sbt(t, "ropefill", [64, S], F32)
            for q in range(4):
                kb.dma("sp", pi_[q * 32:(q + 1) * 32, :], self.d["pos"][0:1, q * 512:(q + 1) * 512].partition_broadcast(32), writes=[("posi", q)])
            invf = self.c32[:, C32["invf2"][0]:C32["invf2"][0] + 1]
            sgn = self.c32[:, C32["sgn2"][0]:C32["sgn2"][0] + 1]
            self.dve(lambda: nc.vector.memset(fill[:], 0.0), [], ["ropefill"])
            kb.dma("sp", self.rope_s[0:64, :], fill[:], reads=["ropefill"], writes=["fillz0"])
            kb.dma("sp", self.rope_s[96:128, :], fill[0:32, :], reads=["ropefill"], writes=["fillz1"])
            self.dve(lambda: nc.vector.memset(fill[:], 1.0), ["fillz0", "fillz1"], ["ropefill"])
            kb.dma("sp", self.rope_c[0:64, :], fill[:], reads=["ropefill"])
            kb.dma("sp", self.rope_c[96:128, :], fill[0:32, :], reads=["ropefill"])
            self.dve(lambda: nc.vector.tensor_copy(out=v[:], in_=pi_[:]), [("posi", q) for q in range(4)], ["ropev"])
            self.dve(lambda: nc.vector.tensor_scalar(out=v[:], in0=v[:], scalar1=invf, scalar2=1.0 / TWO_PI, op0=ALU.mult, op1=ALU.mult), ["ropev", "c32"], ["ropev"])
            for which, shift, dst in (("c", 0.25, Cs), ("s", 0.0, Ss)):
                self.dve(lambda shift=shift: nc.vector.tensor_scalar(out=w[:], in0=v[:], scalar1=shift, scalar2=None, op0=ALU.add), ["ropev"], ["ropew"])
                self.dve(lambda: nc.vector.tensor_copy(out=ki[:], in_=w[:]), ["ropew"], ["ropeki"])
                self.dve(lambda: nc.vector.tensor_copy(out=kf[:], in_=ki[:]), ["ropeki"], ["ropekf"])
                self.dve(lambda: nc.vector.tensor_sub(out=w[:], in0=w[:], in1=kf[:]), ["ropekf", "ropew"], ["ropew"])
                self.dve(lambda: nc.vector.tensor_single_scalar(out=kf[:], in_=w[:], scalar=0.5, op=ALU.is_gt), ["ropew"], ["ropekf"])
                self.dve(lambda: nc.vector.tensor_sub(out=w[:], in0=w[:], in1=kf[:]), ["ropekf", "ropew"], ["ropew"])
                self.dve(lambda: nc.vector.tensor_single_scalar(out=kf[:], in_=w[:], scalar=-0.5, op=ALU.is_lt), ["ropew"], ["ropekf"])
                self.dve(lambda: nc.vector.tensor_add(out=w[:], in0=w[:], in1=kf[:]), ["ropekf", "ropew"], ["ropew"])
                self.act(dst[:], w[:], AF.Sin, reads=["ropew"], writes=["tab" + which], scale=TWO_PI)
            self.dve(lambda: nc.vector.tensor_scalar(out=Ss[:], in0=Ss[:], scalar1=sgn, scalar2=None, op0=ALU.mult), ["tabs", "c32"], ["tabs"])
            for q in range(4):
                kb.dma("sp", self.rope_c[64:96, q * 512:(q + 1) * 512], Cs[q * 32:(q + 1) * 32, :], reads=["tabc"])
                kb.dma("sp", self.rope_s[64:96, q * 512:(q + 1) * 512], Ss[q * 32:(q + 1) * 32, :], reads=["tabs"])
            kb.barrier()

    def cact(self, st):
        nc = self.nc
        kb = self.kb
        cT = self.sbt(st, "cT", [128, 8], F32)
        ca = self.sbt(st, "cact", [128, 8], BF16)
        self.cbc = self.sbt(st, "cbc", [128, 8, 128], BF16)
        kb.dma("sp", cT[:], self.d["cT"], writes=["cT"])
        self.act(ca[:], cT[:], AF.Silu, reads=["cT"], writes=["cact"])
        self.cactb = ca
        self.dve(lambda: nc.vector.tensor_copy(out=self.cbc[:], in_=ca[:].unsqueeze(2).to_broadcast([128, 8, 128])), ["cact"], ["cbc"])

    def layer(self, st, l, xin, xout1, xout2):
        nc = self.nc
        kb = self.kb
        self.l = l
        self.vp = self.sbt(st, "vp", [128, NV], F32)
        kb.dma("sp", self.vp[:], self.d["vp"][l], writes=["vp"])
        self.vpc = lambda n, j=0, w=1: self.vp[:, VP[n][0] + j:VP[n][0] + j + w]
        self.ada(st, l)
        if self.stop_after == (l, "ada"):
            return
        self.hT = self.sbt(st, "hT", [128, 8, S], BF16)
        with ExitStack() as ph:
            self.norm_a(ph, self.load_x(ph, xin, 0), 0)
            for t in range(NT):
                if t + 1 < NT:
                    self.norm_a(ph, self.load_x(ph, xin, t + 1), t + 1)
                self.norm_b(self.s1, self.b1, t, mixed=True)
            kb.barrier()
        self.dump("hT%d" % l, self.hT[:], [128, 8, S], [("hT", b) for b in range(NB)], BF16)
        if self.stop_after == (l, "norm1"):
            return
        with ExitStack() as mix:
            self.mix(mix, l, xin, xout1)
            kb.barrier()
        if self.stop_after is not None and self.stop_after[0] == l and self.stop_after[1] != "ffn":
            return
        with ExitStack() as ph:
            self.ffn(ph, l, xout1, xout2)
            kb.barrier()

    def mix(self, st, l, xin, xout1):
        kb = self.kb
        self.o_a = self.sbt(st, "o_a", [128, 4, S], BF16)
        with ExitStack() as ph:
            self.attention(ph, l)
            kb.barrier()
        self.dump("o_a%d" % l, self.o_a[:], [128, 4, S], "o_a", BF16)
        if self.stop_after == (l, "attn"):
            return
        self.o_c = self.sbt(st, "o_c", [128, 8, S], BF16)
        for g in range(2):
            with ExitStack() as ph:
                self.ssd(ph, l, g)
                kb.barrier()
        self.dump("o_c%d" % l, self.o_c[:], [128, 8, S], "o_c", BF16)
        if self.stop_after == (l, "ssd"):
            return
        self.o_b = self.sbt(st, "o_b", [128, 4, S], BF16)
        self.mergedT = self.sbt(st, "mergedT", [128, 8, S], BF16)
        self.wg_pre = self.sbt(st, "wgate", [128, 8, 3, 128], BF16)
        self.wb_pre = self.sbt(st, "wbr", [128, 16, 128], BF16)
        with ExitStack() as ph:
            self.pool_branch(ph, l)
            kb.barrier()
        self.dump("o_b%d" % l, self.o_b[:], [128, 4, S], "o_b", BF16)
        if self.stop_after == (l, "pool"):
            return
        self.wo = self.sbt(st, "wo", [128, 8, D], BF16)
        with ExitStack() as ph:
            self.merge(ph, l)
            kb.barrier()
        self.dump("merged%d" % l, self.mergedT[:], [128, 8, S], "merged", BF16)
        if self.stop_after == (l, "merge"):
            return
        with ExitStack() as ph:
            self.wout_phase(ph, l, xin, xout1)
            kb.barrier()
        self.dump("h2T%d" % l, self.hT[:], [128, 8, S], [("hT", b) for b in range(NB)], BF16)

    def ada(self, st, l):
        nc = self.nc
        kb = self.kb
        kb.mark("ada")
        self.modp = self.sbt(st, "modp", [128, 6, 8], F32)
        self.gbc = self.sbt(st, "gbc", [128, 2, D], F32)
        self.s1 = self.sbt(st, "s1", [128, 8], F32)
        self.s2 = self.sbt(st, "s2", [128, 8], F32)
        with ExitStack() as ph:
            slots = [self.sbt(ph, "adaw", [128, 8, 1024], BF16) for _ in range(2)]
            abc = self.sbt(ph, "adab_bc", [128, 2, D], F32)
            for gi, pc in enumerate((2, 5)):
                kb.dma("sp", abc[:, gi, :], self.d["ada_b"][l:l + 1, pc * 1024:(pc + 1) * 1024].partition_broadcast(128), writes=[("abc", gi)])
            for i in range(6):
                sl = slots[i % 2]
                key = ("adaw", i % 2)
                self.wload(sl[:], self.d["w_ada"][l, i].rearrange("p (k n) -> p k n", k=8), key)
                if i in (2, 5):
                    gi = 0 if i == 2 else 1
                    for half in range(2):
                        b = self.nb()
                        self.mm(self.ps[b], [(self.cbc[:, kc, :], sl[:, kc, half * 512:(half + 1) * 512]) for kc in range(8)], b, reads=[key, "cbc"])
                        self.dve(lambda b=b, gi=gi, half=half: nc.vector.tensor_add(out=self.gbc[:, gi, half * 512:(half + 1) * 512], in0=self.ps[b], in1=abc[:, gi, half * 512:(half + 1) * 512]),
                                 [("abc", gi)], [("ps", b), ("gbc", gi)])
                else:
                    b = self.nb()
                    fns = []
                    for j in range(8):
                        for kc in range(8):
                            fns.append(lambda j=j, kc=kc, b=b, sl=sl: nc.tensor.matmul(self.ps[b][:, j:j + 1], lhsT=sl[:, kc, j * 128:(j + 1) * 128], rhs=self.cactb[:, kc:kc + 1], start=(kc == 0), stop=(kc == 7)))
                    kb.group("pe", fns, reads=[key, "cact"], writes=[("ps", b)])
                    self.dve(lambda b=b, i=i: nc.vector.tensor_add(out=self.modp[:, i, :], in0=self.ps[b][:, 0:8], in1=self.vpc("ada_b", i * 8, 8)),
                             ["vp"], [("ps", b), ("modp", i)])
            self.dve(lambda: nc.vector.scalar_tensor_tensor(out=self.s1[:], in0=self.modp[:, 1, :], scalar=1.0, in1=self.vpc("norm1_w", 0, 8), op0=ALU.add, op1=ALU.mult), [("modp", 1), "vp"], ["s1"])
            self.dve(lambda: nc.vector.scalar_tensor_tensor(out=self.s2[:], in0=self.modp[:, 4, :], scalar=1.0, in1=self.vpc("norm2_w", 0, 8), op0=ALU.add, op1=ALU.mult), [("modp", 4), "vp"], ["s2"])
            self.b1 = self.modp[:, 0, :]
            self.b2 = self.modp[:, 3, :]
            self.dump("modp%d" % l, self.modp[:], [128, 6, 8], [("modp", i) for i in (0, 1, 3, 4)])
            self.dump("gbc%d" % l, self.gbc[:], [128, 2, D], [("gbc", 0), ("gbc", 1)])
            kb.barrier()

    def load_x(self, ph, xsrc, t):
        if not hasattr(self, "_xbufs") or self._xbufs_ph is not ph:
            self._xbufs = [self.sbt(ph, "xt", [128, D], F32) for _ in range(2)]
            self._xbufs_ph = ph
            self._xi = 0
        i = self._xi
        self._xi = (i + 1) % 2
        xt = self._xbufs[i]
        self.kb.dma("sp", xt[:], xsrc[t * 128:(t + 1) * 128, :], writes=[("xt", i)])
        return (xt, [("xt", i)])

    def norm_a(self, ph, xtk, t):
        nc = self.nc
        xt, xkeys = xtk
        if not hasattr(self, "_nb") or self._nb_ph is not ph:
            self._nb = dict(junk=self.sbt(ph, "njunk", [128, D], BF16),
                            ss=[self.sbt(ph, "nss", [128, 1], F32) for _ in range(2)],
                            xn=[self.sbt(ph, "nxn", [128, D], BF16) for _ in range(2)],
                            tmp=self.sbt(ph, "ntmp", [128, 4, 128], F32))
            self._nb_ph = ph
        i = t % 2
        junk = self._nb["junk"]
        ss = self._nb["ss"][i]
        xn = self._nb["xn"][i]
        self.act(junk[:], xt[:], AF.Square, reads=list(xkeys), writes=["njunk", ("nss", i)], accum=ss[:])
        self.rstd(ss[:], ss[:], 1.0 / D, EPS, [], ("nss", i))
        self.dve(lambda: nc.vector.tensor_scalar(out=xn[:], in0=xt[:], scalar1=ss[:, 0:1], scalar2=None, op0=ALU.mult), list(xkeys) + [("nss", i)], [("nxn", i)])

    def norm_b(self, s_ap, b_ap, t, skey=("s1",), bkey=(("modp", 0),), mixed=False):
        nc = self.nc
        i = t % 2
        xn = self._nb["xn"][i]
        self.kb.group("pe", [lambda kc=kc: nc.tensor.transpose(self.psb[:, kc * 128:(kc + 1) * 128], xn[:, kc * 128:(kc + 1) * 128], self.k16("ident")) for kc in range(8)],
                      reads=[("nxn", i), "c16"], writes=[("ps", 7)])
        nact = 4 if mixed else 8
        for kc in range(nact):
            self.act(self.hT[:, kc, t * 128:(t + 1) * 128], self.psb[:, kc * 128:(kc + 1) * 128], AF.Identity,
                     reads=list(skey) + list(bkey), writes=[("ps", 7), ("hT", t // 4)], scale=s_ap[:, kc:kc + 1], bias=b_ap[:, kc:kc + 1])
        if mixed:
            tmp = self._nb["tmp"]
            pin = self.psb[:, 512:1024].rearrange("p (c n) -> p c n", c=4)
            self.dve(lambda: nc.vector.tensor_tensor(out=tmp[:], in0=pin, in1=s_ap[:, 4:8].unsqueeze(2).to_broadcast([128, 4, 128]), op=ALU.mult),
                     list(skey), [("ps", 7), "ntmp"])
            self.dve(lambda: nc.vector.tensor_tensor(out=self.hT[:, 4:8, t * 128:(t + 1) * 128], in0=tmp[:], in1=b_ap[:, 4:8].unsqueeze(2).to_broadcast([128, 4, 128]), op=ALU.add),
                     list(bkey) + ["ntmp"], [("hT", t // 4)])

    def attention(self, ph, l):
        nc = self.nc
        kb = self.kb
        kb.mark("attention")
        wa = self.sbt(ph, "wattn", [128, 8, 832], BF16)
        wasrc = self.d["w_attn"][l].rearrange("p (k n) -> p k n", k=8)
        self.wload(wa[:, :, 384:640], wasrc[:, :, 384:640], ("wattn", "kv"))
        self.wload(wa[:, :, 640:832], wasrc[:, :, 640:832], ("wattn", "kr"))
        wqb = self.sbt(ph, "wqb", [128, 3, NH, 192], BF16)
        self._late_attn_loads = lambda: (
            self.kb.dma("pool", wa[:, :, 0:384], wasrc[:, :, 0:384], reads=[("ckvn", 0)], writes=[("wattn", "q")]),
            self.kb.dma("pool", wqb[:], self.d["w_qb"][l].rearrange("p (k h n) -> p k h n", k=3, h=NH), reads=[("ckvn", 0)], writes=["wqb"]))
        kT = self.sbt(ph, "kT", [128, NH, S], BF16)
        vext = self.sbt(ph, "vext", [128, NT, NH, 65], BF16)
        self.Ctab = self.sbt(ph, "Ctab", [128, S], F32)
        self.Stab = self.sbt(ph, "Stab", [128, S], F32)
        kb.dma("sp", self.Ctab[:], self.rope_c, writes=["tabc"])
        kb.dma("sp", self.Stab[:], self.rope_s, writes=["tabs"])
        self.dve(lambda: nc.vector.memset(vext[:], 1.0), [], ["vext"])
        ones16 = self.k16("ones")
        bd = self.k16("bd")

        def mkscr(stk, n):
            scr = [self.sbt(stk, "ascr", [128, 512], F32) for _ in range(n)]
            state = {"i": 0}

            def nscr():
                i = state["i"]
                state["i"] = (i + 1) % n
                return scr[i], ("ascr", i)
            return nscr

        with ExitStack() as sa:
            wkn = self.sbt(sa, "wkn", [128, 2, 512], BF16)
            wv = self.sbt(sa, "wv", [128, 2, 512], BF16)
            self.wload(wkn[:], self.d["w_kn"][l].rearrange("p (k n) -> p k n", k=2), "wkn")
            self.wload(wv[:], self.d["w_v"][l].rearrange("p (k n) -> p k n", k=2), "wv")
            ckvn = self.sbt(sa, "ckvn", [128, 2, S], BF16)
            ksq = [self.sbt(sa, "ksq", [64, 512], BF16) for _ in range(3)]
            krs = [self.sbt(sa, "krs", [64, 512], F32) for _ in range(3)]
            sqb = [self.sbt(sa, "asq", [128, 3, 512], BF16) for _ in range(2)]
            nscr = mkscr(sa, 4)
            for b in range(NB):
                bs = slice(b * 512, (b + 1) * 512)
                hk = ("hT", b)
                sq, sqk = sqb[b % 2], ("asq", b % 2)
                banks = []
                for c in range(2):
                    bk = self.nb()
                    banks.append(bk)
                    self.mm(self.ps[bk], [(wa[:, kc, 384 + c * 128:384 + (c + 1) * 128], self.hT[:, kc, bs]) for kc in range(8)], bk, reads=[("wattn", "kv"), hk])
                    self.act(sq[:, c, :], self.ps[bk], AF.Square, reads=[], writes=[("ps", bk), (sqk, c)])
                bss = self.nb()
                self.mm(self.ps[bss], [(ones16, sq[:, c, :]) for c in range(2)], bss, reads=["c16", (sqk, 0), (sqk, 1)])
                rs, rsk = nscr()
                self.rstd(rs[:], self.ps[bss], 1.0 / 256, EPS, [("ps", bss)], rsk)
                for c in range(2):
                    self.dve(lambda c=c, rs=rs, bk=banks[c]: nc.vector.scalar_tensor_tensor(out=ckvn[:, c, bs], in0=self.ps[bk], scalar=self.vpc("kv_a_norm", c), in1=rs[:], op0=ALU.mult, op1=ALU.mult),
                             [rsk, "vp"], [("ps", banks[c]), ("ckvn", b)])
                if b == 0:
                    self._late_attn_loads()
                bA = self.nb()
                self.mm(self.ps[bA][0:96, :], [(wa[:, kc, 640:736], self.hT[:, kc, bs]) for kc in range(8)], bA, reads=[("wattn", "kr"), hk])
                bB = self.nb()
                self.mm(self.ps[bB][0:96, :], [(wa[:, kc, 736:832], self.hT[:, kc, bs]) for kc in range(8)], bB, reads=[("wattn", "kr"), hk])
                self.act(sq[0:96, 2, :], self.ps[bA][0:96, :], AF.Square, reads=[], writes=[("ps", bA), (sqk, 2)])
                bm = self.nb()
                self.mm(self.ps[bm][0:96, :], [(bd[0:96, 0:96], sq[0:96, 2, :])], bm, reads=["c16", (sqk, 2)])
                rs, rsk = nscr()
                self.rstd(rs[0:96, :], self.ps[bm][0:96, :], 1.0, EPS, [("ps", bm)], rsk)
                t1, t1k = nscr()
                t2, t2k = nscr()
                self.dve(lambda t1=t1, bA=bA: nc.vector.scalar_tensor_tensor(out=t1[64:96, :], in0=self.ps[bA][64:96, :], scalar=self.vpc("kwA")[64:96, :], in1=self.Ctab[64:96, bs], op0=ALU.mult, op1=ALU.mult),
                         ["vp", "tabc"], [("ps", bA), t1k])
                self.dve(lambda t2=t2, bB=bB: nc.vector.scalar_tensor_tensor(out=t2[64:96, :], in0=self.ps[bB][64:96, :], scalar=self.vpc("kwB")[64:96, :], in1=self.Stab[64:96, bs], op0=ALU.mult, op1=ALU.mult),
                         ["vp", "tabs"], [("ps", bB), t2k])
                self.dve(lambda t1=t1, t2=t2: nc.vector.tensor_add(out=t1[64:96, :], in0=t1[64:96, :], in1=t2[64:96, :]), [t2k], [t1k])
                self.dve(lambda t1=t1, rs=rs: nc.vector.tensor_mul(out=t1[64:96, :], in0=t1[64:96, :], in1=rs[64:96, :]), [rsk], [t1k])
                self.dve(lambda t1=t1: nc.vector.tensor_copy(out=kT[64:96, :, bs], in_=t1[64:96, :].unsqueeze(1).to_broadcast([32, NH, 512])), [t1k], [("kTr", b)])
                def kchain(h, b=b, bs=bs):
                    def gen(slot):
                        bk, bm = slot, 3 + slot
                        sqh, sqhk = ksq[slot], ("ksq", slot)
                        rs, rsk = krs[slot], ("krs", slot)
                        self.mm(self.ps[bk][0:64, :], [(wkn[:, c, h * 64:(h + 1) * 64], ckvn[:, c, bs]) for c in range(2)], bk, reads=["wkn", ("ckvn", b)])
                        yield
                        self.act(sqh[0:64, :], self.ps[bk][0:64, :], AF.Square, reads=[], writes=[("ps", bk), sqhk])
                        yield
                        self.mm(self.ps[bm][0:64, :], [(bd[0:64, 0:64], sqh[0:64, :])], bm, reads=["c16", sqhk])
                        yield
                        self.act(rs[0:64, :], self.ps[bm][0:64, :], AF.Ln, reads=[], writes=[("ps", bm), rsk], bias=EPS, scale=1.0)
                        yield
                        self.act(rs[0:64, :], rs[0:64, :], AF.Exp, reads=[], writes=[rsk], scale=-0.5)
                        yield
                        self.dve(lambda: nc.vector.scalar_tensor_tensor(out=kT[0:64, h, bs], in0=self.ps[bk][0:64, :], scalar=self.vpc("kwA")[0:64, :], in1=rs[0:64, :], op0=ALU.mult, op1=ALU.mult),
                                 [rsk, "vp"], [("ps", bk), ("kTn", b, h)])
                        yield
                    return gen
                self.run_chains([kchain(h) for h in range(NH)], 3)
                for tt in range(4):
                    t = b * 4 + tt
                    bk = self.nb()
                    self.mm(self.ps[bk], [(ckvn[:, c, t * 128:(t + 1) * 128], wv[:, c, :]) for c in range(2)], bk, reads=["wv", ("ckvn", b)])
                    self.act(vext[:, t, :, 0:64], self.ps[bk].rearrange("p (h d) -> p h d", h=NH), AF.Identity, reads=[], writes=[("ps", bk), "vext"])
            kb.barrier()
        self.dump("kT%d" % l, kT[:], [128, NH, S], [], BF16)
        self.dump("vext%d" % l, vext[:], [128, NT, NH, 65], [], BF16)

        with ExitStack() as sq_:
            sqb = [self.sbt(sq_, "asq", [128, 3, 512], BF16)] * 2
            nscr = mkscr(sq_, 2)
            ql = self.sbt(sq_, "qln", [128, 3, 512], BF16)
            qsq = [self.sbt(sq_, "qsq", [128, 512], BF16) for _ in range(2)]
            qsc = [self.sbt(sq_, "qsc", [128, 512], F32) for _ in range(6)]
            qT = [self.sbt(sq_, "qT", [128, NH, 512], BF16) for _ in range(2)]
            Eb = [self.sbt(sq_, "E", [128, 512], BF16) for _ in range(3)]
            ot = self.sbt(sq_, "otok", [128, 4, 512], BF16)
            rcp = [self.sbt(sq_, "rcp", [128, 4], F32) for _ in range(2)]
            mask01 = self.k16("mask01")
            ei = 0
            for qb_ in range(NB):
                bs = slice(qb_ * 512, (qb_ + 1) * 512)
                hk = ("hT", qb_)
                qlk = "qln"
                qt_, qtk = qT[qb_ % 2], ("qT", qb_ % 2)
                sq, sqk = sqb[0], ("asq", 0)
                banks = []
                for c in range(3):
                    bk = self.nb()
                    banks.append(bk)
                    self.mm(self.ps[bk], [(wa[:, kc, c * 128:(c + 1) * 128], self.hT[:, kc, bs]) for kc in range(8)], bk, reads=[("wattn", "q"), hk])
                    self.act(sq[:, c, :], self.ps[bk], AF.Square, reads=[], writes=[("ps", bk), (sqk, c)])
                bss = self.nb()
                self.mm(self.ps[bss], [(ones16, sq[:, c, :]) for c in range(3)], bss, reads=["c16"] + [(sqk, c) for c in range(3)])
                rs, rsk = nscr()
                self.rstd(rs[:], self.ps[bss], 1.0 / 384, EPS, [("ps", bss)], rsk)
                for c in range(3):
                    self.dve(lambda c=c, rs=rs, bk=banks[c]: nc.vector.scalar_tensor_tensor(out=ql[:, c, :], in0=self.ps[bk], scalar=self.vpc("q_a_norm", c), in1=rs[:], op0=ALU.mult, op1=ALU.mult),
                             [rsk, "vp"], [("ps", banks[c]), (qlk, c)])
                def qchain(h, qb_=qb_, bs=bs, qt_=qt_, qtk=qtk):
                    def gen(slot):
                        bA, bB, bm = 3 * slot, 3 * slot + 1, 3 * slot + 2
                        sqh, sqhk = qsq[slot], ("qsq", slot)
                        rs, rsk = qsc[3 * slot], ("qsc", 3 * slot)
                        t1, t1k = qsc[3 * slot + 1], ("qsc", 3 * slot + 1)
                        t2, t2k = qsc[3 * slot + 2], ("qsc", 3 * slot + 2)
                        self.mm(self.ps[bA][0:96, :], [(wqb[:, c, h, 0:96], ql[:, c, :]) for c in range(3)], bA, reads=["wqb"] + [(qlk, c) for c in range(3)])
                        self.mm(self.ps[bB][0:96, :], [(wqb[:, c, h, 96:192], ql[:, c, :]) for c in range(3)], bB, reads=["wqb"] + [(qlk, c) for c in range(3)])
                        yield
                        self.act(sqh[0:96, :], self.ps[bA][0:96, :], AF.Square, reads=[], writes=[("ps", bA), sqhk])
                        yield
                        self.mm(self.ps[bm][0:96, :], [(bd[0:96, 0:96], sqh[0:96, :])], bm, reads=["c16", sqhk])
                        self.dve(lambda: nc.vector.scalar_tensor_tensor(out=t1[0:96, :], in0=self.ps[bA][0:96, :], scalar=self.vpc("qwA")[0:96, :], in1=self.Ctab[0:96, bs], op0=ALU.mult, op1=ALU.mult),
                                 ["vp", "tabc"], [("ps", bA), t1k])
                        yield
                        self.act(rs[0:96, :], self.ps[bm][0:96, :], AF.Ln, reads=[], writes=[("ps", bm), rsk], bias=EPS / (SM_SCALE ** 2), scale=1.0 / (SM_SCALE ** 2))
                        self.dve(lambda: nc.vector.scalar_tensor_tensor(out=t2[0:96, :], in0=self.ps[bB][0:96, :], scalar=self.vpc("qwB")[0:96, :], in1=self.Stab[0:96, bs], op0=ALU.mult, op1=ALU.mult),
                                 ["vp", "tabs"], [("ps", bB), t2k])
                        yield
                        self.act(rs[0:96, :], rs[0:96, :], AF.Exp, reads=[], writes=[rsk], scale=-0.5)
                        self.dve(lambda: nc.vector.tensor_add(out=t1[0:96, :], in0=t1[0:96, :], in1=t2[0:96, :]), [t2k], [t1k])
                        yield
                        self.dve(lambda: nc.vector.tensor_mul(out=qt_[0:96, h, :], in0=t1[0:96, :], in1=rs[0:96, :]), [rsk, t1k], [(qtk, h)])
                        yield
                    return gen
                self.run_chains([qchain(h) for h in range(NH)], 2)
                if qb_ == 0:
                    self.dump("qT%d" % l, qt_[:], [128, NH, 512], [(qtk, h) for h in range(NH)], BF16)
                otk = "otok"
                LA = 2
                for h in range(NH):
                    bo = 5 + (h % 2)
                    nkt = 4 * qb_ + 4
                    st = {"first": True}
                    pend = {}

                    def score(kt, h=h, qb_=qb_):
                        j0 = max(0, kt - 4 * qb_)
                        cs = slice(j0 * 128, 512)
                        bsT = self.nb5()
                        self.mm(self.ps[bsT][:, cs], [(kT[0:96, h, kt * 128:(kt + 1) * 128], qt_[0:96, h, cs])], bsT, reads=[(qtk, h)])
                        pend[kt] = (bsT, j0, cs)

                    def finish(kt, h=h, qb_=qb_, bo=bo, st=st):
                        nonlocal ei
                        bsT, j0, cs = pend.pop(kt)
                        E, Ek = Eb[ei % 3], ("E", ei % 3)
                        ei += 1
                        self.act(E[:, cs], self.ps[bsT][:, cs], AF.Exp, reads=[], writes=[("ps", bsT), Ek])
                        if kt >= 4 * qb_:
                            self.dve(lambda E=E, j0=j0: nc.vector.tensor_mul(out=E[:, j0 * 128:(j0 + 1) * 128], in0=E[:, j0 * 128:(j0 + 1) * 128], in1=mask01), ["c16"], [Ek])
                        fns = []
                        for j in range(j0, 4):
                            qtile = 4 * qb_ + j
                            st_flag = st["first"]
                            st["first"] = False
                            fns.append(lambda j=j, E=E, kt=kt, st_flag=st_flag, qtile=qtile: nc.tensor.matmul(
                                self.ps[bo][:, j * 65:(j + 1) * 65], lhsT=E[:, j * 128:(j + 1) * 128], rhs=vext[:, kt, h, :],
                                start=st_flag, stop=(kt == qtile), skip_group_check=True))
                        kb.group("pe", fns, reads=[Ek], writes=[("ps", bo)])
                    for kt in range(nkt):
                        score(kt)
                        if kt >= LA:
                            finish(kt - LA)
                    for kt in range(max(0, nkt - LA), nkt):
                        finish(kt)
                    rc, rck = rcp[h % 2], ("rcp", h % 2)
                    pview = self.ps[bo][:, 0:260].rearrange("p (j e) -> p j e", j=4)
                    self.dve(lambda rc=rc, pview=pview: nc.vector.reciprocal(out=rc[:].unsqueeze(2), in_=pview[:, :, 64:65]), [], [("ps", bo), rck])
                    self.dve(lambda rc=rc, pview=pview, h=h: nc.vector.tensor_tensor(out=ot[:, :, h * 64:(h + 1) * 64], in0=pview[:, :, 0:64], in1=rc[:].unsqueeze(2).to_broadcast([128, 4, 64]), op=ALU.mult),
                             [rck], [("ps", bo), (otk, h)])
                for j in range(4):
                    t = 4 * qb_ + j
                    kb.group("pe", [lambda c=c, j=j: nc.tensor.transpose(self.psb[:, c * 128:(c + 1) * 128], ot[:, j, c * 128:(c + 1) * 128], self.k16("ident")) for c in range(4)],
                             reads=[(otk, h) for h in range(NH)] + ["c16"], writes=[("ps", 7)])
                    self.act(self.o_a[:, :, t * 128:(t + 1) * 128], self.psb[:, 0:512].rearrange("p (c n) -> p c n", c=4), AF.Identity, reads=[], writes=[("ps", 7), "o_a"])
            kb.barrier()

    def pool_branch(self, ph, l):
        nc = self.nc
        kb = self.kb
        kb.mark("pool_branch")
        wu = self.sbt(ph, "wu", [128, 8, 512], BF16)
        wp = self.sbt(ph, "wpool", [128, 4, 128], BF16)
        self.wload(wu[:], self.d["w_u"][l].rearrange("p (k n) -> p k n", k=8), "wu")
        self.wload(wp[:], self.d["w_pool"][l].rearrange("p (g n) -> p g n", g=4), "wpool")
        self.wload(self.wg_pre[:], self.d["w_gate"][l, 0].rearrange("p (k x n) -> p k x n", k=8, x=3), ("wgate", 0))
        self.wload(self.wb_pre[:], self.d["w_br"][l, 0].rearrange("p (k n) -> p k n", k=16), ("wbr", 0))
        utok = self.sbt(ph, "utok", [128, NT, 512], BF16)
        pooled = [self.sbt(ph, "pooled", [128, 512], BF16) for _ in range(3)]
        for t in range(NT):
            bk = self.nb()
            self.mm(self.ps[bk], [(self.hT[:, kc, t * 128:(t + 1) * 128], wu[:, kc, :]) for kc in range(8)], bk, reads=["wu", ("hT", t // 4)])
            self.act(utok[:, t, :], self.ps[bk], AF.Identity, reads=[], writes=[("ps", bk), ("utok", t)])
        def pchain(b, g):
            def gen(slot):
                bk, b2 = 2 * slot, 2 * slot + 1
                pl, plk = pooled[slot], ("pooled", slot)
                fns = []
                for tt in range(4):
                    t = 4 * b + tt
                    cur = self.k16("bcur0" if t == 0 else "bcur", g)
                    o_ = self.ps[bk][:, tt * 128:(tt + 1) * 128]
                    if t == 0:
                        fns.append(lambda o_=o_, cur=cur, t=t: nc.tensor.matmul(o_, lhsT=utok[:, t, g * 128:(g + 1) * 128], rhs=cur, start=True, stop=True))
                    else:
                        fns.append(lambda o_=o_, cur=cur, t=t: nc.tensor.matmul(o_, lhsT=utok[:, t, g * 128:(g + 1) * 128], rhs=cur, start=True, stop=False))
                        fns.append(lambda o_=o_, t=t: nc.tensor.matmul(o_, lhsT=utok[:, t - 1, g * 128:(g + 1) * 128], rhs=self.k16("bprev", g), start=False, stop=True))
                kb.group("pe", fns, reads=["c16"] + [("utok", t) for t in range(max(0, 4 * b - 1), 4 * b + 4)], writes=[("ps", bk)])
                yield
                self.dve(lambda: nc.vector.tensor_copy(out=pl[:], in_=self.ps[bk]), [], [("ps", bk), plk])
                yield
                self.mm(self.ps[b2], [(wp[:, g, :], pl[:])], b2, reads=["wpool", plk])
                yield
                self.act(self.o_b[:, g, b * 512:(b + 1) * 512], self.ps[b2], AF.Identity, reads=["vp"], writes=[("ps", b2), "o_b"], scale=self.vpc("pool_scale", g))
                yield
            return gen
        self.run_chains([pchain(b, g) for b in range(NB) for g in range(4)], 3)

    def ssd(self, ph, l, g):
        nc = self.nc
        kb = self.kb
        kb.mark("ssd")
        ws = self.sbt(ph, "wssd", [128, 8, 1288], BF16)
        wssrc = self.d["w_ssd"][l, g].rearrange("p (k n) -> p k n", k=8)
        for j in range(6):
            self.wload(ws[:, :, j * 128:(j + 1) * 128], wssrc[:, :, j * 128:(j + 1) * 128], ("wssd", j))
        self.wload(ws[:, :, 768:1280], wssrc[:, :, 768:1280], ("wssd", "z"))
        self.wload(ws[:, :, 1280:1288], wssrc[:, :, 1280:1288], ("wssd", "dt"))
        sm = self.sbt(ph, "ssdsm", [128, 48], F32)
        kb.dma("sp", sm[:], self.d["ssd_small"][l:l + 1, :].partition_broadcast(128), writes=["ssdsm"])
        abc_ = self.sbt(ph, "ssda", [128, 8], F32)
        self.act(abc_[:], sm[:, 32 + 8 * g:32 + 8 * g + 8], AF.Exp, reads=["ssdsm"], writes=["ssda"])
        self.dve(lambda: nc.vector.tensor_scalar(out=abc_[:], in0=abc_[:], scalar1=-1.0, scalar2=None, op0=ALU.mult), [], ["ssda"])
        Dg = sm[:, 8 * g:8 * g + 8]
        dtb = sm[:, 16 + 8 * g:16 + 8 * g + 8]
        xbc = self.sbt(ph, "xbc", [128, 6, S], BF16)
        zs_all = self.sbt(ph, "zsall", [128, NT, 512], BF16)
        with ExitStack() as s1_:
            raw = self.sbt(s1_, "sraw", [128, 6, 3 + S], BF16)
            cacc = [self.sbt(s1_, "scacc", [128, 512], F32) for _ in range(2)]
            self.dve(lambda: nc.vector.memset(raw[:, :, 0:3], 0.0), [], ["rawpad"])
            for b in range(NB):
                bs = slice(b * 512, (b + 1) * 512)
                for j in range(6):
                    bk = self.nb()
                    self.mm(self.ps[bk], [(ws[:, kc, j * 128:(j + 1) * 128], self.hT[:, kc, bs]) for kc in range(8)], bk, reads=[("wssd", j), ("hT", b)])
                    self.act(raw[:, j, 3 + b * 512:3 + (b + 1) * 512], self.ps[bk], AF.Identity, reads=[], writes=[("ps", bk), ("sraw", j, b)])
                for j in range(6):
                    ch = (4 * g + j) if j < 4 else (8 + g if j == 4 else 10 + g)
                    rd = [("sraw", j, b), "rawpad", "vp"] + ([("sraw", j, b - 1)] if b > 0 else [])
                    ca, ck = cacc[(b * 6 + j) % 2], ("scacc", (b * 6 + j) % 2)
                    rv = lambda k, j=j, b=b: raw[:, j, b * 512 + k:b * 512 + k + 512]
                    self.dve(lambda ca=ca, rv=rv, ch=ch: nc.vector.tensor_scalar(out=ca[:], in0=rv(0), scalar1=self.vpc("ssd_conv_w", 0 * 12 + ch), scalar2=self.vpc("ssd_conv_b", ch), op0=ALU.mult, op1=ALU.add), rd, [ck])
                    for k in range(1, 4):
                        self.dve(lambda ca=ca, rv=rv, ch=ch, k=k: nc.vector.scalar_tensor_tensor(out=ca[:], in0=rv(k), scalar=self.vpc("ssd_conv_w", k * 12 + ch), in1=ca[:], op0=ALU.mult, op1=ALU.add), rd, [ck])
                    self.act(xbc[:, j, bs], ca[:], AF.Silu, reads=[ck], writes=[("xbc", j, b)])
                for tt in range(4):
                    t = 4 * b + tt
                    bk = self.nb()
                    self.mm(self.ps[bk], [(self.hT[:, kc, t * 128:(t + 1) * 128], ws[:, kc, 768:1280]) for kc in range(8)], bk, reads=[("wssd", "z"), ("hT", b)])
                    self.act(zs_all[:, t, :], self.ps[bk], AF.Silu, reads=[], writes=[("ps", bk), ("zsall", t)])

            kb.barrier()
        self.dump("xbc%d_%d" % (l, g), xbc[:], [128, 6, S], [("xbc", j, b) for j in range(6) for b in range(NB)], BF16)
        U32 = self.k32("U")
        ones32 = self.k32("ones")
        ident16 = self.k16("ident")
        maskneg4 = self.c16[:, C16["maskneg4"][0]:C16["maskneg4"][0] + 512]
        ones40 = self.k16("ones")[0:40, :]
        sel16 = self.c16[0:64, C16["sel16"][0]:C16["sel16"][0] + 1024].rearrange("p (r m) -> p r m", r=8)
        prev = self.sbt(ph, "sprev", [128, 512], F32)
        prevb = self.sbt(ph, "sprevb", [128, 512], BF16)
        self.dve(lambda: nc.vector.memset(prev[:], 0.0), [], ["sprev"])
        self.dve(lambda: nc.vector.memset(prevb[:], 0.0), [], ["sprevb"])
        Dm = self.sbt(ph, "sDm", [128, 8, 128], BF16)
        for r in range(8):
            self.dve(lambda r=r: nc.vector.tensor_scalar(out=Dm[:, r, :], in0=ident16, scalar1=Dg[:, r:r + 1], scalar2=None, op0=ALU.mult), ["ssdsm", "c16"], ["sDm"])
        dt_all = self.sbt(ph, "sdtall", [128, NT, 8], F32)
        da_all = self.sbt(ph, "sdaall", [128, NT, 8], F32)
        dae_all = self.sbt(ph, "sdaeall", [128, NT, 2, 32], F32)
        acs_all = self.sbt(ph, "sacsall", [128, NT, 8], F32)
        ea_all = self.sbt(ph, "seaall", [128, NT, 8], F32)
        cd_all = self.sbt(ph, "scdall", [128, NT, 8], F32)
        dsd_all = self.sbt(ph, "sdsdall", [128, NT, 8], F32)
        hl_all = self.sbt(ph, "shlall", [64, NT, 128], BF16)
        nhl_all = self.sbt(ph, "snhlall", [64, NT, 128], BF16)
        hc_q = self.sbt(ph, "shcq", [64, 4, 128], BF16)
        self.dve(lambda: nc.vector.memset(dae_all[:], 0.0), [], ["sdae"])
        self.dve(lambda: nc.vector.memset(hl_all[:], 0.0), [], ["shl"])
        bdt = self.nb()
        fns = []
        for t in range(NT):
            for kc in range(8):
                fns.append(lambda t=t, kc=kc: nc.tensor.matmul(self.ps[bdt][:, t * 8:(t + 1) * 8], lhsT=self.hT[:, kc, t * 128:(t + 1) * 128], rhs=ws[:, kc, 1280:1288], start=(kc == 0), stop=(kc == 7)))
        kb.group("pe", fns, reads=[("wssd", "dt")] + [("hT", b) for b in range(NB)], writes=[("ps", bdt)])
        f2 = lambda a: a.rearrange("p t r -> p (t r)")
        self.dve(lambda: nc.vector.tensor_tensor(out=dt_all[:], in0=self.ps[bdt][:, 0:128].rearrange("p (t r) -> p t r", r=8), in1=dtb.unsqueeze(1).to_broadcast([128, NT, 8]), op=ALU.add), ["ssdsm"], [("ps", bdt), "sdt"])
        self.act(f2(dt_all[:]), f2(dt_all[:]), AF.Exp, reads=[], writes=["sdt"])
        self.act(f2(dt_all[:]), f2(dt_all[:]), AF.Ln, reads=[], writes=["sdt"], bias=1.0)
        self.dve(lambda: nc.vector.tensor_tensor(out=da_all[:], in0=dt_all[:], in1=abc_[:].unsqueeze(1).to_broadcast([128, NT, 8]), op=ALU.mult), ["sdt", "ssda"], ["sda"])
        for hh in range(2):
            self.dve(lambda hh=hh: nc.vector.tensor_copy(out=dae_all[:, :, hh, 0:8], in_=da_all[:]), ["sda"], ["sdae"])
        bcs = self.nb()
        kb.group("pe", [lambda: nc.tensor.matmul(self.ps[bcs][:, 0:128], lhsT=U32, rhs=f2(da_all[:]), start=True, stop=True),
                        lambda: nc.tensor.matmul(self.ps[bcs][:, 128:256], lhsT=ones32, rhs=f2(da_all[:]), start=True, stop=True)],
                 reads=["sda", "c32"], writes=[("ps", bcs)])
        self.act(f2(acs_all[:]), self.ps[bcs][:, 0:128], AF.Identity, reads=[], writes=[("ps", bcs), "sacs"])
        self.act(f2(ea_all[:]), self.ps[bcs][:, 0:128], AF.Exp, reads=[], writes=[("ps", bcs), "sea"])
        self.act(f2(cd_all[:]), self.ps[bcs][:, 128:256], AF.Exp, reads=[], writes=[("ps", bcs), "scd"])
        self.dve(lambda: nc.vector.tensor_sub(out=f2(dsd_all[:]), in0=self.ps[bcs][:, 128:256], in1=f2(acs_all[:])), ["sacs"], [("ps", bcs), "sdsd"])
        self.act(f2(dsd_all[:]), f2(dsd_all[:]), AF.Exp, reads=[], writes=["sdsd"])
        for q4 in range(4):
            bq = self.nb()
            fns = []
            for tt in range(4):
                t = 4 * q4 + tt
                lhs = dae_all[:, t].rearrange("p a b -> p (a b)")[:, 0:40]
                fns.append(lambda tt=tt, lhs=lhs, bq=bq: nc.tensor.matmul(self.ps[bq][0:40, tt * 128:(tt + 1) * 128], lhsT=lhs, rhs=U32, start=True, stop=True))
            kb.group("pe", fns, reads=["sdae", "c32"], writes=[("ps", bq)])
            tsl = slice(4 * q4, 4 * q4 + 4)
            pv = lambda lo, hi, bq=bq: self.ps[bq][lo:hi, :].rearrange("p (t m) -> p t m", t=4)
            self.act(hl_all[0:8, tsl, :], pv(0, 8), AF.Identity, reads=[], writes=[("ps", bq), "shl"])
            self.act(hc_q[32:40, :, :], pv(32, 40), AF.Identity, reads=[], writes=[("ps", bq), "shc"])
            self.dve(lambda tsl=tsl, pv=pv: nc.vector.tensor_sub(out=hl_all[32:40, tsl, :], in0=pv(32, 40), in1=hc_q[32:40, :, :]), ["shc"], [("ps", bq), "shl"])
        self.dve(lambda: nc.vector.tensor_scalar(out=nhl_all[0:40], in0=hl_all[0:40], scalar1=-1.0, scalar2=None, op0=ALU.mult), ["shl"], ["snhl"])

        R = 2

        def rot(name, shape, dt):
            return [self.sbt(ph, name, shape, dt) for _ in range(R)]
        one = lambda name, shape, dt: [self.sbt(ph, name, shape, dt)] * R
        bsel = one("sbsel", [64, 8, 128], BF16)
        xsb = one("sxsb", [128, 512], BF16)
        Eexp = one("sE", [128, 8, 128], BF16)
        Mt = Eexp
        cbT = one("scbT", [128, 128], BF16)
        xdt = one("sxdt", [128, 512], BF16)
        xdt2 = rot("sxdt2", [128, 512], BF16)
        Btok = rot("sBtok", [128, 128], BF16)
        yb = one("sy", [128, 512], F32)
        gt = one("sgt", [128, 512], BF16)
        junk = self.sbt(ph, "sjunk", [128, 512], BF16)
        ssq = one("sssq", [128, 1], F32)
        x3 = lambda a: a.rearrange("p (r d) -> p r d", r=8)
        bc8 = lambda a: a.unsqueeze(2).to_broadcast([128, 8, 64])
        nwbc = self.sbt(ph, "snwbc", [128, 512], F32)
        kb.dma("sp", nwbc[:], self.d["ssd_nw"][l:l + 1, 512 * g:512 * (g + 1)].partition_broadcast(128), writes=["snwbc"])
        psbA = self.ps[6].bitcast(BF16)
        poolA = {"i": 0}
        poolB = {"i": 0}

        def nbA():
            poolA["i"] ^= 1
            return poolA["i"]

        def nbB():
            poolB["i"] ^= 1
            return 2 + poolB["i"]

        SINGLE = {"sbsel", "sxsb", "sE", "sM", "scbT", "sxdt", "sy", "sgt", "sssq"}

        def stageA(t):
            i = t % R
            ts_ = slice(t * 128, (t + 1) * 128)
            b = t // 4
            K = lambda n: ("sE", 0) if n == "sM" else ((n, 0) if n in SINGLE else (n, i))
            by = 4 + i
            kb.group("pe", [lambda j=j: nc.tensor.transpose(psbA[:, j * 128:(j + 1) * 128], xbc[:, j, ts_], ident16) for j in range(5)],
                     reads=[("xbc", j, b) for j in range(5)] + ["c16"], writes=[("ps", 6)])
            yield
            bcb = nbA()
            self.mm(self.ps[bcb][:, 0:128], [(xbc[:, 4, ts_], xbc[:, 5, ts_])], bcb, reads=[("xbc", 4, b), ("xbc", 5, b)])
            yield
            self.dve(lambda: nc.vector.tensor_tensor(out=bsel[i][0:40], in0=sel16[0:40], in1=hl_all[0:40, t, :].unsqueeze(1).to_broadcast([40, 8, 128]), op=ALU.mult), ["shl", "c16"], [K("sbsel")])
            yield
            self.act(cbT[i][:], self.ps[bcb][:, 0:128], AF.Identity, reads=[], writes=[("ps", bcb), K("scbT")])
            yield
            self.act(xsb[i][:], psbA[:, 0:512], AF.Identity, reads=[], writes=[("ps", 6), K("sxsb")])
            yield
            self.act(Btok[i][:], psbA[:, 512:640], AF.Identity, reads=[], writes=[("ps", 6), K("sBtok")])
            yield
            bp = [nbA(), nbA()]
            for half in range(2):
                hsl = slice(half * 4, (half + 1) * 4)
                fns = [lambda hsl=hsl, half=half: nc.tensor.matmul(self.ps[bp[half]], lhsT=ones40, rhs=bsel[i][0:40, hsl, :].rearrange("p r m -> p (r m)"), start=True, stop=False),
                       lambda hsl=hsl, half=half: nc.tensor.matmul(self.ps[bp[half]], lhsT=nhl_all[0:40, t, :], rhs=sel16[0:40, hsl, :].rearrange("p r m -> p (r m)"), start=False, stop=False),
                       lambda half=half: nc.tensor.matmul(self.ps[bp[half]], lhsT=ident16, rhs=maskneg4, start=False, stop=True)]
                kb.group("pe", fns, reads=[K("sbsel"), "snhl", "c16"], writes=[("ps", bp[half])])
                yield
            self.dve(lambda: nc.vector.tensor_tensor(out=x3(xdt[i][:]), in0=x3(xsb[i][:]), in1=bc8(dt_all[:, t, :]), op=ALU.mult), [K("sxsb"), "sdt"], [K("sxdt")])
            yield
            for half in range(2):
                hsl = slice(half * 4, (half + 1) * 4)
                ek = ("sE", half)
                self.act(Eexp[i][:, hsl, :], self.ps[bp[half]].rearrange("p (r m) -> p r m", r=4), AF.Exp, reads=[], writes=[("ps", bp[half]), ek])
                yield
                self.dve(lambda hsl=hsl: nc.vector.tensor_mul(out=Mt[i][:, hsl, :], in0=Eexp[i][:, hsl, :], in1=cbT[i][:].unsqueeze(1).to_broadcast([128, 4, 128])), [K("scbT")], [ek])
                yield
                fns = []
                for r in range(half * 4, half * 4 + 4):
                    fns.append(lambda r=r: nc.tensor.matmul(self.ps[by][:, r * 64:(r + 1) * 64], lhsT=Mt[i][:, r, :], rhs=xdt[i][:, r * 64:(r + 1) * 64], start=True, stop=False))
                    fns.append(lambda r=r: nc.tensor.matmul(self.ps[by][:, r * 64:(r + 1) * 64], lhsT=Dm[:, r, :], rhs=xsb[i][:, r * 64:(r + 1) * 64], start=False, stop=True))
                kb.group("pe", fns, reads=[ek, K("sxdt"), K("sxsb"), "sDm"], writes=[("ps", by)])
                yield
            self.dve(lambda: nc.vector.tensor_tensor(out=x3(xdt2[i][:]), in0=x3(xdt[i][:]), in1=bc8(dsd_all[:, t, :]), op=ALU.mult), [K("sxdt"), "sdsd"], [K("sxdt2")])
            yield

        def stageB(t):
            i = t % R
            ts_ = slice(t * 128, (t + 1) * 128)
            b = t // 4
            K = lambda n: ("sE", 0) if n == "sM" else ((n, 0) if n in SINGLE else (n, i))
            by = 4 + i
            bo = nbB()
            self.mm(self.ps[bo], [(xbc[:, 5, ts_], prevb[:])], bo, reads=[("xbc", 5, b), "sprevb"])
            yield
            bst = nbB()
            self.mm(self.ps[bst], [(Btok[i][:], xdt2[i][:])], bst, reads=[K("sBtok"), K("sxdt2")])
            yield
            self.dve(lambda: nc.vector.tensor_tensor(out=x3(prev[:]), in0=x3(prev[:]), in1=bc8(cd_all[:, t, :]), op=ALU.mult), ["scd"], ["sprev"])
            yield
            self.dve(lambda: nc.vector.tensor_add(out=prev[:], in0=prev[:], in1=self.ps[bst]), [], [("ps", bst), "sprev"])
            yield
            self.dve(lambda: nc.vector.tensor_copy(out=prevb[:], in_=prev[:]), ["sprev"], ["sprevb"])
            yield
            self.dve(lambda: nc.vector.tensor_tensor(out=x3(yb[i][:]), in0=x3(self.ps[bo]), in1=bc8(ea_all[:, t, :]), op=ALU.mult), ["sea"], [("ps", bo), K("sy")])
            yield
            self.dve(lambda: nc.vector.tensor_add(out=yb[i][:], in0=yb[i][:], in1=self.ps[by]), [], [("ps", by), K("sy")])
            yield
            self.dve(lambda: nc.vector.tensor_mul(out=yb[i][:], in0=yb[i][:], in1=zs_all[:, t, :]), [], [K("sy")])
            yield
            self.act(junk[:], yb[i][:], AF.Square, reads=[K("sy")], writes=["sjunk", K("sssq")], accum=ssq[i][:])
            yield
            self.rstd(ssq[i][:], ssq[i][:], 1.0 / 512, EPS, [], K("sssq"))
            yield
            self.dve(lambda: nc.vector.scalar_tensor_tensor(out=gt[i][:], in0=yb[i][:], scalar=ssq[i][:, 0:1], in1=nwbc[:], op0=ALU.mult, op1=ALU.mult), [K("sy"), K("sssq"), "snwbc"], [K("sgt")])
            yield
            kb.group("pe", [lambda j=j: nc.tensor.transpose(self.psb[:, j * 128:(j + 1) * 128], gt[i][:, j * 128:(j + 1) * 128], ident16) for j in range(4)],
                     reads=[K("sgt"), "c16"], writes=[("ps", 7)])
            yield
            self.act(self.o_c[:, 4 * g:4 * g + 4, ts_], self.psb[:, 0:512].rearrange("p (c n) -> p c n", c=4), AF.Identity, reads=[], writes=[("ps", 7), "o_c"])
            yield

        def interleave(ga, gb, ra=1, rb=1):
            alive = [ga, gb]
            while alive:
                for g_, n_ in ((ga, ra), (gb, rb)):
                    if g_ in alive:
                        for _ in range(n_):
                            try:
                                next(g_)
                            except StopIteration:
                                alive.remove(g_)
                                break

        for _ in stageA(0):
            pass
        for t in range(NT):
            if t + 1 < NT:
                interleave(stageA(t + 1), stageB(t))
            else:
                for _ in stageB(t):
                    pass

    def merge(self, ph, l):
        nc = self.nc
        kb = self.kb
        kb.mark("merge")
        wg = [self.wg_pre, self.sbt(ph, "wgate", [128, 8, 3, 128], BF16)]
        wb = [self.wb_pre, self.sbt(ph, "wbr", [128, 16, 128], BF16)]
        sig = [self.sbt(ph, "msig", [128, 512], F32) for _ in range(6)]
        acc = [self.sbt(ph, "macc", [128, 512], F32) for _ in range(2)]
        srcs = [(self.o_a, 4, 0, "o_a"), (self.o_b, 4, 4, "o_b"), (self.o_c, 8, 8, "o_c")]
        wo = self.wo
        wosrc = self.d["w_o"][l].rearrange("p (k n) -> p k n", k=8)
        si = 0
        ai = 0
        for m in range(8):
            g_, gk = wg[m % 2], ("wgate", m % 2)
            b_, bk_ = wb[m % 2], ("wbr", m % 2)
            if m > 0:
                self.wload(g_[:], self.d["w_gate"][l, m].rearrange("p (k x n) -> p k x n", k=8, x=3), gk)
                self.wload(b_[:], self.d["w_br"][l, m].rearrange("p (k n) -> p k n", k=16), bk_)
            if m == 1:
                for half in range(2):
                    self.wload(wo[:, :, half * 512:(half + 1) * 512], wosrc[:, :, half * 512:(half + 1) * 512], ("wo", half))
            if m == 6:
                for half in range(2):
                    for kc in range(8):
                        self.dve(lambda kc=kc, half=half: nc.vector.tensor_tensor(out=wo[:, kc, half * 512:(half + 1) * 512], in0=wo[:, kc, half * 512:(half + 1) * 512], in1=self.gbc[:, 0, half * 512:(half + 1) * 512], op=ALU.mult),
                                 [("gbc", 0)], [("wo", half)])
            for b in range(NB):
                bs = slice(b * 512, (b + 1) * 512)
                ac, ack = acc[ai % 2], ("macc", ai % 2)
                ai += 1
                for x, (src, nch, off, skey) in enumerate(srcs):
                    bg = self.nb()
                    self.mm(self.ps[bg], [(g_[:, kc, x, :], self.hT[:, kc, bs]) for kc in range(8)], bg, reads=[gk, ("hT", b)])
                    sg, sgk = sig[si % 6], ("msig", si % 6)
                    si += 1
                    self.act(sg[:], self.ps[bg], AF.Sigmoid, reads=[], writes=[("ps", bg), sgk])
                    by = self.nb()
                    self.mm(self.ps[by], [(b_[:, off + c, :], src[:, c, bs]) for c in range(nch)], by, reads=[bk_, skey])
                    if x == 0:
                        self.dve(lambda ac=ac, sg=sg, by=by: nc.vector.tensor_mul(out=ac[:], in0=sg[:], in1=self.ps[by]), [sgk], [("ps", by), ack])
                    else:
                        self.dve(lambda sg=sg, by=by: nc.vector.tensor_mul(out=sg[:], in0=sg[:], in1=self.ps[by]), [], [("ps", by), sgk])
                        if x == 1:
                            self.dve(lambda ac=ac, sg=sg: nc.vector.tensor_add(out=ac[:], in0=ac[:], in1=sg[:]), [sgk], [ack])
                        else:
                            self.dve(lambda ac=ac, sg=sg, m=m, bs=bs: nc.vector.tensor_add(out=self.mergedT[:, m, bs], in0=ac[:], in1=sg[:]), [sgk, ack], ["merged"])

    def wout_phase(self, ph, l, xin, xout1):
        nc = self.nc
        kb = self.kb
        kb.mark("wout_phase")
        wo = self.wo
        xn = [self.sbt(ph, "xnew", [128, D], F32) for _ in range(2)]

        def compute(t):
            ts_ = slice(t * 128, (t + 1) * 128)
            xt, xkeys = self.load_x(ph, xin, t)
            xo, xok = xn[t % 2], ("xnew", t % 2)
            for half in range(2):
                hs = slice(half * 512, (half + 1) * 512)
                bk = self.nb()
                self.mm(self.ps[bk], [(self.mergedT[:, kc, ts_], wo[:, kc, hs]) for kc in range(8)], bk, reads=[("wo", half), "merged"])
                self.dve(lambda xo=xo, xt=xt, bk=bk, hs=hs: nc.vector.tensor_add(out=xo[:, hs], in0=self.ps[bk], in1=xt[:, hs]), list(xkeys), [("ps", bk), (xok, half)])
            kb.dma("sp", xout1[ts_, :], xo[:], reads=[(xok, 0), (xok, 1)], writes=[("xo1", t)])
            if t == 0:
                self.dump("xo0_%d" % l, xo[:], [128, D], [(xok, 0), (xok, 1)])
                self.dump("wo_%d" % l, wo[:], [128, 8, D], [("wo", 0), ("wo", 1)], BF16)
            self.norm_a(ph, (xo, [(xok, 0), (xok, 1)]), t)
        compute(0)
        for t in range(NT):
            if t + 1 < NT:
                compute(t + 1)
            self.norm_b(self.s2, self.b2, t, skey=("s2",), bkey=(("modp", 3),), mixed=True)

    def ffn(self, ph, l, xmid, xout2):
        nc = self.nc
        kb = self.kb
        kb.mark("ffn")
        actT = self.sbt(ph, "actT", [128, NJ, S], BF16)
        wd0 = self.sbt(ph, "wdn", [128, NJ, 512], BF16)
        wdsrc = self.d["w_dn"][l].rearrange("p (k n) -> p k n", k=NJ)
        with ExitStack() as up:
            wup = [self.sbt(up, "wup", [128, 8, 256], BF16) for _ in range(2)]
            raw = [self.sbt(up, "fraw", [128, 2, 2 + S], BF16) for _ in range(2)]
            dg = [self.sbt(up, "fdiag", [128, 3, 2, 128], BF16) for _ in range(2)]
            sg = [self.sbt(up, "fsil", [128, 512], F32) for _ in range(3)]
            vacc = [self.sbt(up, "fvacc", [128, 512], F32) for _ in range(2)]
            si = 0
            for p_ in range(2):
                self.dve(lambda p_=p_: nc.vector.memset(raw[p_][:, :, 0:2], 0.0), [], [("frawpad", p_)])
            def prep(j):
                p_ = j % 2
                self.wload(wup[p_][:], self.d["w_up"][l, j].rearrange("p (k n) -> p k n", k=8), ("wup", p_))
                if j == 2:
                    self.wload(wd0[:], wdsrc[:, :, 0:512], ("wdn", 0))
                for k in range(3):
                    self.dve(lambda k=k, p_=p_: nc.vector.tensor_scalar(out=dg[p_][:, k, 0, :], in0=self.k16("ident"), scalar1=self.vpc("ffn_conv_w", k * 44 + j), scalar2=None, op0=ALU.mult),
                             ["c16", "vp"], [("fdiag", p_)])

            def proj(step):
                j, b = divmod(step, NB)
                p_ = j % 2
                w_, wk = wup[p_], ("wup", p_)
                bs = slice(b * 512, (b + 1) * 512)
                for x in range(2):
                    bk = 2 * (step % 2) + x
                    self.mm(self.ps[bk], [(w_[:, kc, x * 128:(x + 1) * 128], self.hT[:, kc, bs]) for kc in range(8)], bk, reads=[wk, ("hT", b)])
                    if x == 0:
                        self.act(raw[p_][:, x, 2 + b * 512:2 + (b + 1) * 512], self.ps[bk], AF.Identity, reads=[], writes=[("ps", bk), ("fraw", p_, x, b)])
                    else:
                        self.dve(lambda p_=p_, x=x, b=b, bk=bk: nc.vector.tensor_copy(out=raw[p_][:, x, 2 + b * 512:2 + (b + 1) * 512], in_=self.ps[bk]), [], [("ps", bk), ("fraw", p_, x, b)])

            def conv(step):
                nonlocal si
                j, b = divmod(step, NB)
                p_ = j % 2
                bs = slice(b * 512, (b + 1) * 512)
                bk = 4 + (step % 2)
                rd = [("fdiag", p_), ("fraw", p_, 0, b), ("frawpad", p_)] + ([("fraw", p_, 0, b - 1)] if b > 0 else [])
                self.mm(self.ps[bk], [(dg[p_][:, k, 0, :], raw[p_][:, 0, b * 512 + k:b * 512 + k + 512]) for k in range(3)], bk, reads=rd)
                s_, sk = sg[si % 3], ("fsil", si % 3)
                va, vk = vacc[si % 2], ("fvacc", si % 2)
                si += 1
                self.act(s_[:], self.ps[bk], AF.Silu, reads=["vp"], writes=[("ps", bk), sk], bias=self.vpc("ffn_conv_b", j))
                rdv = [("fraw", p_, 1, b), ("frawpad", p_), "vp"] + ([("fraw", p_, 1, b - 1)] if b > 0 else [])
                rv = lambda k: raw[p_][:, 1, b * 512 + k:b * 512 + k + 512]
                chv = NJ + j
                self.dve(lambda: nc.vector.tensor_scalar(out=va[:], in0=rv(0), scalar1=self.vpc("ffn_conv_w", 0 * 44 + chv), scalar2=self.vpc("ffn_conv_b", chv), op0=ALU.mult, op1=ALU.add), rdv, [vk])
                self.dve(lambda: nc.vector.scalar_tensor_tensor(out=va[:], in0=rv(1), scalar=self.vpc("ffn_conv_w", 1 * 44 + chv), in1=va[:], op0=ALU.mult, op1=ALU.add), rdv, [vk])
                self.dve(lambda: nc.vector.scalar_tensor_tensor(out=va[:], in0=rv(2), scalar=self.vpc("ffn_conv_w", 2 * 44 + chv), in1=va[:], op0=ALU.mult, op1=ALU.add), rdv, [vk])
                self.dve(lambda: nc.vector.tensor_mul(out=actT[:, j, bs], in0=va[:], in1=s_[:]), [sk, vk], [("actT", b)])
            nsteps = NJ * NB
            prep(0)
            proj(0)
            for step in range(nsteps):
                if step + 1 < nsteps:
                    if (step + 1) % NB == 0:
                        prep((step + 1) // NB)
                    proj(step + 1)
                conv(step)
            kb.barrier()
        self.dump("actT%d" % l, actT[:], [128, NJ, S], [("actT", b) for b in range(NB)], BF16)
        with ExitStack() as dn:
            wd = [wd0, self.sbt(dn, "wdn", [128, NJ, 512], BF16)]
            xh = [self.sbt(dn, "xh", [128, 512], F32) for _ in range(2)]
            xn = [self.sbt(dn, "xnew2", [128, 512], F32) for _ in range(2)]
            wsrc = self.d["w_dn"][l].rearrange("p (k n) -> p k n", k=NJ)
            self.wload(wd[1][:], wsrc[:, :, 512:1024], ("wdn", 1))
            ci = 0
            for half in range(2):
                hs = slice(half * 512, (half + 1) * 512)
                for t in range(NT):
                    ts_ = slice(t * 128, (t + 1) * 128)
                    i = ci % 2
                    ci += 1
                    kb.dma("sp", xh[i][:], xmid[ts_, hs], reads=[("xo1", t)] if False else [], writes=[("xh", i)])
                    bk = self.nb()
                    self.mm(self.ps[bk], [(actT[:, j, ts_], wd[half][:, j, :]) for j in range(NJ)], bk, reads=[("wdn", half), ("actT", t // 4)])
                    self.dve(lambda i=i, bk=bk, hs=hs: nc.vector.tensor_mul(out=xn[i][:], in0=self.ps[bk], in1=self.gbc[:, 1, hs]), [("gbc", 1)], [("ps", bk), ("xnew2", i)])
                    self.dve(lambda i=i: nc.vector.tensor_add(out=xn[i][:], in0=xn[i][:], in1=xh[i][:]), [("xh", i)], [("xnew2", i)])
                    kb.dma("sp", xout2[ts_, hs], xn[i][:], reads=[("xnew2", i)], writes=[("xo2", t, half)])
            kb.barrier()


_CACHE = {}


def make_in_maps(inputs, n_cores=8):
    shared = host_layout(inputs)
    x = np.asarray(inputs["x"], np.float32)
    c = np.asarray(inputs["c"], np.float32)
    pos = np.asarray(inputs["positions"], np.int32)
    maps = []
    for b in range(n_cores):
        m = dict(shared)
        m["x"] = np.ascontiguousarray(x[b])
        m["cT"] = np.ascontiguousarray(c[b].reshape(8, 128).T)
        m["pos"] = np.ascontiguousarray(pos[b:b + 1])
        maps.append(m)
    return maps


def kernel(**inputs):
    maps = make_in_maps(inputs, 8)
    prog = Prog()
    nc = prog.build()
    res = run_bass_kernel_spmd(nc, maps, core_ids=list(range(8)))
    out = np.stack([np.asarray(res.results[b]["out"], np.float32).reshape(S, D) for b in range(8)], axis=0)
    return out
```

```python
import math
from contextlib import ExitStack

import numpy as np
import concourse.bass as bass
import concourse.mybir as mybir
from concourse.bass_utils import run_bass_kernel_spmd

F32 = mybir.dt.float32
BF16 = mybir.dt.bfloat16
I32 = mybir.dt.int32
AF = mybir.ActivationFunctionType
ALU = mybir.AluOpType
AX = mybir.AxisListType

S = 2048
D = 1024
NT = 16
NB = 4
L = 2
NH = 8
EPS = 1e-6
FF = 2816
NJ = 22
IN_DIM = 6832
SM_SCALE = 96 ** -0.5
TWO_PI = 2.0 * math.pi


class KB:
    NDSEM = 24

    def __init__(self, nc, stack):
        self.nc = nc
        self.eng = {"pe": nc.tensor, "act": nc.scalar, "dve": nc.vector,
                    "pool": nc.gpsimd, "sp": nc.sync}
        self.sem = {}
        self.cnt = {}
        for e in self.eng:
            self.sem[e] = stack.enter_context(nc.semaphore("s_" + e))
            self.cnt[e] = 0
        self.dsem = [stack.enter_context(nc.semaphore("d%d" % i)) for i in range(self.NDSEM)]
        self.dcnt = [0] * self.NDSEM
        self.dnext = 0
        self.semobj = dict(self.sem)
        for i, s in enumerate(self.dsem):
            self.semobj[("d", i)] = s
        self.waited = {}
        self.last_w = {}
        self.readers = {}
        self.nwaits = 0
        self.nops = 0
        self.npe = 0
        self.marks = []

    def mark(self, name):
        self.marks.append((name, self.npe))

    def _deps(self, reads, writes):
        deps = {}

        def add(d):
            if d is None:
                return
            sk, v = d
            if deps.get(sk, 0) < v:
                deps[sk] = v
        for r in reads:
            add(self.last_w.get(r))
        for w in writes:
            add(self.last_w.get(w))
            for d in self.readers.get(w, ()):
                add(d)
        return deps

    def _emit_waits(self, e, deps, skip_self=False):
        for sk, v in deps.items():
            if skip_self and sk == e:
                continue
            if self.waited.get((e, sk), 0) >= v:
                continue
            self.eng[e].wait_ge(self.semobj[sk], v)
            self.waited[(e, sk)] = v
            self.nwaits += 1

    def _record(self, mark, reads, writes):
        for w in writes:
            self.last_w[w] = mark
            self.readers[w] = []
        for r in reads:
            self.readers.setdefault(r, []).append(mark)

    def group(self, e, fns, reads=(), writes=()):
        deps = self._deps(reads, writes)
        self._emit_waits(e, deps, skip_self=(e == "pe"))
        ins = None
        for fn in fns:
            ins = fn()
        if e == "pe":
            self.npe += len(fns)
        self.cnt[e] += 1
        ins.then_inc(self.sem[e], 1)
        self._record((e, self.cnt[e]), reads, writes)
        self.nops += len(fns)
        return ins

    def op(self, e, fn, reads=(), writes=()):
        return self.group(e, [fn], reads, writes)

    def dma(self, q, out, in_, reads=(), writes=(), **kw):
        deps = self._deps(reads, writes)
        j = self.dnext
        self.dnext = (self.dnext + 1) % self.NDSEM
        if self.dcnt[j] > 0:
            deps[("d", j)] = max(deps.get(("d", j), 0), self.dcnt[j])
        self._emit_waits(q, deps)
        ins = self.eng[q].dma_start(out=out, in_=in_, **kw)
        self.dcnt[j] += 16
        ins.then_inc(self.dsem[j], 16)
        self._record((("d", j), self.dcnt[j]), reads, writes)
        self.nops += 1
        return ins

    def barrier(self, name=None):
        self.mark("barrier")
        for e in self.eng:
            deps = {}
            for f in self.eng:
                if f != e and self.cnt[f] > 0:
                    deps[f] = self.cnt[f]
            for j in range(self.NDSEM):
                if self.dcnt[j] > 0:
                    deps[("d", j)] = self.dcnt[j]
            self._emit_waits(e, deps)
        self.last_w.clear()
        self.readers.clear()

    def finish(self):
        deps = {}
        for f in self.eng:
            if f != "sp" and self.cnt[f] > 0:
                deps[f] = self.cnt[f]
        for j in range(self.NDSEM):
            if self.dcnt[j] > 0:
                deps[("d", j)] = self.dcnt[j]
        self._emit_waits("sp", deps)


VP = {}
_off = 0
for _n, _w in [("ada_b", 48), ("norm1_w", 8), ("norm2_w", 8), ("q_a_norm", 3), ("kv_a_norm", 2),
               ("qwA", 1), ("qwB", 1), ("kwA", 1), ("kwB", 1), ("pool_scale", 4),
               ("ssd_conv_w", 48), ("ssd_conv_b", 12), ("ssd_norm_w", 8),
               ("ffn_conv_w", 132), ("ffn_conv_b", 44)]:
    VP[_n] = (_off, _w)
    _off += _w
NV = _off

C16 = {}
_off = 0
for _n, _w in [("ident", 128), ("ones", 128), ("bd", 128), ("maskneg", 128), ("mask01", 128),
               ("bcur", 512), ("bprev", 512), ("bcur0", 512), ("sel16", 1024), ("maskneg4", 512)]:
    C16[_n] = (_off, _w)
    _off += _w
N16 = _off
C32 = {}
_off = 0
for _n, _w in [("U", 128), ("ones", 128), ("sel", 1024), ("invf", 1), ("sgn", 1), ("ident", 128), ("invf2", 1), ("sgn2", 1)]:
    C32[_n] = (_off, _w)
    _off += _w
N32 = _off


def _kc(w):
    k, n = w.shape
    return np.ascontiguousarray(w.reshape(k // 128, 128, n).transpose(1, 0, 2)).reshape(128, (k // 128) * n)


def _col(v):
    return np.ascontiguousarray(v.reshape(-1, 128).T)


def host_constants():
    c16 = np.zeros((128, N16), np.float32)
    c32 = np.zeros((128, N32), np.float32)
    i = np.arange(128)
    o, w = C16["ident"]; c16[:, o:o + w] = np.eye(128)
    o, w = C16["ones"]; c16[:, o:o + w] = 1.0
    bd = np.zeros((128, 128), np.float32)
    bd[0:64, 0:64] = 1.0 / 64
    bd[64:96, 64:96] = 1.0 / 32
    o, w = C16["bd"]; c16[:, o:o + w] = bd
    o, w = C16["maskneg"]; c16[:, o:o + w] = np.where(i[:, None] > i[None, :], -30000.0, 0.0)
    o, w = C16["mask01"]; c16[:, o:o + w] = (i[:, None] <= i[None, :]).astype(np.float32)
    o, w = C16["maskneg4"]; c16[:, o:o + w] = np.tile(np.where(i[:, None] > i[None, :], -30000.0, 0.0), (1, 4))
    for g, win in enumerate((2, 4, 8, 16)):
        s = i[:, None]
        t = i[None, :]
        cur = np.where((t - s >= 0) & (t - s < win), 1.0 / win, 0.0) - (s == t)
        prev = np.where((t + 128 - s) < win, 1.0 / win, 0.0)
        cnt = np.minimum(t + 1, win).astype(np.float32)
        cur0 = np.where((t - s >= 0) & (t - s < win), 1.0 / cnt, 0.0) - (s == t)
        o, _ = C16["bcur"]; c16[:, o + g * 128:o + (g + 1) * 128] = cur
        o, _ = C16["bprev"]; c16[:, o + g * 128:o + (g + 1) * 128] = prev
        o, _ = C16["bcur0"]; c16[:, o + g * 128:o + (g + 1) * 128] = cur0
    sel16 = np.zeros((128, 8, 128), np.float32)
    for r in range(8):
        sel16[r, r, :] = 1.0
        sel16[32 + r, r, :] = 1.0
    o, w = C16["sel16"]; c16[:, o:o + w] = sel16.reshape(128, 1024)
    o, w = C32["U"]; c32[:, o:o + w] = (i[:, None] <= i[None, :]).astype(np.float32)
    o, w = C32["ones"]; c32[:, o:o + w] = 1.0
    o, w = C32["ident"]; c32[:, o:o + w] = np.eye(128)
    sel = np.zeros((128, 8, 128), np.float32)
    for r in range(8):
        sel[r, r, :] = 1.0
    o, w = C32["sel"]; c32[:, o:o + w] = sel.reshape(128, 1024)
    inv_freq = (10000.0 ** (-np.arange(0, 32, 2, dtype=np.float32) / np.float32(32))).astype(np.float32)
    invf = np.zeros(128, np.float32)
    invf[64:80] = inv_freq
    invf[80:96] = inv_freq
    sgn = np.zeros(128, np.float32)
    sgn[64:80] = -1.0
    sgn[80:96] = 1.0
    c32[:, C32["invf"][0]] = invf
    c32[:, C32["sgn"][0]] = sgn
    c32[:, C32["invf2"][0]] = np.tile(np.concatenate([inv_freq, inv_freq]), 4)
    c32[:, C32["sgn2"][0]] = np.tile(np.concatenate([-np.ones(16, np.float32), np.ones(16, np.float32)]), 4)
    return c16, c32


def host_layout(inp):
    f = lambda a: np.asarray(a, dtype=np.float32)
    w_in = f(inp["w_in"]); w_q_b = f(inp["w_q_b"]); w_kv_b = f(inp["w_kv_b"])
    out = {}
    z64 = np.zeros((1024, 64), np.float32)
    w_attn, w_qb, w_kn, w_v, w_u, w_pool, w_ssd, w_gate, w_br, w_o, w_up, w_dn, w_ada, vp = ([] for _ in range(14))
    for l in range(L):
        wi = w_in[l]
        krA = np.concatenate([z64, wi[:, 640:672]], axis=1)
        krB = np.concatenate([z64, wi[:, 656:672], wi[:, 640:656]], axis=1)
        w_attn.append(_kc(np.concatenate([wi[:, 0:640], krA, krB], axis=1)))
        qb = w_q_b[l]
        z = np.zeros((384, 64), np.float32)
        cols = []
        for h in range(NH):
            cols.append(qb[:, 96 * h:96 * h + 96])
            cols.append(np.concatenate([z, qb[:, 96 * h + 80:96 * h + 96], qb[:, 96 * h + 64:96 * h + 80]], axis=1))
        w_qb.append(_kc(np.concatenate(cols, axis=1)))
        kvb = w_kv_b[l].reshape(256, NH, 128)
        w_kn.append(_kc(np.ascontiguousarray(kvb[:, :, 0:64]).reshape(256, 512)))
        w_v.append(_kc(np.ascontiguousarray(kvb[:, :, 64:128]).reshape(256, 512)))
        w_u.append(_kc(wi[:, 672:1184]))
        w_pool.append(np.ascontiguousarray(f(inp["pool_w"])[l].transpose(1, 0, 2)).reshape(128, 512))
        gs = []
        for g in range(2):
            xb = 2208
            parts = [wi[:, xb + 512 * g + 128 * j: xb + 512 * g + 128 * (j + 1)] for j in range(4)]
            parts.append(wi[:, xb + 1024 + 128 * g: xb + 1024 + 128 * (g + 1)])
            parts.append(wi[:, xb + 1280 + 128 * g: xb + 1280 + 128 * (g + 1)])
            parts.append(wi[:, 1184 + 512 * g:1184 + 512 * (g + 1)])
            parts.append(wi[:, 3744 + 8 * g:3744 + 8 * (g + 1)])
            gs.append(_kc(np.concatenate(parts, axis=1)))
        w_ssd.append(np.stack(gs))
        gm = []
        bm = []
        for m in range(8):
            gm.append(_kc(np.concatenate([wi[:, 3760 + x * 1024 + m * 128:3760 + x * 1024 + (m + 1) * 128] for x in range(3)], axis=1)))
            bm.append(_kc(f(inp["w_branch"])[l][:, m * 128:(m + 1) * 128]))
        w_gate.append(np.stack(gm)); w_br.append(np.stack(bm))
        w_o.append(_kc(f(inp["w_out"])[l]))
        fu = f(inp["ffn_up"])[l]
        w_up.append(np.stack([_kc(np.concatenate([fu[:, j * 128:(j + 1) * 128], fu[:, FF + j * 128:FF + (j + 1) * 128]], axis=1)) for j in range(NJ)]))
        w_dn.append(_kc(f(inp["ffn_down"])[l]))
        aw = f(inp["ada_w"])[l]
        w_ada.append(np.stack([_kc(aw[:, i * 1024:(i + 1) * 1024]) for i in range(6)]))
        v = np.zeros((128, NV), np.float32)

        def put(name, arr):
            o, w = VP[name]
            assert arr.shape == (128, w), (name, arr.shape)
            v[:, o:o + w] = arr
        put("ada_b", _col(f(inp["ada_b"])[l]))
        put("norm1_w", _col(f(inp["norm1_w"])[l]))
        put("norm2_w", _col(f(inp["norm2_w"])[l]))
        put("q_a_norm", _col(f(inp["q_a_norm"])[l]))
        put("kv_a_norm", _col(f(inp["kv_a_norm"])[l]))
        for nm, src in (("q", f(inp["q_norm"])[l]), ("k", f(inp["k_norm"])[l])):
            a = np.zeros((128, 1), np.float32)
            b = np.zeros((128, 1), np.float32)
            a[0:96, 0] = src
            b[64:80, 0] = src[80:96]
            b[80:96, 0] = src[64:80]
            put(nm + "wA", a); put(nm + "wB", b)
        put("pool_scale", _col(f(inp["pool_scale"])[l]))
        cw = f(inp["ssd_conv_w"])[l]
        put("ssd_conv_w", np.concatenate([_col(cw[k]) for k in range(4)], axis=1))
        put("ssd_conv_b", _col(f(inp["ssd_conv_b"])[l]))
        put("ssd_norm_w", _col(f(inp["ssd_norm_w"])[l]))
        fw_ = f(inp["ffn_conv_w"])[l]
        put("ffn_conv_w", np.concatenate([_col(fw_[k]) for k in range(3)], axis=1))
        put("ffn_conv_b", _col(f(inp["ffn_conv_b"])[l]))
        vp.append(v)
    st = lambda xs: np.ascontiguousarray(np.stack(xs))
    out["w_attn"] = st(w_attn); out["w_qb"] = st(w_qb); out["w_kn"] = st(w_kn); out["w_v"] = st(w_v)
    out["w_u"] = st(w_u); out["w_pool"] = st(w_pool); out["w_ssd"] = st(w_ssd); out["w_gate"] = st(w_gate)
    out["w_br"] = st(w_br); out["w_o"] = st(w_o); out["w_up"] = st(w_up); out["w_dn"] = st(w_dn)
    out["w_ada"] = st(w_ada); out["vp"] = st(vp)
    out["ada_b"] = np.ascontiguousarray(f(inp["ada_b"]))
    out["ssd_small"] = np.ascontiguousarray(np.concatenate([f(inp["ssd_d"]), f(inp["ssd_dt_bias"]), f(inp["ssd_a_log"])], axis=1))
    out["ssd_nw"] = np.ascontiguousarray(f(inp["ssd_norm_w"]))
    c16, c32 = host_constants()
    out["c16"] = c16; out["c32"] = c32
    return out


SHARED_SHAPES = {
    "w_attn": [L, 128, 8 * 832], "w_qb": [L, 128, 3 * 8 * 192], "w_kn": [L, 128, 1024], "w_v": [L, 128, 1024],
    "w_u": [L, 128, 8 * 512], "w_pool": [L, 128, 512], "w_ssd": [L, 2, 128, 8 * 1288],
    "w_gate": [L, 8, 128, 8 * 384], "w_br": [L, 8, 128, 16 * 128], "w_o": [L, 128, 8 * 1024],
    "w_up": [L, NJ, 128, 8 * 256], "w_dn": [L, 128, NJ * 1024], "w_ada": [L, 6, 128, 8 * 1024],
    "vp": [L, 128, NV], "ada_b": [L, 6144], "ssd_small": [L, 48], "ssd_nw": [L, 1024], "c16": [128, N16], "c32": [128, N32],
}


class Prog:
    def __init__(self, nlayers=L, stop_after=None, dumps=()):
        self.nlayers = nlayers
        self.stop_after = stop_after
        self.dumps = set(dumps)
        self.dump_specs = {}
        nc = bass.Bass("TRN2", target_bir_lowering=False)
        self.nc = nc
        self.d = {}
        for n, shp in SHARED_SHAPES.items():
            self.d[n] = nc.dram_tensor(n, shp, F32, kind="ExternalInput").ap()
        self.d["x"] = nc.dram_tensor("x", [S, D], F32, kind="ExternalInput").ap()
        self.d["cT"] = nc.dram_tensor("cT", [128, 8], F32, kind="ExternalInput").ap()
        self.d["pos"] = nc.dram_tensor("pos", [1, S], I32, kind="ExternalInput").ap()
        self.d["out"] = nc.dram_tensor("out", [S, D], F32, kind="ExternalOutput").ap()
        self.xs = [nc.dram_tensor("xs%d" % i, [S, D], F32).ap() for i in range(3)]
        self.bk = 0
        self.bk5 = 0
        self.uid = 0

    def nb(self):
        b = self.bk
        self.bk = (b + 1) % 7
        return b

    def nb5(self):
        b = self.bk5
        self.bk5 = (b + 1) % 5
        return b

    def sbt(self, st, name, shape, dt):
        self.uid += 1
        return st.enter_context(self.nc.sbuf_tensor("%s_%d" % (name, self.uid), shape, dt))

    def dump(self, name, ap, shape, key, dt=F32):
        if name not in self.dumps:
            return
        dr = self.nc.dram_tensor("dbg_" + name, shape, dt, kind="ExternalOutput").ap()
        self.dump_specs[name] = shape
        self.kb.dma("sp", dr, ap, reads=[key] if not isinstance(key, list) else key)

    def mm(self, out, pairs, bank, reads, start=True):
        nc = self.nc
        n = len(pairs)
        fns = []
        for i, (l_, r_) in enumerate(pairs):
            fns.append(lambda i=i, l_=l_, r_=r_: nc.tensor.matmul(out, lhsT=l_, rhs=r_, start=(start and i == 0), stop=(i == n - 1)))
        self.kb.group("pe", fns, reads=reads, writes=[("ps", bank)])

    def act(self, out, in_, func, reads, writes, bias=0.0, scale=1.0, accum=None):
        nc = self.nc
        if accum is None:
            fn = lambda: nc.scalar.activation(out=out, in_=in_, func=func, bias=bias, scale=scale)
        else:
            fn = lambda: nc.scalar.activation(out=out, in_=in_, func=func, bias=bias, scale=scale, accum_out=accum)
        self.kb.op("act", fn, reads=reads, writes=writes)

    def rstd(self, out, in_, scale, bias, in_keys, wkey):
        self.act(out, in_, AF.Ln, reads=[], writes=list(in_keys) + [wkey], bias=bias, scale=scale)
        self.act(out, out, AF.Exp, reads=[], writes=[wkey], scale=-0.5)

    def run_chains(self, factories, width):
        todo = list(factories)
        free = list(range(width))
        active = []

        def refill():
            while todo and free:
                sl = free.pop(0)
                active.append((todo.pop(0)(sl), sl))
        refill()
        while active:
            for ent in list(active):
                try:
                    next(ent[0])
                except StopIteration:
                    active.remove(ent)
                    free.append(ent[1])
            refill()

    def dve(self, fn, reads, writes):
        self.kb.op("dve", fn, reads=reads, writes=writes)

    def wload(self, dst, src, key):
        self.kb.dma("pool", dst, src, writes=[key])

    def build(self):
        nc = self.nc
        with ExitStack() as st:
            self.kb = KB(nc, st)
            kb = self.kb
            ps_all = st.enter_context(nc.psum_tensor("ps_all", [128, 7 * 512], F32))
            self.ps = [ps_all[:, i * 512:(i + 1) * 512] for i in range(7)]
            self.psb = st.enter_context(nc.psum_tensor("psb", [128, 1024], BF16))
            self.c16 = self.sbt(st, "c16", [128, N16], BF16)
            self.c32 = self.sbt(st, "c32", [128, N32], F32)
            kb.dma("pool", self.c16[:], self.d["c16"], writes=["c16"])
            kb.dma("sp", self.c32[:], self.d["c32"], writes=["c32"])
            self.k16 = lambda n, j=0: self.c16[:, C16[n][0] + j * 128:C16[n][0] + (j + 1) * 128]
            self.k32 = lambda n, j=0: self.c32[:, C32[n][0] + j * 128:C32[n][0] + (j + 1) * 128]
            self.rope_tables(st)
            self.cact(st)
            kb.barrier()
            xin = self.d["x"]
            for l in range(self.nlayers):
                xout1 = self.xs[2 * l % 3]
                xout2 = self.d["out"] if l == self.nlayers - 1 else self.xs[(2 * l + 1) % 3]
                with ExitStack() as lst:
                    self.layer(lst, l, xin, xout1, xout2)
                    kb.barrier()
                xin = xout2
                if self.stop_after is not None and self.stop_after[0] == l:
                    break
            kb.finish()
        return nc

    def rope_tables(self, st):
        nc = self.nc
        kb = self.kb
        self.rope_c = nc.dram_tensor("rope_c", [128, S], F32).ap()
        self.rope_s = nc.dram_tensor("rope_s", [128, S], F32).ap()
        with ExitStack() as t:
            Cs = self.sbt(t, "Ctab", [128, 512], F32)
            Ss = self.sbt(t, "Stab", [128, 512], F32)
            pi_ = self.sbt(t, "posi", [128, 512], I32)
            v = self.sbt(t, "ropev", [128, 512], F32)
            ki = self.sbt(t, "ropeki", [128, 512], I32)
            kf = self.sbt(t, "ropekf", [128, 512], F32)
            w = self.sbt(t, "ropew", [128, 512], F32)
            fill = self.sbt(t, "ropefill", [64, S], F32)
            for q in range(4):
                kb.dma("sp", pi_[q * 32:(q + 1) * 32, :], self.d["pos"][0:1, q * 512:(q + 1) * 512].partition_broadcast(32), writes=[("posi", q)])
            invf = self.c32[:, C32["invf2"][0]:C32["invf2"][0] + 1]
            sgn = self.c32[:, C32["sgn2"][0]:C32["sgn2"][0] + 1]
            self.dve(lambda: nc.vector.memset(fill[:], 0.0), [], ["ropefill"])
            kb.dma("sp", self.rope_s[0:64, :], fill[:], reads=["ropefill"], writes=["fillz0"])
            kb.dma("sp", self.rope_s[96:128, :], fill[0:32, :], reads=["ropefill"], writes=["fillz1"])
            self.dve(lambda: nc.vector.memset(fill[:], 1.0), ["fillz0", "fillz1"], ["ropefill"])
            kb.dma("sp", self.rope_c[0:64, :], fill[:], reads=["ropefill"])
            kb.dma("sp", self.rope_c[96:128, :], fill[0:32, :], reads=["ropefill"])
            self.dve(lambda: nc.vector.tensor_copy(out=v[:], in_=pi_[:]), [("posi", q) for q in range(4)], ["ropev"])
            self.dve(lambda: nc.vector.tensor_scalar(out=v[:], in0=v[:], scalar1=invf, scalar2=1.0 / TWO_PI, op0=ALU.mult, op1=ALU.mult), ["ropev", "c32"], ["ropev"])
            for which, shift, dst in (("c", 0.25, Cs), ("s", 0.0, Ss)):
                self.dve(lambda shift=shift: nc.vector.tensor_scalar(out=w[:], in0=v[:], scalar1=shift, scalar2=None, op0=ALU.add), ["ropev"], ["ropew"])
                self.dve(lambda: nc.vector.tensor_copy(out=ki[:], in_=w[:]), ["ropew"], ["ropeki"])
                self.dve(lambda: nc.vector.tensor_copy(out=kf[:], in_=ki[:]), ["ropeki"], ["ropekf"])
                self.dve(lambda: nc.vector.tensor_sub(out=w[:], in0=w[:], in1=kf[:]), ["ropekf", "ropew"], ["ropew"])
                self.dve(lambda: nc.vector.tensor_single_scalar(out=kf[:], in_=w[:], scalar=0.5, op=ALU.is_gt), ["ropew"], ["ropekf"])
                self.dve(lambda: nc.vector.tensor_sub(out=w[:], in0=w[:], in1=kf[:]), ["ropekf", "ropew"], ["ropew"])
                self.dve(lambda: nc.vector.tensor_single_scalar(out=kf[:], in_=w[:], scalar=-0.5, op=ALU.is_lt), ["ropew"], ["ropekf"])
                self.dve(lambda: nc.vector.tensor_add(out=w[:], in0=w[:], in1=kf[:]), ["ropekf", "ropew"], ["ropew"])
                self.act(dst[:], w[:], AF.Sin, reads=["ropew"], writes=["tab" + which], scale=TWO_PI)
            self.dve(lambda: nc.vector.tensor_scalar(out=Ss[:], in0=Ss[:], scalar1=sgn, scalar2=None, op0=ALU.mult), ["tabs", "c32"], ["tabs"])
            for q in range(4):
                kb.dma("sp", self.rope_c[64:96, q * 512:(q + 1) * 512], Cs[q * 32:(q + 1) * 32, :], reads=["tabc"])
                kb.dma("sp", self.rope_s[64:96, q * 512:(q + 1) * 512], Ss[q * 32:(q + 1) * 32, :], reads=["tabs"])
            kb.barrier()

    def cact(self, st):
        nc = self.nc
        kb = self.kb
        cT = self.sbt(st, "cT", [128, 8], F32)
        ca = self.sbt(st, "cact", [128, 8], BF16)
        self.cbc = self.sbt(st, "cbc", [128, 8, 128], BF16)
        kb.dma("sp", cT[:], self.d["cT"], writes=["cT"])
        self.act(ca[:], cT[:], AF.Silu, reads=["cT"], writes=["cact"])
        self.cactb = ca
        self.dve(lambda: nc.vector.tensor_copy(out=self.cbc[:], in_=ca[:].unsqueeze(2).to_broadcast([128, 8, 128])), ["cact"], ["cbc"])

    def layer(self, st, l, xin, xout1, xout2):
        nc = self.nc
        kb = self.kb
        self.l = l
        self.vp = self.sbt(st, "vp", [128, NV], F32)
        kb.dma("sp", self.vp[:], self.d["vp"][l], writes=["vp"])
        self.vpc = lambda n, j=0, w=1: self.vp[:, VP[n][0] + j:VP[n][0] + j + w]
        self.ada(st, l)
        if self.stop_after == (l, "ada"):
            return
        self.hT = self.sbt(st, "hT", [128, 8, S], BF16)
        with ExitStack() as ph:
            self.norm_a(ph, self.load_x(ph, xin, 0), 0)
            for t in range(NT):
                if t + 1 < NT:
                    self.norm_a(ph, self.load_x(ph, xin, t + 1), t + 1)
                self.norm_b(self.s1, self.b1, t, mixed=True)
            kb.barrier()
        self.dump("hT%d" % l, self.hT[:], [128, 8, S], [("hT", b) for b in range(NB)], BF16)
        if self.stop_after == (l, "norm1"):
            return
        with ExitStack() as mix:
            self.mix(mix, l, xin, xout1)
            kb.barrier()
        if self.stop_after is not None and self.stop_after[0] == l and self.stop_after[1] != "ffn":
            return
        with ExitStack() as ph:
            self.ffn(ph, l, xout1, xout2)
            kb.barrier()

    def mix(self, st, l, xin, xout1):
        kb = self.kb
        self.o_a = self.sbt(st, "o_a", [128, 4, S], BF16)
        with ExitStack() as ph:
            self.attention(ph, l)
            kb.barrier()
        self.dump("o_a%d" % l, self.o_a[:], [128, 4, S], "o_a", BF16)
        if self.stop_after == (l, "attn"):
            return
        self.o_c = self.sbt(st, "o_c", [128, 8, S], BF16)
        for g in range(2):
            with ExitStack() as ph:
                self.ssd(ph, l, g)
                kb.barrier()
        self.dump("o_c%d" % l, self.o_c[:], [128, 8, S], "o_c", BF16)
        if self.stop_after == (l, "ssd"):
            return
        self.o_b = self.sbt(st, "o_b", [128, 4, S], BF16)
        self.mergedT = self.sbt(st, "mergedT", [128, 8, S], BF16)
        self.wg_pre = self.sbt(st, "wgate", [128, 8, 3, 128], BF16)
        self.wb_pre = self.sbt(st, "wbr", [128, 16, 128], BF16)
        with ExitStack() as ph:
            self.pool_branch(ph, l)
            kb.barrier()
        self.dump("o_b%d" % l, self.o_b[:], [128, 4, S], "o_b", BF16)
        if self.stop_after == (l, "pool"):
            return
        self.wo = self.sbt(st, "wo", [128, 8, D], BF16)
        with ExitStack() as ph:
            self.merge(ph, l)
            kb.barrier()
        self.dump("merged%d" % l, self.mergedT[:], [128, 8, S], "merged", BF16)
        if self.stop_after == (l, "merge"):
            return
        with ExitStack() as ph:
            self.wout_phase(ph, l, xin, xout1)
            kb.barrier()
        self.dump("h2T%d" % l, self.hT[:], [128, 8, S], [("hT", b) for b in range(NB)], BF16)

    def ada(self, st, l):
        nc = self.nc
        kb = self.kb
        kb.mark("ada")
        self.modp = self.sbt(st, "modp", [128, 6, 8], F32)
        self.gbc = self.sbt(st, "gbc", [128, 2, D], F32)
        self.s1 = self.sbt(st, "s1", [128, 8], F32)
        self.s2 = self.sbt(st, "s2", [128, 8], F32)
        with ExitStack() as ph:
            slots = [self.sbt(ph, "adaw", [128, 8, 1024], BF16) for _ in range(2)]
            abc = self.sbt(ph, "adab_bc", [128, 2, D], F32)
            for gi, pc in enumerate((2, 5)):
                kb.dma("sp", abc[:, gi, :], self.d["ada_b"][l:l + 1, pc * 1024:(pc + 1) * 1024].partition_broadcast(128), writes=[("abc", gi)])
            for i in range(6):
                sl = slots[i % 2]
                key = ("adaw", i % 2)
                self.wload(sl[:], self.d["w_ada"][l, i].rearrange("p (k n) -> p k n", k=8), key)
                if i in (2, 5):
                    gi = 0 if i == 2 else 1
                    for half in range(2):
                        b = self.nb()
                        self.mm(self.ps[b], [(self.cbc[:, kc, :], sl[:, kc, half * 512:(half + 1) * 512]) for kc in range(8)], b, reads=[key, "cbc"])
                        self.dve(lambda b=b, gi=gi, half=half: nc.vector.tensor_add(out=self.gbc[:, gi, half * 512:(half + 1) * 512], in0=self.ps[b], in1=abc[:, gi, half * 512:(half + 1) * 512]),
                                 [("abc", gi)], [("ps", b), ("gbc", gi)])
                else:
                    b = self.nb()
                    fns = []
                    for j in range(8):
                        for kc in range(8):
                            fns.append(lambda j=j, kc=kc, b=b, sl=sl: nc.tensor.matmul(self.ps[b][:, j:j + 1], lhsT=sl[:, kc, j * 128:(j + 1) * 128], rhs=self.cactb[:, kc:kc + 1], start=(kc == 0), stop=(kc == 7)))
                    kb.group("pe", fns, reads=[key, "cact"], writes=[("ps", b)])
                    self.dve(lambda b=b, i=i: nc.vector.tensor_add(out=self.modp[:, i, :], in0=self.ps[b][:, 0:8], in1=self.vpc("ada_b", i * 8, 8)),
                             ["vp"], [("ps", b), ("modp", i)])
            self.dve(lambda: nc.vector.scalar_tensor_tensor(out=self.s1[:], in0=self.modp[:, 1, :], scalar=1.0, in1=self.vpc("norm1_w", 0, 8), op0=ALU.add, op1=ALU.mult), [("modp", 1), "vp"], ["s1"])
            self.dve(lambda: nc.vector.scalar_tensor_tensor(out=self.s2[:], in0=self.modp[:, 4, :], scalar=1.0, in1=self.vpc("norm2_w", 0, 8), op0=ALU.add, op1=ALU.mult), [("modp", 4), "vp"], ["s2"])
            self.b1 = self.modp[:, 0, :]
            self.b2 = self.modp[:, 3, :]
            self.dump("modp%d" % l, self.modp[:], [128, 6, 8], [("modp", i) for i in (0, 1, 3, 4)])
            self.dump("gbc%d" % l, self.gbc[:], [128, 2, D], [("gbc", 0), ("gbc", 1)])
            kb.barrier()

    def load_x(self, ph, xsrc, t):
        if not hasattr(self, "_xbufs") or self._xbufs_ph is not ph:
            self._xbufs = [self.sbt(ph, "xt", [128, D], F32) for _ in range(2)]
            self._xbufs_ph = ph
            self._xi = 0
        i = self._xi
        self._xi = (i + 1) % 2
        xt = self._xbufs[i]
        self.kb.dma("sp", xt[:], xsrc[t * 128:(t + 1) * 128, :], writes=[("xt", i)])
        return (xt, [("xt", i)])

    def norm_a(self, ph, xtk, t):
        nc = self.nc
        xt, xkeys = xtk
        if not hasattr(self, "_nb") or self._nb_ph is not ph:
            self._nb = dict(junk=self.sbt(ph, "njunk", [128, D], BF16),
                            ss=[self.sbt(ph, "nss", [128, 1], F32) for _ in range(2)],
                            xn=[self.sbt(ph, "nxn", [128, D], BF16) for _ in range(2)],
                            tmp=self.sbt(ph, "ntmp", [128, 4, 128], F32))
            self._nb_ph = ph
        i = t % 2
        junk = self._nb["junk"]
        ss = self._nb["ss"][i]
        xn = self._nb["xn"][i]
        self.act(junk[:], xt[:], AF.Square, reads=list(xkeys), writes=["njunk", ("nss", i)], accum=ss[:])
        self.rstd(ss[:], ss[:], 1.0 / D, EPS, [], ("nss", i))
        self.dve(lambda: nc.vector.tensor_scalar(out=xn[:], in0=xt[:], scalar1=ss[:, 0:1], scalar2=None, op0=ALU.mult), list(xkeys) + [("nss", i)], [("nxn", i)])

    def norm_b(self, s_ap, b_ap, t, skey=("s1",), bkey=(("modp", 0),), mixed=False):
        nc = self.nc
        i = t % 2
        xn = self._nb["xn"][i]
        self.kb.group("pe", [lambda kc=kc: nc.tensor.transpose(self.psb[:, kc * 128:(kc + 1) * 128], xn[:, kc * 128:(kc + 1) * 128], self.k16("ident")) for kc in range(8)],
                      reads=[("nxn", i), "c16"], writes=[("ps", 7)])
        nact = 4 if mixed else 8
        for kc in range(nact):
            self.act(self.hT[:, kc, t * 128:(t + 1) * 128], self.psb[:, kc * 128:(kc + 1) * 128], AF.Identity,
                     reads=list(skey) + list(bkey), writes=[("ps", 7), ("hT", t // 4)], scale=s_ap[:, kc:kc + 1], bias=b_ap[:, kc:kc + 1])
        if mixed:
            tmp = self._nb["tmp"]
            pin = self.psb[:, 512:1024].rearrange("p (c n) -> p c n", c=4)
            self.dve(lambda: nc.vector.tensor_tensor(out=tmp[:], in0=pin, in1=s_ap[:, 4:8].unsqueeze(2).to_broadcast([128, 4, 128]), op=ALU.mult),
                     list(skey), [("ps", 7), "ntmp"])
            self.dve(lambda: nc.vector.tensor_tensor(out=self.hT[:, 4:8, t * 128:(t + 1) * 128], in0=tmp[:], in1=b_ap[:, 4:8].unsqueeze(2).to_broadcast([128, 4, 128]), op=ALU.add),
                     list(bkey) + ["ntmp"], [("hT", t // 4)])

    def attention(self, ph, l):
        nc = self.nc
        kb = self.kb
        kb.mark("attention")
        wa = self.sbt(ph, "wattn", [128, 8, 832], BF16)
        wasrc = self.d["w_attn"][l].rearrange("p (k n) -> p k n", k=8)
        self.wload(wa[:, :, 384:640], wasrc[:, :, 384:640], ("wattn", "kv"))
        self.wload(wa[:, :, 640:832], wasrc[:, :, 640:832], ("wattn", "kr"))
        wqb = self.sbt(ph, "wqb", [128, 3, NH, 192], BF16)
        self._late_attn_loads = lambda: (
            self.kb.dma("pool", wa[:, :, 0:384], wasrc[:, :, 0:384], reads=[("ckvn", 0)], writes=[("wattn", "q")]),
            self.kb.dma("pool", wqb[:], self.d["w_qb"][l].rearrange("p (k h n) -> p k h n", k=3, h=NH), reads=[("ckvn", 0)], writes=["wqb"]))
        kT = self.sbt(ph, "kT", [128, NH, S], BF16)
        vext = self.sbt(ph, "vext", [128, NT, NH, 65], BF16)
        self.Ctab = self.sbt(ph, "Ctab", [128, S], F32)
        self.Stab = self.sbt(ph, "Stab", [128, S], F32)
        kb.dma("sp", self.Ctab[:], self.rope_c, writes=["tabc"])
        kb.dma("sp", self.Stab[:], self.rope_s, writes=["tabs"])
        self.dve(lambda: nc.vector.memset(vext[:], 1.0), [], ["vext"])
        ones16 = self.k16("ones")
        bd = self.k16("bd")

        def mkscr(stk, n):
            scr = [self.sbt(stk, "ascr", [128, 512], F32) for _ in range(n)]
            state = {"i": 0}

            def nscr():
                i = state["i"]
                state["i"] = (i + 1) % n
                return scr[i], ("ascr", i)
            return nscr

        with ExitStack() as sa:
            wkn = self.sbt(sa, "wkn", [128, 2, 512], BF16)
            wv = self.sbt(sa, "wv", [128, 2, 512], BF16)
            self.wload(wkn[:], self.d["w_kn"][l].rearrange("p (k n) -> p k n", k=2), "wkn")
            self.wload(wv[:], self.d["w_v"][l].rearrange("p (k n) -> p k n", k=2), "wv")
            ckvn = self.sbt(sa, "ckvn", [128, 2, S], BF16)
            ksq = [self.sbt(sa, "ksq", [64, 512], BF16) for _ in range(3)]
            krs = [self.sbt(sa, "krs", [64, 512], F32) for _ in range(3)]
            sqb = [self.sbt(sa, "asq", [128, 3, 512], BF16) for _ in range(2)]
            nscr = mkscr(sa, 4)
            for b in range(NB):
                bs = slice(b * 512, (b + 1) * 512)
                hk = ("hT", b)
                sq, sqk = sqb[b % 2], ("asq", b % 2)
                banks = []
                for c in range(2):
                    bk = self.nb()
                    banks.append(bk)
                    self.mm(self.ps[bk], [(wa[:, kc, 384 + c * 128:384 + (c + 1) * 128], self.hT[:, kc, bs]) for kc in range(8)], bk, reads=[("wattn", "kv"), hk])
                    self.act(sq[:, c, :], self.ps[bk], AF.Square, reads=[], writes=[("ps", bk), (sqk, c)])
                bss = self.nb()
                self.mm(self.ps[bss], [(ones16, sq[:, c, :]) for c in range(2)], bss, reads=["c16", (sqk, 0), (sqk, 1)])
                rs, rsk = nscr()
                self.rstd(rs[:], self.ps[bss], 1.0 / 256, EPS, [("ps", bss)], rsk)
                for c in range(2):
                    self.dve(lambda c=c, rs=rs, bk=banks[c]: nc.vector.scalar_tensor_tensor(out=ckvn[:, c, bs], in0=self.ps[bk], scalar=self.vpc("kv_a_norm", c), in1=rs[:], op0=ALU.mult, op1=ALU.mult),
                             [rsk, "vp"], [("ps", banks[c]), ("ckvn", b)])
                if b == 0:
                    self._late_attn_loads()
                bA = self.nb()
                self.mm(self.ps[bA][0:96, :], [(wa[:, kc, 640:736], self.hT[:, kc, bs]) for kc in range(8)], bA, reads=[("wattn", "kr"), hk])
                bB = self.nb()
                self.mm(self.ps[bB][0:96, :], [(wa[:, kc, 736:832], self.hT[:, kc, bs]) for kc in range(8)], bB, reads=[("wattn", "kr"), hk])
                self.act(sq[0:96, 2, :], self.ps[bA][0:96, :], AF.Square, reads=[], writes=[("ps", bA), (sqk, 2)])
                bm = self.nb()
                self.mm(self.ps[bm][0:96, :], [(bd[0:96, 0:96], sq[0:96, 2, :])], bm, reads=["c16", (sqk, 2)])
                rs, rsk = nscr()
                self.rstd(rs[0:96, :], self.ps[bm][0:96, :], 1.0, EPS, [("ps", bm)], rsk)
                t1, t1k = nscr()
                t2, t2k = nscr()
                self.dve(lambda t1=t1, bA=bA: nc.vector.scalar_tensor_tensor(out=t1[64:96, :], in0=self.ps[bA][64:96, :], scalar=self.vpc("kwA")[64:96, :], in1=self.Ctab[64:96, bs], op0=ALU.mult, op1=ALU.mult),
                         ["vp", "tabc"], [("ps", bA), t1k])
                self.dve(lambda t2=t2, bB=bB: nc.vector.scalar_tensor_tensor(out=t2[64:96, :], in0=self.ps[bB][64:96, :], scalar=self.vpc("kwB")[64:96, :], in1=self.Stab[64:96, bs], op0=ALU.mult, op1=ALU.mult),
                         ["vp", "tabs"], [("ps", bB), t2k])
                self.dve(lambda t1=t1, t2=t2: nc.vector.tensor_add(out=t1[64:96, :], in0=t1[64:96, :], in1=t2[64:96, :]), [t2k], [t1k])
                self.dve(lambda t1=t1, rs=rs: nc.vector.tensor_mul(out=t1[64:96, :], in0=t1[64:96, :], in1=rs[64:96, :]), [rsk], [t1k])
                self.dve(lambda t1=t1: nc.vector.tensor_copy(out=kT[64:96, :, bs], in_=t1[64:96, :].unsqueeze(1).to_broadcast([32, NH, 512])), [t1k], [("kTr", b)])
                def kchain(h, b=b, bs=bs):
                    def gen(slot):
                        bk, bm = slot, 3 + slot
                        sqh, sqhk = ksq[slot], ("ksq", slot)
                        rs, rsk = krs[slot], ("krs", slot)
                        self.mm(self.ps[bk][0:64, :], [(wkn[:, c, h * 64:(h + 1) * 64], ckvn[:, c, bs]) for c in range(2)], bk, reads=["wkn", ("ckvn", b)])
                        yield
                        self.act(sqh[0:64, :], self.ps[bk][0:64, :], AF.Square, reads=[], writes=[("ps", bk), sqhk])
                        yield
                        self.mm(self.ps[bm][0:64, :], [(bd[0:64, 0:64], sqh[0:64, :])], bm, reads=["c16", sqhk])
                        yield
                        self.act(rs[0:64, :], self.ps[bm][0:64, :], AF.Ln, reads=[], writes=[("ps", bm), rsk], bias=EPS, scale=1.0)
                        yield
                        self.act(rs[0:64, :], rs[0:64, :], AF.Exp, reads=[], writes=[rsk], scale=-0.5)
                        yield
                        self.dve(lambda: nc.vector.scalar_tensor_tensor(out=kT[0:64, h, bs], in0=self.ps[bk][0:64, :], scalar=self.vpc("kwA")[0:64, :], in1=rs[0:64, :], op0=ALU.mult, op1=ALU.mult),
                                 [rsk, "vp"], [("ps", bk), ("kTn", b, h)])
                        yield
                    return gen
                self.run_chains([kchain(h) for h in range(NH)], 3)
                for tt in range(4):
                    t = b * 4 + tt
                    bk = self.nb()
                    self.mm(self.ps[bk], [(ckvn[:, c, t * 128:(t + 1) * 128], wv[:, c, :]) for c in range(2)], bk, reads=["wv", ("ckvn", b)])
                    self.act(vext[:, t, :, 0:64], self.ps[bk].rearrange("p (h d) -> p h d", h=NH), AF.Identity, reads=[], writes=[("ps", bk), "vext"])
            kb.barrier()
        self.dump("kT%d" % l, kT[:], [128, NH, S], [], BF16)
        self.dump("vext%d" % l, vext[:], [128, NT, NH, 65], [], BF16)

        with ExitStack() as sq_:
            sqb = [self.sbt(sq_, "asq", [128, 3, 512], BF16)] * 2
            nscr = mkscr(sq_, 2)
            ql = self.sbt(sq_, "qln", [128, 3, 512], BF16)
            qsq = [self.sbt(sq_, "qsq", [128, 512], BF16) for _ in range(2)]
            qsc = [self.sbt(sq_, "qsc", [128, 512], F32) for _ in range(6)]
            qT = [self.sbt(sq_, "qT", [128, NH, 512], BF16) for _ in range(2)]
            Eb = [self.sbt(sq_, "E", [128, 512], BF16) for _ in range(3)]
            ot = self.sbt(sq_, "otok", [128, 4, 512], BF16)
            rcp = [self.sbt(sq_, "rcp", [128, 4], F32) for _ in range(2)]
            mask01 = self.k16("mask01")
            ei = 0
            for qb_ in range(NB):
                bs = slice(qb_ * 512, (qb_ + 1) * 512)
                hk = ("hT", qb_)
                qlk = "qln"
                qt_, qtk = qT[qb_ % 2], ("qT", qb_ % 2)
                sq, sqk = sqb[0], ("asq", 0)
                banks = []
                for c in range(3):
                    bk = self.nb()
                    banks.append(bk)
                    self.mm(self.ps[bk], [(wa[:, kc, c * 128:(c + 1) * 128], self.hT[:, kc, bs]) for kc in range(8)], bk, reads=[("wattn", "q"), hk])
                    self.act(sq[:, c, :], self.ps[bk], AF.Square, reads=[], writes=[("ps", bk), (sqk, c)])
                bss = self.nb()
                self.mm(self.ps[bss], [(ones16, sq[:, c, :]) for c in range(3)], bss, reads=["c16"] + [(sqk, c) for c in range(3)])
                rs, rsk = nscr()
                self.rstd(rs[:], self.ps[bss], 1.0 / 384, EPS, [("ps", bss)], rsk)
                for c in range(3):
                    self.dve(lambda c=c, rs=rs, bk=banks[c]: nc.vector.scalar_tensor_tensor(out=ql[:, c, :], in0=self.ps[bk], scalar=self.vpc("q_a_norm", c), in1=rs[:], op0=ALU.mult, op1=ALU.mult),
                             [rsk, "vp"], [("ps", banks[c]), (qlk, c)])
                def qchain(h, qb_=qb_, bs=bs, qt_=qt_, qtk=qtk):
                    def gen(slot):
                        bA, bB, bm = 3 * slot, 3 * slot + 1, 3 * slot + 2
                        sqh, sqhk = qsq[slot], ("qsq", slot)
                        rs, rsk = qsc[3 * slot], ("qsc", 3 * slot)
                        t1, t1k = qsc[3 * slot + 1], ("qsc", 3 * slot + 1)
                        t2, t2k = qsc[3 * slot + 2], ("qsc", 3 * slot + 2)
                        self.mm(self.ps[bA][0:96, :], [(wqb[:, c, h, 0:96], ql[:, c, :]) for c in range(3)], bA, reads=["wqb"] + [(qlk, c) for c in range(3)])
                        self.mm(self.ps[bB][0:96, :], [(wqb[:, c, h, 96:192], ql[:, c, :]) for c in range(3)], bB, reads=["wqb"] + [(qlk, c) for c in range(3)])
                        yield
                        self.act(sqh[0:96, :], self.ps[bA][0:96, :], AF.Square, reads=[], writes=[("ps", bA), sqhk])
                        yield
                        self.mm(self.ps[bm][0:96, :], [(bd[0:96, 0:96], sqh[0:96, :])], bm, reads=["c16", sqhk])
                        self.dve(lambda: nc.vector.scalar_tensor_tensor(out=t1[0:96, :], in0=self.ps[bA][0:96, :], scalar=self.vpc("qwA")[0:96, :], in1=self.Ctab[0:96, bs], op0=ALU.mult, op1=ALU.mult),
                                 ["vp", "tabc"], [("ps", bA), t1k])
                        yield
                        self.act(rs[0:96, :], self.ps[bm][0:96, :], AF.Ln, reads=[], writes=[("ps", bm), rsk], bias=EPS / (SM_SCALE ** 2), scale=1.0 / (SM_SCALE ** 2))
                        self.dve(lambda: nc.vector.scalar_tensor_tensor(out=t2[0:96, :], in0=self.ps[bB][0:96, :], scalar=self.vpc("qwB")[0:96, :], in1=self.Stab[0:96, bs], op0=ALU.mult, op1=ALU.mult),
                                 ["vp", "tabs"], [("ps", bB), t2k])
                        yield
                        self.act(rs[0:96, :], rs[0:96, :], AF.Exp, reads=[], writes=[rsk], scale=-0.5)
                        self.dve(lambda: nc.vector.tensor_add(out=t1[0:96, :], in0=t1[0:96, :], in1=t2[0:96, :]), [t2k], [t1k])
                        yield
                        self.dve(lambda: nc.vector.tensor_mul(out=qt_[0:96, h, :], in0=t1[0:96, :], in1=rs[0:96, :]), [rsk, t1k], [(qtk, h)])
                        yield
                    return gen
                self.run_chains([qchain(h) for h in range(NH)], 2)
                if qb_ == 0:
                    self.dump("qT%d" % l, qt_[:], [128, NH, 512], [(qtk, h) for h in range(NH)], BF16)
                otk = "otok"
                LA = 2
                for h in range(NH):
                    bo = 5 + (h % 2)
                    nkt = 4 * qb_ + 4
                    st = {"first": True}
                    pend = {}

                    def score(kt, h=h, qb_=qb_):
                        j0 = max(0, kt - 4 * qb_)
                        cs = slice(j0 * 128, 512)
                        bsT = self.nb5()
                        self.mm(self.ps[bsT][:, cs], [(kT[0:96, h, kt * 128:(kt + 1) * 128], qt_[0:96, h, cs])], bsT, reads=[(qtk, h)])
                        pend[kt] = (bsT, j0, cs)

                    def finish(kt, h=h, qb_=qb_, bo=bo, st=st):
                        nonlocal ei
                        bsT, j0, cs = pend.pop(kt)
                        E, Ek = Eb[ei % 3], ("E", ei % 3)
                        ei += 1
                        self.act(E[:, cs], self.ps[bsT][:, cs], AF.Exp, reads=[], writes=[("ps", bsT), Ek])
                        if kt >= 4 * qb_:
                            self.dve(lambda E=E, j0=j0: nc.vector.tensor_mul(out=E[:, j0 * 128:(j0 + 1) * 128], in0=E[:, j0 * 128:(j0 + 1) * 128], in1=mask01), ["c16"], [Ek])
                        fns = []
                        for j in range(j0, 4):
                            qtile = 4 * qb_ + j
                            st_flag = st["first"]
                            st["first"] = False
                            fns.append(lambda j=j, E=E, kt=kt, st_flag=st_flag, qtile=qtile: nc.tensor.matmul(
                                self.ps[bo][:, j * 65:(j + 1) * 65], lhsT=E[:, j * 128:(j + 1) * 128], rhs=vext[:, kt, h, :],
                                start=st_flag, stop=(kt == qtile), skip_group_check=True))
                        kb.group("pe", fns, reads=[Ek], writes=[("ps", bo)])
                    for kt in range(nkt):
                        score(kt)
                        if kt >= LA:
                            finish(kt - LA)
                    for kt in range(max(0, nkt - LA), nkt):
                        finish(kt)
                    rc, rck = rcp[h % 2], ("rcp", h % 2)
                    pview = self.ps[bo][:, 0:260].rearrange("p (j e) -> p j e", j=4)
                    self.dve(lambda rc=rc, pview=pview: nc.vector.reciprocal(out=rc[:].unsqueeze(2), in_=pview[:, :, 64:65]), [], [("ps", bo), rck])
                    self.dve(lambda rc=rc, pview=pview, h=h: nc.vector.tensor_tensor(out=ot[:, :, h * 64:(h + 1) * 64], in0=pview[:, :, 0:64], in1=rc[:].unsqueeze(2).to_broadcast([128, 4, 64]), op=ALU.mult),
                             [rck], [("ps", bo), (otk, h)])
                for j in range(4):
                    t = 4 * qb_ + j
                    kb.group("pe", [lambda c=c, j=j: nc.tensor.transpose(self.psb[:, c * 128:(c + 1) * 128], ot[:, j, c * 128:(c + 1) * 128], self.k16("ident")) for c in range(4)],
                             reads=[(otk, h) for h in range(NH)] + ["c16"], writes=[("ps", 7)])
                    self.act(self.o_a[:, :, t * 128:(t + 1) * 128], self.psb[:, 0:512].rearrange("p (c n) -> p c n", c=4), AF.Identity, reads=[], writes=[("ps", 7), "o_a"])
            kb.barrier()

    def pool_branch(self, ph, l):
        nc = self.nc
        kb = self.kb
        kb.mark("pool_branch")
        wu = self.sbt(ph, "wu", [128, 8, 512], BF16)
        wp = self.sbt(ph, "wpool", [128, 4, 128], BF16)
        self.wload(wu[:], self.d["w_u"][l].rearrange("p (k n) -> p k n", k=8), "wu")
        self.wload(wp[:], self.d["w_pool"][l].rearrange("p (g n) -> p g n", g=4), "wpool")
        self.wload(self.wg_pre[:], self.d["w_gate"][l, 0].rearrange("p (k x n) -> p k x n", k=8, x=3), ("wgate", 0))
        self.wload(self.wb_pre[:], self.d["w_br"][l, 0].rearrange("p (k n) -> p k n", k=16), ("wbr", 0))
        utok = self.sbt(ph, "utok", [128, NT, 512], BF16)
        pooled = [self.sbt(ph, "pooled", [128, 512], BF16) for _ in range(3)]
        for t in range(NT):
            bk = self.nb()
            self.mm(self.ps[bk], [(self.hT[:, kc, t * 128:(t + 1) * 128], wu[:, kc, :]) for kc in range(8)], bk, reads=["wu", ("hT", t // 4)])
            self.act(utok[:, t, :], self.ps[bk], AF.Identity, reads=[], writes=[("ps", bk), ("utok", t)])
        def pchain(b, g):
            def gen(slot):
                bk, b2 = 2 * slot, 2 * slot + 1
                pl, plk = pooled[slot], ("pooled", slot)
                fns = []
                for tt in range(4):
                    t = 4 * b + tt
                    cur = self.k16("bcur0" if t == 0 else "bcur", g)
                    o_ = self.ps[bk][:, tt * 128:(tt + 1) * 128]
                    if t == 0:
                        fns.append(lambda o_=o_, cur=cur, t=t: nc.tensor.matmul(o_, lhsT=utok[:, t, g * 128:(g + 1) * 128], rhs=cur, start=True, stop=True))
                    else:
                        fns.append(lambda o_=o_, cur=cur, t=t: nc.tensor.matmul(o_, lhsT=utok[:, t, g * 128:(g + 1) * 128], rhs=cur, start=True, stop=False))
                        fns.append(lambda o_=o_, t=t: nc.tensor.matmul(o_, lhsT=utok[:, t - 1, g * 128:(g + 1) * 128], rhs=self.k16("bprev", g), start=False, stop=True))
                kb.group("pe", fns, reads=["c16"] + [("utok", t) for t in range(max(0, 4 * b - 1), 4 * b + 4)], writes=[("ps", bk)])
                yield
                self.dve(lambda: nc.vector.tensor_copy(out=pl[:], in_=self.ps[bk]), [], [("ps", bk), plk])
                yield
                self.mm(self.ps[b2], [(wp[:, g, :], pl[:])], b2, reads=["wpool", plk])
                yield
                self.act(self.o_b[:, g, b * 512:(b + 1) * 512], self.ps[b2], AF.Identity, reads=["vp"], writes=[("ps", b2), "o_b"], scale=self.vpc("pool_scale", g))
                yield
            return gen
        self.run_chains([pchain(b, g) for b in range(NB) for g in range(4)], 3)

    def ssd(self, ph, l, g):
        nc = self.nc
        kb = self.kb
        kb.mark("ssd")
        ws = self.sbt(ph, "wssd", [128, 8, 1288], BF16)
        wssrc = self.d["w_ssd"][l, g].rearrange("p (k n) -> p k n", k=8)
        for j in range(6):
            self.wload(ws[:, :, j * 128:(j + 1) * 128], wssrc[:, :, j * 128:(j + 1) * 128], ("wssd", j))
        self.wload(ws[:, :, 768:1280], wssrc[:, :, 768:1280], ("wssd", "z"))
        self.wload(ws[:, :, 1280:1288], wssrc[:, :, 1280:1288], ("wssd", "dt"))
        sm = self.sbt(ph, "ssdsm", [128, 48], F32)
        kb.dma("sp", sm[:], self.d["ssd_small"][l:l + 1, :].partition_broadcast(128), writes=["ssdsm"])
        abc_ = self.sbt(ph, "ssda", [128, 8], F32)
        self.act(abc_[:], sm[:, 32 + 8 * g:32 + 8 * g + 8], AF.Exp, reads=["ssdsm"], writes=["ssda"])
        self.dve(lambda: nc.vector.tensor_scalar(out=abc_[:], in0=abc_[:], scalar1=-1.0, scalar2=None, op0=ALU.mult), [], ["ssda"])
        Dg = sm[:, 8 * g:8 * g + 8]
        dtb = sm[:, 16 + 8 * g:16 + 8 * g + 8]
        xbc = self.sbt(ph, "xbc", [128, 6, S], BF16)
        zs_all = self.sbt(ph, "zsall", [128, NT, 512], BF16)
        with ExitStack() as s1_:
            raw = self.sbt(s1_, "sraw", [128, 6, 3 + S], BF16)
            cacc = [self.sbt(s1_, "scacc", [128, 512], F32) for _ in range(2)]
            self.dve(lambda: nc.vector.memset(raw[:, :, 0:3], 0.0), [], ["rawpad"])
            dg5 = self.sbt(s1_, "sdiag5", [128, 4, 128], BF16)
            for k in range(4):
                self.dve(lambda k=k: nc.vector.tensor_scalar(out=dg5[:, k, :], in0=self.k16("ident"), scalar1=self.vpc("ssd_conv_w", k * 12 + 10 + g), scalar2=None, op0=ALU.mult),
                         ["c16", "vp"], ["sdiag5"])
            for b in range(NB):
                bs = slice(b * 512, (b + 1) * 512)
                for j in range(6):
                    bk = self.nb()
                    self.mm(self.ps[bk], [(ws[:, kc, j * 128:(j + 1) * 128], self.hT[:, kc, bs]) for kc in range(8)], bk, reads=[("wssd", j), ("hT", b)])
                    self.act(raw[:, j, 3 + b * 512:3 + (b + 1) * 512], self.ps[bk], AF.Identity, reads=[], writes=[("ps", bk), ("sraw", j, b)])
                for j in range(6):
                    ch = (4 * g + j) if j < 4 else (8 + g if j == 4 else 10 + g)
                    rd = [("sraw", j, b), "rawpad", "vp"] + ([("sraw", j, b - 1)] if b > 0 else [])
                    if j == 5:
                        bk = self.nb()
                        self.mm(self.ps[bk], [(dg5[:, k, :], raw[:, j, b * 512 + k:b * 512 + k + 512]) for k in range(4)], bk, reads=rd + ["sdiag5"])
                        self.act(xbc[:, j, bs], self.ps[bk], AF.Silu, reads=["vp"], writes=[("ps", bk), ("xbc", j, b)], bias=self.vpc("ssd_conv_b", ch))
                        continue
                    ca, ck = cacc[(b * 6 + j) % 2], ("scacc", (b * 6 + j) % 2)
                    rv = lambda k, j=j, b=b: raw[:, j, b * 512 + k:b * 512 + k + 512]
                    self.dve(lambda ca=ca, rv=rv, ch=ch: nc.vector.tensor_scalar(out=ca[:], in0=rv(0), scalar1=self.vpc("ssd_conv_w", 0 * 12 + ch), scalar2=self.vpc("ssd_conv_b", ch), op0=ALU.mult, op1=ALU.add), rd, [ck])
                    for k in range(1, 4):
                        self.dve(lambda ca=ca, rv=rv, ch=ch, k=k: nc.vector.scalar_tensor_tensor(out=ca[:], in0=rv(k), scalar=self.vpc("ssd_conv_w", k * 12 + ch), in1=ca[:], op0=ALU.mult, op1=ALU.add), rd, [ck])
                    self.act(xbc[:, j, bs], ca[:], AF.Silu, reads=[ck], writes=[("xbc", j, b)])
                for tt in range(4):
                    t = 4 * b + tt
                    bk = self.nb()
                    self.mm(self.ps[bk], [(self.hT[:, kc, t * 128:(t + 1) * 128], ws[:, kc, 768:1280]) for kc in range(8)], bk, reads=[("wssd", "z"), ("hT", b)])
                    self.act(zs_all[:, t, :], self.ps[bk], AF.Silu, reads=[], writes=[("ps", bk), ("zsall", t)])

            kb.barrier()
        self.dump("xbc%d_%d" % (l, g), xbc[:], [128, 6, S], [("xbc", j, b) for j in range(6) for b in range(NB)], BF16)
        U32 = self.k32("U")
        ones32 = self.k32("ones")
        ident16 = self.k16("ident")
        maskneg4 = self.c16[:, C16["maskneg4"][0]:C16["maskneg4"][0] + 512]
        ones40 = self.k16("ones")[0:40, :]
        sel16 = self.c16[0:64, C16["sel16"][0]:C16["sel16"][0] + 1024].rearrange("p (r m) -> p r m", r=8)
        prev = self.sbt(ph, "sprev", [128, 512], F32)
        prevb = self.sbt(ph, "sprevb", [128, 512], BF16)
        self.dve(lambda: nc.vector.memset(prev[:], 0.0), [], ["sprev"])
        self.dve(lambda: nc.vector.memset(prevb[:], 0.0), [], ["sprevb"])
        Dm = self.sbt(ph, "sDm", [128, 8, 128], BF16)
        for r in range(8):
            self.dve(lambda r=r: nc.vector.tensor_scalar(out=Dm[:, r, :], in0=ident16, scalar1=Dg[:, r:r + 1], scalar2=None, op0=ALU.mult), ["ssdsm", "c16"], ["sDm"])
        dt_all = self.sbt(ph, "sdtall", [128, NT, 8], F32)
        da_all = self.sbt(ph, "sdaall", [128, NT, 8], F32)
        dae_all = self.sbt(ph, "sdaeall", [128, NT, 2, 32], F32)
        acs_all = self.sbt(ph, "sacsall", [128, NT, 8], F32)
        ea_all = self.sbt(ph, "seaall", [128, NT, 8], F32)
        cd_all = self.sbt(ph, "scdall", [128, NT, 8], F32)
        dsd_all = self.sbt(ph, "sdsdall", [128, NT, 8], F32)
        hl_all = self.sbt(ph, "shlall", [64, NT, 128], BF16)
        nhl_all = self.sbt(ph, "snhlall", [64, NT, 128], BF16)
        hc_q = self.sbt(ph, "shcq", [64, 4, 128], BF16)
        self.dve(lambda: nc.vector.memset(dae_all[:], 0.0), [], ["sdae"])
        self.dve(lambda: nc.vector.memset(hl_all[:], 0.0), [], ["shl"])
        bdt = self.nb()
        fns = []
        for t in range(NT):
            for kc in range(8):
                fns.append(lambda t=t, kc=kc: nc.tensor.matmul(self.ps[bdt][:, t * 8:(t + 1) * 8], lhsT=self.hT[:, kc, t * 128:(t + 1) * 128], rhs=ws[:, kc, 1280:1288], start=(kc == 0), stop=(kc == 7)))
        kb.group("pe", fns, reads=[("wssd", "dt")] + [("hT", b) for b in range(NB)], writes=[("ps", bdt)])
        f2 = lambda a: a.rearrange("p t r -> p (t r)")
        self.dve(lambda: nc.vector.tensor_tensor(out=dt_all[:], in0=self.ps[bdt][:, 0:128].rearrange("p (t r) -> p t r", r=8), in1=dtb.unsqueeze(1).to_broadcast([128, NT, 8]), op=ALU.add), ["ssdsm"], [("ps", bdt), "sdt"])
        self.act(f2(dt_all[:]), f2(dt_all[:]), AF.Exp, reads=[], writes=["sdt"])
        self.act(f2(dt_all[:]), f2(dt_all[:]), AF.Ln, reads=[], writes=["sdt"], bias=1.0)
        self.dve(lambda: nc.vector.tensor_tensor(out=da_all[:], in0=dt_all[:], in1=abc_[:].unsqueeze(1).to_broadcast([128, NT, 8]), op=ALU.mult), ["sdt", "ssda"], ["sda"])
        for hh in range(2):
            self.dve(lambda hh=hh: nc.vector.tensor_copy(out=dae_all[:, :, hh, 0:8], in_=da_all[:]), ["sda"], ["sdae"])
        bcs = self.nb()
        kb.group("pe", [lambda: nc.tensor.matmul(self.ps[bcs][:, 0:128], lhsT=U32, rhs=f2(da_all[:]), start=True, stop=True),
                        lambda: nc.tensor.matmul(self.ps[bcs][:, 128:256], lhsT=ones32, rhs=f2(da_all[:]), start=True, stop=True)],
                 reads=["sda", "c32"], writes=[("ps", bcs)])
        self.act(f2(acs_all[:]), self.ps[bcs][:, 0:128], AF.Identity, reads=[], writes=[("ps", bcs), "sacs"])
        self.act(f2(ea_all[:]), self.ps[bcs][:, 0:128], AF.Exp, reads=[], writes=[("ps", bcs), "sea"])
        self.act(f2(cd_all[:]), self.ps[bcs][:, 128:256], AF.Exp, reads=[], writes=[("ps", bcs), "scd"])
        self.dve(lambda: nc.vector.tensor_sub(out=f2(dsd_all[:]), in0=self.ps[bcs][:, 128:256], in1=f2(acs_all[:])), ["sacs"], [("ps", bcs), "sdsd"])
        self.act(f2(dsd_all[:]), f2(dsd_all[:]), AF.Exp, reads=[], writes=["sdsd"])
        for q4 in range(4):
            bq = self.nb()
            fns = []
            for tt in range(4):
                t = 4 * q4 + tt
                lhs = dae_all[:, t].rearrange("p a b -> p (a b)")[:, 0:40]
                fns.append(lambda tt=tt, lhs=lhs, bq=bq: nc.tensor.matmul(self.ps[bq][0:40, tt * 128:(tt + 1) * 128], lhsT=lhs, rhs=U32, start=True, stop=True))
            kb.group("pe", fns, reads=["sdae", "c32"], writes=[("ps", bq)])
            tsl = slice(4 * q4, 4 * q4 + 4)
            pv = lambda lo, hi, bq=bq: self.ps[bq][lo:hi, :].rearrange("p (t m) -> p t m", t=4)
            self.act(hl_all[0:8, tsl, :], pv(0, 8), AF.Identity, reads=[], writes=[("ps", bq), "shl"])
            self.act(hc_q[32:40, :, :], pv(32, 40), AF.Identity, reads=[], writes=[("ps", bq), "shc"])
            self.dve(lambda tsl=tsl, pv=pv: nc.vector.tensor_sub(out=hl_all[32:40, tsl, :], in0=pv(32, 40), in1=hc_q[32:40, :, :]), ["shc"], [("ps", bq), "shl"])
        self.dve(lambda: nc.vector.tensor_scalar(out=nhl_all[0:40], in0=hl_all[0:40], scalar1=-1.0, scalar2=None, op0=ALU.mult), ["shl"], ["snhl"])

        R = 2

        def rot(name, shape, dt):
            return [self.sbt(ph, name, shape, dt) for _ in range(R)]
        one = lambda name, shape, dt: [self.sbt(ph, name, shape, dt)] * R
        bsel = one("sbsel", [64, 8, 128], BF16)
        xsb = one("sxsb", [128, 512], BF16)
        Eexp = one("sE", [128, 8, 128], BF16)
        Mt = Eexp
        cbT = one("scbT", [128, 128], BF16)
        xdt = one("sxdt", [128, 512], BF16)
        xdt2 = rot("sxdt2", [128, 512], BF16)
        Btok = rot("sBtok", [128, 128], BF16)
        yb = one("sy", [128, 512], F32)
        gt = one("sgt", [128, 512], BF16)
        junk = self.sbt(ph, "sjunk", [128, 512], BF16)
        ssq = one("sssq", [128, 1], F32)
        x3 = lambda a: a.rearrange("p (r d) -> p r d", r=8)
        bc8 = lambda a: a.unsqueeze(2).to_broadcast([128, 8, 64])
        nwbc = self.sbt(ph, "snwbc", [128, 512], F32)
        kb.dma("sp", nwbc[:], self.d["ssd_nw"][l:l + 1, 512 * g:512 * (g + 1)].partition_broadcast(128), writes=["snwbc"])
        psbA = self.ps[6].bitcast(BF16)
        poolA = {"i": 0}
        poolB = {"i": 0}

        def nbA():
            poolA["i"] ^= 1
            return poolA["i"]

        def nbB():
            poolB["i"] ^= 1
            return 2 + poolB["i"]

        SINGLE = {"sbsel", "sxsb", "sE", "sM", "scbT", "sxdt", "sy", "sgt", "sssq"}

        def stageA(t):
            i = t % R
            ts_ = slice(t * 128, (t + 1) * 128)
            b = t // 4
            K = lambda n: ("sE", 0) if n == "sM" else ((n, 0) if n in SINGLE else (n, i))
            by = 4 + i
            kb.group("pe", [lambda j=j: nc.tensor.transpose(psbA[:, j * 128:(j + 1) * 128], xbc[:, j, ts_], ident16) for j in range(5)],
                     reads=[("xbc", j, b) for j in range(5)] + ["c16"], writes=[("ps", 6)])
            bcb = nbA()
            self.mm(self.ps[bcb][:, 0:128], [(xbc[:, 4, ts_], xbc[:, 5, ts_])], bcb, reads=[("xbc", 4, b), ("xbc", 5, b)])
            yield
            self.dve(lambda: nc.vector.tensor_tensor(out=bsel[i][0:40], in0=sel16[0:40], in1=hl_all[0:40, t, :].unsqueeze(1).to_broadcast([40, 8, 128]), op=ALU.mult), ["shl", "c16"], [K("sbsel")])
            yield
            self.act(cbT[i][:], self.ps[bcb][:, 0:128], AF.Identity, reads=[], writes=[("ps", bcb), K("scbT")])
            self.act(xsb[i][:], psbA[:, 0:512], AF.Identity, reads=[], writes=[("ps", 6), K("sxsb")])
            self.act(Btok[i][:], psbA[:, 512:640], AF.Identity, reads=[], writes=[("ps", 6), K("sBtok")])
            yield
            bp = [nbA(), nbA()]
            for half in range(2):
                hsl = slice(half * 4, (half + 1) * 4)
                fns = [lambda hsl=hsl, half=half: nc.tensor.matmul(self.ps[bp[half]], lhsT=ones40, rhs=bsel[i][0:40, hsl, :].rearrange("p r m -> p (r m)"), start=True, stop=False),
                       lambda hsl=hsl, half=half: nc.tensor.matmul(self.ps[bp[half]], lhsT=nhl_all[0:40, t, :], rhs=sel16[0:40, hsl, :].rearrange("p r m -> p (r m)"), start=False, stop=False),
                       lambda half=half: nc.tensor.matmul(self.ps[bp[half]], lhsT=ident16, rhs=maskneg4, start=False, stop=True)]
                kb.group("pe", fns, reads=[K("sbsel"), "snhl", "c16"], writes=[("ps", bp[half])])
                yield
            self.dve(lambda: nc.vector.tensor_tensor(out=x3(xdt[i][:]), in0=x3(xsb[i][:]), in1=bc8(dt_all[:, t, :]), op=ALU.mult), [K("sxsb"), "sdt"], [K("sxdt")])
            yield
            for half in range(2):
                hsl = slice(half * 4, (half + 1) * 4)
                ek = ("sE", half)
                self.act(Eexp[i][:, hsl, :], self.ps[bp[half]].rearrange("p (r m) -> p r m", r=4), AF.Exp, reads=[], writes=[("ps", bp[half]), ek])
                yield
                self.dve(lambda hsl=hsl: nc.vector.tensor_mul(out=Mt[i][:, hsl, :], in0=Eexp[i][:, hsl, :], in1=cbT[i][:].unsqueeze(1).to_broadcast([128, 4, 128])), [K("scbT")], [ek])
                yield
                fns = []
                for r in range(half * 4, half * 4 + 4):
                    fns.append(lambda r=r: nc.tensor.matmul(self.ps[by][:, r * 64:(r + 1) * 64], lhsT=Mt[i][:, r, :], rhs=xdt[i][:, r * 64:(r + 1) * 64], start=True, stop=False))
                    fns.append(lambda r=r: nc.tensor.matmul(self.ps[by][:, r * 64:(r + 1) * 64], lhsT=Dm[:, r, :], rhs=xsb[i][:, r * 64:(r + 1) * 64], start=False, stop=True))
                kb.group("pe", fns, reads=[ek, K("sxdt"), K("sxsb"), "sDm"], writes=[("ps", by)])
                yield
            self.dve(lambda: nc.vector.tensor_tensor(out=x3(xdt2[i][:]), in0=x3(xdt[i][:]), in1=bc8(dsd_all[:, t, :]), op=ALU.mult), [K("sxdt"), "sdsd"], [K("sxdt2")])
            yield

        def stageB(t):
            i = t % R
            ts_ = slice(t * 128, (t + 1) * 128)
            b = t // 4
            K = lambda n: ("sE", 0) if n == "sM" else ((n, 0) if n in SINGLE else (n, i))
            by = 4 + i
            bo = nbB()
            self.mm(self.ps[bo], [(xbc[:, 5, ts_], prevb[:])], bo, reads=[("xbc", 5, b), "sprevb"])
            bst = nbB()
            self.mm(self.ps[bst], [(Btok[i][:], xdt2[i][:])], bst, reads=[K("sBtok"), K("sxdt2")])
            yield
            self.dve(lambda: nc.vector.tensor_tensor(out=x3(prev[:]), in0=x3(prev[:]), in1=bc8(cd_all[:, t, :]), op=ALU.mult), ["scd"], ["sprev"])
            yield
            self.dve(lambda: nc.vector.tensor_add(out=prev[:], in0=prev[:], in1=self.ps[bst]), [], [("ps", bst), "sprev"])
            yield
            self.dve(lambda: nc.vector.tensor_copy(out=prevb[:], in_=prev[:]), ["sprev"], ["sprevb"])
            yield
            self.dve(lambda: nc.vector.tensor_tensor(out=x3(yb[i][:]), in0=x3(self.ps[bo]), in1=bc8(ea_all[:, t, :]), op=ALU.mult), ["sea"], [("ps", bo), K("sy")])
            yield
            self.dve(lambda: nc.vector.tensor_add(out=yb[i][:], in0=yb[i][:], in1=self.ps[by]), [], [("ps", by), K("sy")])
            yield
            self.dve(lambda: nc.vector.tensor_mul(out=yb[i][:], in0=yb[i][:], in1=zs_all[:, t, :]), [], [K("sy")])
            yield
            self.act(junk[:], yb[i][:], AF.Square, reads=[K("sy")], writes=["sjunk", K("sssq")], accum=ssq[i][:])
            self.rstd(ssq[i][:], ssq[i][:], 1.0 / 512, EPS, [], K("sssq"))
            yield
            self.dve(lambda: nc.vector.scalar_tensor_tensor(out=gt[i][:], in0=yb[i][:], scalar=ssq[i][:, 0:1], in1=nwbc[:], op0=ALU.mult, op1=ALU.mult), [K("sy"), K("sssq"), "snwbc"], [K("sgt")])
            yield
            kb.group("pe", [lambda j=j: nc.tensor.transpose(self.psb[:, j * 128:(j + 1) * 128], gt[i][:, j * 128:(j + 1) * 128], ident16) for j in range(4)],
                     reads=[K("sgt"), "c16"], writes=[("ps", 7)])
            yield
            self.act(self.o_c[:, 4 * g:4 * g + 4, ts_], self.psb[:, 0:512].rearrange("p (c n) -> p c n", c=4), AF.Identity, reads=[], writes=[("ps", 7), "o_c"])
            yield

        def interleave(ga, gb, ra=1, rb=1):
            alive = [ga, gb]
            while alive:
                for g_, n_ in ((ga, ra), (gb, rb)):
                    if g_ in alive:
                        for _ in range(n_):
                            try:
                                next(g_)
                            except StopIteration:
                                alive.remove(g_)
                                break

        for _ in stageA(0):
            pass
        for t in range(NT):
            if t + 1 < NT:
                interleave(stageA(t + 1), stageB(t))
            else:
                for _ in stageB(t):
                    pass

    def merge(self, ph, l):
        nc = self.nc
        kb = self.kb
        kb.mark("merge")
        wg = [self.wg_pre, self.sbt(ph, "wgate", [128, 8, 3, 128], BF16)]
        wb = [self.wb_pre, self.sbt(ph, "wbr", [128, 16, 128], BF16)]
        sig = [self.sbt(ph, "msig", [128, 512], F32) for _ in range(6)]
        acc = [self.sbt(ph, "macc", [128, 512], F32) for _ in range(2)]
        srcs = [(self.o_a, 4, 0, "o_a"), (self.o_b, 4, 4, "o_b"), (self.o_c, 8, 8, "o_c")]
        wo = self.wo
        wosrc = self.d["w_o"][l].rearrange("p (k n) -> p k n", k=8)
        si = 0
        ai = 0
        for m in range(8):
            g_, gk = wg[m % 2], ("wgate", m % 2)
            b_, bk_ = wb[m % 2], ("wbr", m % 2)
            if m > 0:
                self.wload(g_[:], self.d["w_gate"][l, m].rearrange("p (k x n) -> p k x n", k=8, x=3), gk)
                self.wload(b_[:], self.d["w_br"][l, m].rearrange("p (k n) -> p k n", k=16), bk_)
            if m == 1:
                for half in range(2):
                    self.wload(wo[:, :, half * 512:(half + 1) * 512], wosrc[:, :, half * 512:(half + 1) * 512], ("wo", half))
            if m == 6:
                for half in range(2):
                    for kc in range(8):
                        self.dve(lambda kc=kc, half=half: nc.vector.tensor_tensor(out=wo[:, kc, half * 512:(half + 1) * 512], in0=wo[:, kc, half * 512:(half + 1) * 512], in1=self.gbc[:, 0, half * 512:(half + 1) * 512], op=ALU.mult),
                                 [("gbc", 0)], [("wo", half)])
            for b in range(NB):
                bs = slice(b * 512, (b + 1) * 512)
                ac, ack = acc[ai % 2], ("macc", ai % 2)
                ai += 1
                for x, (src, nch, off, skey) in enumerate(srcs):
                    bg = self.nb()
                    self.mm(self.ps[bg], [(g_[:, kc, x, :], self.hT[:, kc, bs]) for kc in range(8)], bg, reads=[gk, ("hT", b)])
                    sg, sgk = sig[si % 6], ("msig", si % 6)
                    si += 1
                    self.act(sg[:], self.ps[bg], AF.Sigmoid, reads=[], writes=[("ps", bg), sgk])
                    by = self.nb()
                    self.mm(self.ps[by], [(b_[:, off + c, :], src[:, c, bs]) for c in range(nch)], by, reads=[bk_, skey])
                    if x == 0:
                        self.dve(lambda ac=ac, sg=sg, by=by: nc.vector.tensor_mul(out=ac[:], in0=sg[:], in1=self.ps[by]), [sgk], [("ps", by), ack])
                    else:
                        self.dve(lambda sg=sg, by=by: nc.vector.tensor_mul(out=sg[:], in0=sg[:], in1=self.ps[by]), [], [("ps", by), sgk])
                        if x == 1:
                            self.dve(lambda ac=ac, sg=sg: nc.vector.tensor_add(out=ac[:], in0=ac[:], in1=sg[:]), [sgk], [ack])
                        else:
                            self.dve(lambda ac=ac, sg=sg, m=m, bs=bs: nc.vector.tensor_add(out=self.mergedT[:, m, bs], in0=ac[:], in1=sg[:]), [sgk, ack], ["merged"])

    def wout_phase(self, ph, l, xin, xout1):
        nc = self.nc
        kb = self.kb
        kb.mark("wout_phase")
        wo = self.wo
        xn = [self.sbt(ph, "xnew", [128, D], F32) for _ in range(2)]

        def compute(t):
            ts_ = slice(t * 128, (t + 1) * 128)
            xt, xkeys = self.load_x(ph, xin, t)
            xo, xok = xn[t % 2], ("xnew", t % 2)
            for half in range(2):
                hs = slice(half * 512, (half + 1) * 512)
                bk = self.nb()
                self.mm(self.ps[bk], [(self.mergedT[:, kc, ts_], wo[:, kc, hs]) for kc in range(8)], bk, reads=[("wo", half), "merged"])
                self.dve(lambda xo=xo, xt=xt, bk=bk, hs=hs: nc.vector.tensor_add(out=xo[:, hs], in0=self.ps[bk], in1=xt[:, hs]), list(xkeys), [("ps", bk), (xok, half)])
            kb.dma("sp", xout1[ts_, :], xo[:], reads=[(xok, 0), (xok, 1)], writes=[("xo1", t)])
            if t == 0:
                self.dump("xo0_%d" % l, xo[:], [128, D], [(xok, 0), (xok, 1)])
                self.dump("wo_%d" % l, wo[:], [128, 8, D], [("wo", 0), ("wo", 1)], BF16)
            self.norm_a(ph, (xo, [(xok, 0), (xok, 1)]), t)
        compute(0)
        for t in range(NT):
            if t + 1 < NT:
                compute(t + 1)
            self.norm_b(self.s2, self.b2, t, skey=("s2",), bkey=(("modp", 3),), mixed=True)

    def ffn(self, ph, l, xmid, xout2):
        nc = self.nc
        kb = self.kb
        kb.mark("ffn")
        actT = self.sbt(ph, "actT", [128, NJ, S], BF16)
        wd0 = self.sbt(ph, "wdn", [128, NJ, 512], BF16)
        wdsrc = self.d["w_dn"][l].rearrange("p (k n) -> p k n", k=NJ)
        with ExitStack() as up:
            wup = [self.sbt(up, "wup", [128, 8, 256], BF16) for _ in range(2)]
            raw = [self.sbt(up, "fraw", [128, 2, 2 + S], BF16) for _ in range(2)]
            dg = [self.sbt(up, "fdiag", [128, 3, 2, 128], BF16) for _ in range(2)]
            sg = [self.sbt(up, "fsil", [128, 512], F32) for _ in range(3)]
            vacc = [self.sbt(up, "fvacc", [128, 512], F32) for _ in range(2)]
            si = 0
            for p_ in range(2):
                self.dve(lambda p_=p_: nc.vector.memset(raw[p_][:, :, 0:2], 0.0), [], [("frawpad", p_)])
            def prep(j):
                p_ = j % 2
                self.wload(wup[p_][:], self.d["w_up"][l, j].rearrange("p (k n) -> p k n", k=8), ("wup", p_))
                if j == 2:
                    self.wload(wd0[:], wdsrc[:, :, 0:512], ("wdn", 0))
                for k in range(3):
                    self.dve(lambda k=k, p_=p_: nc.vector.tensor_scalar(out=dg[p_][:, k, 0, :], in0=self.k16("ident"), scalar1=self.vpc("ffn_conv_w", k * 44 + j), scalar2=None, op0=ALU.mult),
                             ["c16", "vp"], [("fdiag", p_)])

            def proj(step):
                j, b = divmod(step, NB)
                p_ = j % 2
                w_, wk = wup[p_], ("wup", p_)
                bs = slice(b * 512, (b + 1) * 512)
                for x in range(2):
                    bk = 2 * (step % 2) + x
                    self.mm(self.ps[bk], [(w_[:, kc, x * 128:(x + 1) * 128], self.hT[:, kc, bs]) for kc in range(8)], bk, reads=[wk, ("hT", b)])
                    if x == 0:
                        self.act(raw[p_][:, x, 2 + b * 512:2 + (b + 1) * 512], self.ps[bk], AF.Identity, reads=[], writes=[("ps", bk), ("fraw", p_, x, b)])
                    else:
                        self.dve(lambda p_=p_, x=x, b=b, bk=bk: nc.vector.tensor_copy(out=raw[p_][:, x, 2 + b * 512:2 + (b + 1) * 512], in_=self.ps[bk]), [], [("ps", bk), ("fraw", p_, x, b)])

            def conv(step):
                nonlocal si
                j, b = divmod(step, NB)
                p_ = j % 2
                bs = slice(b * 512, (b + 1) * 512)
                bk = 4 + (step % 2)
                rd = [("fdiag", p_), ("fraw", p_, 0, b), ("frawpad", p_)] + ([("fraw", p_, 0, b - 1)] if b > 0 else [])
                self.mm(self.ps[bk], [(dg[p_][:, k, 0, :], raw[p_][:, 0, b * 512 + k:b * 512 + k + 512]) for k in range(3)], bk, reads=rd)
                s_, sk = sg[si % 3], ("fsil", si % 3)
                va, vk = vacc[si % 2], ("fvacc", si % 2)
                si += 1
                self.act(s_[:], self.ps[bk], AF.Silu, reads=["vp"], writes=[("ps", bk), sk], bias=self.vpc("ffn_conv_b", j))
                rdv = [("fraw", p_, 1, b), ("frawpad", p_), "vp"] + ([("fraw", p_, 1, b - 1)] if b > 0 else [])
                rv = lambda k: raw[p_][:, 1, b * 512 + k:b * 512 + k + 512]
                chv = NJ + j
                self.dve(lambda: nc.vector.tensor_scalar(out=va[:], in0=rv(0), scalar1=self.vpc("ffn_conv_w", 0 * 44 + chv), scalar2=self.vpc("ffn_conv_b", chv), op0=ALU.mult, op1=ALU.add), rdv, [vk])
                self.dve(lambda: nc.vector.scalar_tensor_tensor(out=va[:], in0=rv(1), scalar=self.vpc("ffn_conv_w", 1 * 44 + chv), in1=va[:], op0=ALU.mult, op1=ALU.add), rdv, [vk])
                self.dve(lambda: nc.vector.scalar_tensor_tensor(out=va[:], in0=rv(2), scalar=self.vpc("ffn_conv_w", 2 * 44 + chv), in1=va[:], op0=ALU.mult, op1=ALU.add), rdv, [vk])
                self.dve(lambda: nc.vector.tensor_mul(out=actT[:, j, bs], in0=va[:], in1=s_[:]), [sk, vk], [("actT", b)])
            nsteps = NJ * NB
            prep(0)
            proj(0)
            for step in range(nsteps):
                if step + 1 < nsteps:
                    if (step + 1) % NB == 0:
                        prep((step + 1) // NB)
                    proj(step + 1)
                conv(step)
            kb.barrier()
        self.dump("actT%d" % l, actT[:], [128, NJ, S], [("actT", b) for b in range(NB)], BF16)
        with ExitStack() as dn:
            wd = [wd0, self.sbt(dn, "wdn", [128, NJ, 512], BF16)]
            xh = [self.sbt(dn, "xh", [128, 512], F32) for _ in range(2)]
            xn = [self.sbt(dn, "xnew2", [128, 512], F32) for _ in range(2)]
            wsrc = self.d["w_dn"][l].rearrange("p (k n) -> p k n", k=NJ)
            self.wload(wd[1][:], wsrc[:, :, 512:1024], ("wdn", 1))
            ci = 0
            for half in range(2):
                hs = slice(half * 512, (half + 1) * 512)
                for t in range(NT):
                    ts_ = slice(t * 128, (t + 1) * 128)
                    i = ci % 2
                    ci += 1
                    kb.dma("sp", xh[i][:], xmid[ts_, hs], reads=[("xo1", t)] if False else [], writes=[("xh", i)])
                    bk = self.nb()
                    self.mm(self.ps[bk], [(actT[:, j, ts_], wd[half][:, j, :]) for j in range(NJ)], bk, reads=[("wdn", half), ("actT", t // 4)])
                    self.dve(lambda i=i, bk=bk, hs=hs: nc.vector.tensor_mul(out=xn[i][:], in0=self.ps[bk], in1=self.gbc[:, 1, hs]), [("gbc", 1)], [("ps", bk), ("xnew2", i)])
                    self.dve(lambda i=i: nc.vector.tensor_add(out=xn[i][:], in0=xn[i][:], in1=xh[i][:]), [("xh", i)], [("xnew2", i)])
                    kb.dma("sp", xout2[ts_, hs], xn[i][:], reads=[("xnew2", i)], writes=[("xo2", t, half)])
            kb.barrier()


_CACHE = {}


def make_in_maps(inputs, n_cores=8):
    shared = host_layout(inputs)
    x = np.asarray(inputs["x"], np.float32)
    c = np.asarray(inputs["c"], np.float32)
    pos = np.asarray(inputs["positions"], np.int32)
    maps = []
    for b in range(n_cores):
        m = dict(shared)
        m["x"] = np.ascontiguousarray(x[b])
        m["cT"] = np.ascontiguousarray(c[b].reshape(8, 128).T)
        m["pos"] = np.ascontiguousarray(pos[b:b + 1])
        maps.append(m)
    return maps


def kernel(**inputs):
    maps = make_in_maps(inputs, 8)
    prog = Prog()
    nc = prog.build()
    res = run_bass_kernel_spmd(nc, maps, core_ids=list(range(8)))
    out = np.stack([np.asarray(res.results[b]["out"], np.float32).reshape(S, D) for b in range(8)], axis=0)
    return out
```

```python
import math
from contextlib import ExitStack

import numpy as np
import concourse.bass as bass
import concourse.mybir as mybir
from concourse.bass_utils import run_bass_kernel_spmd

F32 = mybir.dt.float32
BF16 = mybir.dt.bfloat16
I32 = mybir.dt.int32
AF = mybir.ActivationFunctionType
ALU = mybir.AluOpType
AX = mybir.AxisListType

S = 2048
D = 1024
NT = 16
NB = 4
L = 2
NH = 8
EPS = 1e-6
FF = 2816
NJ = 22
IN_DIM = 6832
SM_SCALE = 96 ** -0.5
TWO_PI = 2.0 * math.pi


class KB:
    NDSEM = 24

    def __init__(self, nc, stack):
        self.nc = nc
        self.eng = {"pe": nc.tensor, "act": nc.scalar, "dve": nc.vector,
                    "pool": nc.gpsimd, "sp": nc.sync}
        self.sem = {}
        self.cnt = {}
        for e in self.eng:
            self.sem[e] = stack.enter_context(nc.semaphore("s_" + e))
            self.cnt[e] = 0
        self.dsem = [stack.enter_context(nc.semaphore("d%d" % i)) for i in range(self.NDSEM)]
        self.dcnt = [0] * self.NDSEM
        self.dnext = 0
        self.semobj = dict(self.sem)
        for i, s in enumerate(self.dsem):
            self.semobj[("d", i)] = s
        self.waited = {}
        self.last_w = {}
        self.readers = {}
        self.nwaits = 0
        self.nops = 0
        self.npe = 0
        self.marks = []

    def mark(self, name):
        self.marks.append((name, self.npe))

    def _deps(self, reads, writes):
        deps = {}

        def add(d):
            if d is None:
                return
            sk, v = d
            if deps.get(sk, 0) < v:
                deps[sk] = v
        for r in reads:
            add(self.last_w.get(r))
        for w in writes:
            add(self.last_w.get(w))
            for d in self.readers.get(w, ()):
                add(d)
        return deps

    def _emit_waits(self, e, deps, skip_self=False):
        for sk, v in deps.items():
            if skip_self and sk == e:
                continue
            if self.waited.get((e, sk), 0) >= v:
                continue
            self.eng[e].wait_ge(self.semobj[sk], v)
            self.waited[(e, sk)] = v
            self.nwaits += 1

    def _record(self, mark, reads, writes):
        for w in writes:
            self.last_w[w] = mark
            self.readers[w] = []
        for r in reads:
            self.readers.setdefault(r, []).append(mark)

    def group(self, e, fns, reads=(), writes=()):
        deps = self._deps(reads, writes)
        self._emit_waits(e, deps, skip_self=(e == "pe"))
        ins = None
        for fn in fns:
            ins = fn()
        if e == "pe":
            self.npe += len(fns)
        self.cnt[e] += 1
        ins.then_inc(self.sem[e], 1)
        self._record((e, self.cnt[e]), reads, writes)
        self.nops += len(fns)
        return ins

    def op(self, e, fn, reads=(), writes=()):
        return self.group(e, [fn], reads, writes)

    def dma(self, q, out, in_, reads=(), writes=(), **kw):
        deps = self._deps(reads, writes)
        j = self.dnext
        self.dnext = (self.dnext + 1) % self.NDSEM
        if self.dcnt[j] > 0:
            deps[("d", j)] = max(deps.get(("d", j), 0), self.dcnt[j])
        self._emit_waits(q, deps)
        ins = self.eng[q].dma_start(out=out, in_=in_, **kw)
        self.dcnt[j] += 16
        ins.then_inc(self.dsem[j], 16)
        self._record((("d", j), self.dcnt[j]), reads, writes)
        self.nops += 1
        return ins

    def barrier(self, name=None):
        self.mark("barrier")
        for e in self.eng:
            deps = {}
            for f in self.eng:
                if f != e and self.cnt[f] > 0:
                    deps[f] = self.cnt[f]
            for j in range(self.NDSEM):
                if self.dcnt[j] > 0:
                    deps[("d", j)] = self.dcnt[j]
            self._emit_waits(e, deps)
        self.last_w.clear()
        self.readers.clear()

    def finish(self):
        deps = {}
        for f in self.eng:
            if f != "sp" and self.cnt[f] > 0:
                deps[f] = self.cnt[f]
        for j in range(self.NDSEM):
            if self.dcnt[j] > 0:
                deps[("d", j)] = self.dcnt[j]
        self._emit_waits("sp", deps)


VP = {}
_off = 0
for _n, _w in [("ada_b", 48), ("norm1_w", 8), ("norm2_w", 8), ("q_a_norm", 3), ("kv_a_norm", 2),
               ("qwA", 1), ("qwB", 1), ("kwA", 1), ("kwB", 1), ("pool_scale", 4),
               ("ssd_conv_w", 48), ("ssd_conv_b", 12), ("ssd_norm_w", 8),
               ("ffn_conv_w", 132), ("ffn_conv_b", 44)]:
    VP[_n] = (_off, _w)
    _off += _w
NV = _off

C16 = {}
_off = 0
for _n, _w in [("ident", 128), ("ones", 128), ("bd", 128), ("maskneg", 128), ("mask01", 128),
               ("bcur", 512), ("bprev", 512), ("bcur0", 512), ("sel16", 1024), ("maskneg4", 512)]:
    C16[_n] = (_off, _w)
    _off += _w
N16 = _off
C32 = {}
_off = 0
for _n, _w in [("U", 128), ("ones", 128), ("sel", 1024), ("invf", 1), ("sgn", 1), ("ident", 128), ("invf2", 1), ("sgn2", 1)]:
    C32[_n] = (_off, _w)
    _off += _w
N32 = _off


def _kc(w):
    k, n = w.shape
    return np.ascontiguousarray(w.reshape(k // 128, 128, n).transpose(1, 0, 2)).reshape(128, (k // 128) * n)


def _col(v):
    return np.ascontiguousarray(v.reshape(-1, 128).T)


def host_constants():
    c16 = np.zeros((128, N16), np.float32)
    c32 = np.zeros((128, N32), np.float32)
    i = np.arange(128)
    o, w = C16["ident"]; c16[:, o:o + w] = np.eye(128)
    o, w = C16["ones"]; c16[:, o:o + w] = 1.0
    bd = np.zeros((128, 128), np.float32)
    bd[0:64, 0:64] = 1.0 / 64
    bd[64:96, 64:96] = 1.0 / 32
    o, w = C16["bd"]; c16[:, o:o + w] = bd
    o, w = C16["maskneg"]; c16[:, o:o + w] = np.where(i[:, None] > i[None, :], -30000.0, 0.0)
    o, w = C16["mask01"]; c16[:, o:o + w] = (i[:, None] <= i[None, :]).astype(np.float32)
    o, w = C16["maskneg4"]; c16[:, o:o + w] = np.tile(np.where(i[:, None] > i[None, :], -30000.0, 0.0), (1, 4))
    for g, win in enumerate((2, 4, 8, 16)):
        s = i[:, None]
        t = i[None, :]
        cur = np.where((t - s >= 0) & (t - s < win), 1.0 / win, 0.0) - (s == t)
        prev = np.where((t + 128 - s) < win, 1.0 / win, 0.0)
        cnt = np.minimum(t + 1, win).astype(np.float32)
        cur0 = np.where((t - s >= 0) & (t - s < win), 1.0 / cnt, 0.0) - (s == t)
        o, _ = C16["bcur"]; c16[:, o + g * 128:o + (g + 1) * 128] = cur
        o, _ = C16["bprev"]; c16[:, o + g * 128:o + (g + 1) * 128] = prev
        o, _ = C16["bcur0"]; c16[:, o + g * 128:o + (g + 1) * 128] = cur0
    sel16 = np.zeros((128, 8, 128), np.float32)
    for r in range(8):
        sel16[r, r, :] = 1.0
        sel16[32 + r, r, :] = 1.0
    o, w = C16["sel16"]; c16[:, o:o + w] = sel16.reshape(128, 1024)
    o, w = C32["U"]; c32[:, o:o + w] = (i[:, None] <= i[None, :]).astype(np.float32)
    o, w = C32["ones"]; c32[:, o:o + w] = 1.0
    o, w = C32["ident"]; c32[:, o:o + w] = np.eye(128)
    sel = np.zeros((128, 8, 128), np.float32)
    for r in range(8):
        sel[r, r, :] = 1.0
    o, w = C32["sel"]; c32[:, o:o + w] = sel.reshape(128, 1024)
    inv_freq = (10000.0 ** (-np.arange(0, 32, 2, dtype=np.float32) / np.float32(32))).astype(np.float32)
    invf = np.zeros(128, np.float32)
    invf[64:80] = inv_freq
    invf[80:96] = inv_freq
    sgn = np.zeros(128, np.float32)
    sgn[64:80] = -1.0
    sgn[80:96] = 1.0
    c32[:, C32["invf"][0]] = invf
    c32[:, C32["sgn"][0]] = sgn
    c32[:, C32["invf2"][0]] = np.tile(np.concatenate([inv_freq, inv_freq]), 4)
    c32[:, C32["sgn2"][0]] = np.tile(np.concatenate([-np.ones(16, np.float32), np.ones(16, np.float32)]), 4)
    return c16, c32


def host_layout(inp):
    f = lambda a: np.asarray(a, dtype=np.float32)
    w_in = f(inp["w_in"]); w_q_b = f(inp["w_q_b"]); w_kv_b = f(inp["w_kv_b"])
    out = {}
    z64 = np.zeros((1024, 64), np.float32)
    w_attn, w_qb, w_kn, w_v, w_u, w_pool, w_ssd, w_gate, w_br, w_o, w_up, w_dn, w_ada, vp = ([] for _ in range(14))
    for l in range(L):
        wi = w_in[l]
        krA = np.concatenate([z64, wi[:, 640:672]], axis=1)
        krB = np.concatenate([z64, wi[:, 656:672], wi[:, 640:656]], axis=1)
        w_attn.append(_kc(np.concatenate([wi[:, 0:640], krA, krB], axis=1)))
        qb = w_q_b[l]
        z = np.zeros((384, 64), np.float32)
        cols = []
        for h in range(NH):
            cols.append(qb[:, 96 * h:96 * h + 96])
            cols.append(np.concatenate([z, qb[:, 96 * h + 80:96 * h + 96], qb[:, 96 * h + 64:96 * h + 80]], axis=1))
        w_qb.append(_kc(np.concatenate(cols, axis=1)))
        kvb = w_kv_b[l].reshape(256, NH, 128)
        w_kn.append(_kc(np.ascontiguousarray(kvb[:, :, 0:64]).reshape(256, 512)))
        w_v.append(_kc(np.ascontiguousarray(kvb[:, :, 64:128]).reshape(256, 512)))
        w_u.append(_kc(wi[:, 672:1184]))
        w_pool.append(np.ascontiguousarray(f(inp["pool_w"])[l].transpose(1, 0, 2)).reshape(128, 512))
        gs = []
        for g in range(2):
            xb = 2208
            parts = [wi[:, xb + 512 * g + 128 * j: xb + 512 * g + 128 * (j + 1)] for j in range(4)]
            parts.append(wi[:, xb + 1024 + 128 * g: xb + 1024 + 128 * (g + 1)])
            parts.append(wi[:, xb + 1280 + 128 * g: xb + 1280 + 128 * (g + 1)])
            parts.append(wi[:, 1184 + 512 * g:1184 + 512 * (g + 1)])
            parts.append(wi[:, 3744 + 8 * g:3744 + 8 * (g + 1)])
            gs.append(_kc(np.concatenate(parts, axis=1)))
        w_ssd.append(np.stack(gs))
        gm = []
        bm = []
        for m in range(8):
            gm.append(_kc(np.concatenate([wi[:, 3760 + x * 1024 + m * 128:3760 + x * 1024 + (m + 1) * 128] for x in range(3)], axis=1)))
            bm.append(_kc(f(inp["w_branch"])[l][:, m * 128:(m + 1) * 128]))
        w_gate.append(np.stack(gm)); w_br.append(np.stack(bm))
        w_o.append(_kc(f(inp["w_out"])[l]))
        fu = f(inp["ffn_up"])[l]
        w_up.append(np.stack([_kc(np.concatenate([fu[:, j * 128:(j + 1) * 128], fu[:, FF + j * 128:FF + (j + 1) * 128]], axis=1)) for j in range(NJ)]))
        w_dn.append(_kc(f(inp["ffn_down"])[l]))
        aw = f(inp["ada_w"])[l]
        w_ada.append(np.stack([_kc(aw[:, i * 1024:(i + 1) * 1024]) for i in range(6)]))
        v = np.zeros((128, NV), np.float32)

        def put(name, arr):
            o, w = VP[name]
            assert arr.shape == (128, w), (name, arr.shape)
            v[:, o:o + w] = arr
        put("ada_b", _col(f(inp["ada_b"])[l]))
        put("norm1_w", _col(f(inp["norm1_w"])[l]))
        put("norm2_w", _col(f(inp["norm2_w"])[l]))
        put("q_a_norm", _col(f(inp["q_a_norm"])[l]))
        put("kv_a_norm", _col(f(inp["kv_a_norm"])[l]))
        for nm, src in (("q", f(inp["q_norm"])[l]), ("k", f(inp["k_norm"])[l])):
            a = np.zeros((128, 1), np.float32)
            b = np.zeros((128, 1), np.float32)
            a[0:96, 0] = src
            b[64:80, 0] = src[80:96]
            b[80:96, 0] = src[64:80]
            put(nm + "wA", a); put(nm + "wB", b)
        put("pool_scale", _col(f(inp["pool_scale"])[l]))
        cw = f(inp["ssd_conv_w"])[l]
        put("ssd_conv_w", np.concatenate([_col(cw[k]) for k in range(4)], axis=1))
        put("ssd_conv_b", _col(f(inp["ssd_conv_b"])[l]))
        put("ssd_norm_w", _col(f(inp["ssd_norm_w"])[l]))
        fw_ = f(inp["ffn_conv_w"])[l]
        put("ffn_conv_w", np.concatenate([_col(fw_[k]) for k in range(3)], axis=1))
        put("ffn_conv_b", _col(f(inp["ffn_conv_b"])[l]))
        vp.append(v)
    st = lambda xs: np.ascontiguousarray(np.stack(xs))
    out["w_attn"] = st(w_attn); out["w_qb"] = st(w_qb); out["w_kn"] = st(w_kn); out["w_v"] = st(w_v)
    out["w_u"] = st(w_u); out["w_pool"] = st(w_pool); out["w_ssd"] = st(w_ssd); out["w_gate"] = st(w_gate)
    out["w_br"] = st(w_br); out["w_o"] = st(w_o); out["w_up"] = st(w_up); out["w_dn"] = st(w_dn)
    out["w_ada"] = st(w_ada); out["vp"] = st(vp)
    out["ada_b"] = np.ascontiguousarray(f(inp["ada_b"]))
    out["ssd_small"] = np.ascontiguousarray(np.concatenate([f(inp["ssd_d"]), f(inp["ssd_dt_bias"]), f(inp["ssd_a_log"])], axis=1))
    out["ssd_nw"] = np.ascontiguousarray(f(inp["ssd_norm_w"]))
    c16, c32 = host_constants()
    out["c16"] = c16; out["c32"] = c32
    return out


SHARED_SHAPES = {
    "w_attn": [L, 128, 8 * 832], "w_qb": [L, 128, 3 * 8 * 192], "w_kn": [L, 128, 1024], "w_v": [L, 128, 1024],
    "w_u": [L, 128, 8 * 512], "w_pool": [L, 128, 512], "w_ssd": [L, 2, 128, 8 * 1288],
    "w_gate": [L, 8, 128, 8 * 384], "w_br": [L, 8, 128, 16 * 128], "w_o": [L, 128, 8 * 1024],
    "w_up": [L, NJ, 128, 8 * 256], "w_dn": [L, 128, NJ * 1024], "w_ada": [L, 6, 128, 8 * 1024],
    "vp": [L, 128, NV], "ada_b": [L, 6144], "ssd_small": [L, 48], "ssd_nw": [L, 1024], "c16": [128, N16], "c32": [128, N32],
}


class Prog:
    def __init__(self, nlayers=L, stop_after=None, dumps=()):
        self.nlayers = nlayers
        self.stop_after = stop_after
        self.dumps = set(dumps)
        self.dump_specs = {}
        nc = bass.Bass("TRN2", target_bir_lowering=False)
        self.nc = nc
        self.d = {}
        for n, shp in SHARED_SHAPES.items():
            self.d[n] = nc.dram_tensor(n, shp, F32, kind="ExternalInput").ap()
        self.d["x"] = nc.dram_tensor("x", [S, D], F32, kind="ExternalInput").ap()
        self.d["cT"] = nc.dram_tensor("cT", [128, 8], F32, kind="ExternalInput").ap()
        self.d["pos"] = nc.dram_tensor("pos", [1, S], I32, kind="ExternalInput").ap()
        self.d["out"] = nc.dram_tensor("out", [S, D], F32, kind="ExternalOutput").ap()
        self.xs = [nc.dram_tensor("xs%d" % i, [S, D], F32).ap() for i in range(3)]
        self.bk = 0
        self.bk5 = 0
        self.uid = 0

    def nb(self):
        b = self.bk
        self.bk = (b + 1) % 7
        return b

    def nb5(self):
        b = self.bk5
        self.bk5 = (b + 1) % 5
        return b

    def sbt(self, st, name, shape, dt):
        self.uid += 1
        return st.enter_context(self.nc.sbuf_tensor("%s_%d" % (name, self.uid), shape, dt))

    def dump(self, name, ap, shape, key, dt=F32):
        if name not in self.dumps:
            return
        dr = self.nc.dram_tensor("dbg_" + name, shape, dt, kind="ExternalOutput").ap()
        self.dump_specs[name] = shape
        self.kb.dma("sp", dr, ap, reads=[key] if not isinstance(key, list) else key)

    def mm(self, out, pairs, bank, reads, start=True):
        nc = self.nc
        n = len(pairs)
        fns = []
        for i, (l_, r_) in enumerate(pairs):
            fns.append(lambda i=i, l_=l_, r_=r_: nc.tensor.matmul(out, lhsT=l_, rhs=r_, start=(start and i == 0), stop=(i == n - 1)))
        self.kb.group("pe", fns, reads=reads, writes=[("ps", bank)])

    def act(self, out, in_, func, reads, writes, bias=0.0, scale=1.0, accum=None):
        nc = self.nc
        if accum is None:
            fn = lambda: nc.scalar.activation(out=out, in_=in_, func=func, bias=bias, scale=scale)
        else:
            fn = lambda: nc.scalar.activation(out=out, in_=in_, func=func, bias=bias, scale=scale, accum_out=accum)
        self.kb.op("act", fn, reads=reads, writes=writes)

    def rstd(self, out, in_, scale, bias, in_keys, wkey):
        self.act(out, in_, AF.Ln, reads=[], writes=list(in_keys) + [wkey], bias=bias, scale=scale)
        self.act(out, out, AF.Exp, reads=[], writes=[wkey], scale=-0.5)

    def run_chains(self, factories, width):
        todo = list(factories)
        free = list(range(width))
        active = []

        def refill():
            while todo and free:
                sl = free.pop(0)
                active.append((todo.pop(0)(sl), sl))
        refill()
        while active:
            for ent in list(active):
                try:
                    next(ent[0])
                except StopIteration:
                    active.remove(ent)
                    free.append(ent[1])
            refill()

    def dve(self, fn, reads, writes):
        self.kb.op("dve", fn, reads=reads, writes=writes)

    def wload(self, dst, src, key):
        self.kb.dma("pool", dst, src, writes=[key])

    def build(self):
        nc = self.nc
        with ExitStack() as st:
            self.kb = KB(nc, st)
            kb = self.kb
            ps_all = st.enter_context(nc.psum_tensor("ps_all", [128, 7 * 512], F32))
            self.ps = [ps_all[:, i * 512:(i + 1) * 512] for i in range(7)]
            self.psb = st.enter_context(nc.psum_tensor("psb", [128, 1024], BF16))
            self.c16 = self.sbt(st, "c16", [128, N16], BF16)
            self.c32 = self.sbt(st, "c32", [128, N32], F32)
            kb.dma("pool", self.c16[:], self.d["c16"], writes=["c16"])
            kb.dma("sp", self.c32[:], self.d["c32"], writes=["c32"])
            self.k16 = lambda n, j=0: self.c16[:, C16[n][0] + j * 128:C16[n][0] + (j + 1) * 128]
            self.k32 = lambda n, j=0: self.c32[:, C32[n][0] + j * 128:C32[n][0] + (j + 1) * 128]
            self.rope_tables(st)
            self.cact(st)
            kb.barrier()
            xin = self.d["x"]
            for l in range(self.nlayers):
                xout1 = self.xs[2 * l % 3]
                xout2 = self.d["out"] if l == self.nlayers - 1 else self.xs[(2 * l + 1) % 3]
                with ExitStack() as lst:
                    self.layer(lst, l, xin, xout1, xout2)
                    kb.barrier()
                xin = xout2
                if self.stop_after is not None and self.stop_after[0] == l:
                    break
            kb.finish()
        return nc

    def rope_tables(self, st):
        nc = self.nc
        kb = self.kb
        self.rope_c = nc.dram_tensor("rope_c", [128, S], F32).ap()
        self.rope_s = nc.dram_tensor("rope_s", [128, S], F32).ap()
        with ExitStack() as t:
            Cs = self.sbt(t, "Ctab", [128, 512], F32)
            Ss = self.sbt(t, "Stab", [128, 512], F32)
            pi_ = self.sbt(t, "posi", [128, 512], I32)
            v = self.sbt(t, "ropev", [128, 512], F32)
            ki = self.sbt(t, "ropeki", [128, 512], I32)
            kf = self.sbt(t, "ropekf", [128, 512], F32)
            w = self.sbt(t, "ropew", [128, 512], F32)
            fill = self.sbt(t, "ropefill", [64, S], F32)
            for q in range(4):
                kb.dma("sp", pi_[q * 32:(q + 1) * 32, :], self.d["pos"][0:1, q * 512:(q + 1) * 512].partition_broadcast(32), writes=[("posi", q)])
            invf = self.c32[:, C32["invf2"][0]:C32["invf2"][0] + 1]
            sgn = self.c32[:, C32["sgn2"][0]:C32["sgn2"][0] + 1]
            self.dve(lambda: nc.vector.memset(fill[:], 0.0), [], ["ropefill"])
            kb.dma("sp", self.rope_s[0:64, :], fill[:], reads=["ropefill"], writes=["fillz0"])
            kb.dma("sp", self.rope_s[96:128, :], fill[0:32, :], reads=["ropefill"], writes=["fillz1"])
            self.dve(lambda: nc.vector.memset(fill[:], 1.0), ["fillz0", "fillz1"], ["ropefill"])
            kb.dma("sp", self.rope_c[0:64, :], fill[:], reads=["ropefill"])
            kb.dma("sp", self.rope_c[96:128, :], fill[0:32, :], reads=["ropefill"])
            self.dve(lambda: nc.vector.tensor_copy(out=v[:], in_=pi_[:]), [("posi", q) for q in range(4)], ["ropev"])
            self.dve(lambda: nc.vector.tensor_scalar(out=v[:], in0=v[:], scalar1=invf, scalar2=1.0 / TWO_PI, op0=ALU.mult, op1=ALU.mult), ["ropev", "c32"], ["ropev"])
            for which, shift, dst in (("c", 0.25, Cs), ("s", 0.0, Ss)):
                self.dve(lambda shift=shift: nc.vector.tensor_scalar(out=w[:], in0=v[:], scalar1=shift, scalar2=None, op0=ALU.add), ["ropev"], ["ropew"])
                self.dve(lambda: nc.vector.tensor_copy(out=ki[:], in_=w[:]), ["ropew"], ["ropeki"])
                self.dve(lambda: nc.vector.tensor_copy(out=kf[:], in_=ki[:]), ["ropeki"], ["ropekf"])
                self.dve(lambda: nc.vector.tensor_sub(out=w[:], in0=w[:], in1=kf[:]), ["ropekf", "ropew"], ["ropew"])
                self.dve(lambda: nc.vector.tensor_single_scalar(out=kf[:], in_=w[:], scalar=0.5, op=ALU.is_gt), ["ropew"], ["ropekf"])
                self.dve(lambda: nc.vector.tensor_sub(out=w[:], in0=w[:], in1=kf[:]), ["ropekf", "ropew"], ["ropew"])
                self.dve(lambda: nc.vector.tensor_single_scalar(out=kf[:], in_=w[:], scalar=-0.5, op=ALU.is_lt), ["ropew"], ["ropekf"])
                self.dve(lambda: nc.vector.tensor_add(out=w[:], in0=w[:], in1=kf[:]), ["ropekf", "ropew"], ["ropew"])
                self.act(dst[:], w[:], AF.Sin, reads=["ropew"], writes=["tab" + which], scale=TWO_PI)
            self.dve(lambda: nc.vector.tensor_scalar(out=Ss[:], in0=Ss[:], scalar1=sgn, scalar2=None, op0=ALU.mult), ["tabs", "c32"], ["tabs"])
            for q in range(4):
                kb.dma("sp", self.rope_c[64:96, q * 512:(q + 1) * 512], Cs[q * 32:(q + 1) * 32, :], reads=["tabc"])
                kb.dma("sp", self.rope_s[64:96, q * 512:(q + 1) * 512], Ss[q * 32:(q + 1) * 32, :], reads=["tabs"])
            kb.barrier()

    def cact(self, st):
        nc = self.nc
        kb = self.kb
        cT = self.sbt(st, "cT", [128, 8], F32)
        ca = self.sbt(st, "cact", [128, 8], BF16)
        self.cbc = self.sbt(st, "cbc", [128, 8, 128], BF16)
        kb.dma("sp", cT[:], self.d["cT"], writes=["cT"])
        self.act(ca[:], cT[:], AF.Silu, reads=["cT"], writes=["cact"])
        self.cactb = ca
        self.dve(lambda: nc.vector.tensor_copy(out=self.cbc[:], in_=ca[:].unsqueeze(2).to_broadcast([128, 8, 128])), ["cact"], ["cbc"])

    def layer(self, st, l, xin, xout1, xout2):
        nc = self.nc
        kb = self.kb
        self.l = l
        self.vp = self.sbt(st, "vp", [128, NV], F32)
        kb.dma("sp", self.vp[:], self.d["vp"][l], writes=["vp"])
        self.vpc = lambda n, j=0, w=1: self.vp[:, VP[n][0] + j:VP[n][0] + j + w]
        self.ada(st, l)
        if self.stop_after == (l, "ada"):
            return
        self.hT = self.sbt(st, "hT", [128, 8, S], BF16)
        with ExitStack() as ph:
            self.norm_a(ph, self.load_x(ph, xin, 0), 0)
            for t in range(NT):
                if t + 1 < NT:
                    self.norm_a(ph, self.load_x(ph, xin, t + 1), t + 1)
                self.norm_b(self.s1, self.b1, t, mixed=True)
            kb.barrier()
        self.dump("hT%d" % l, self.hT[:], [128, 8, S], [("hT", b) for b in range(NB)], BF16)
        if self.stop_after == (l, "norm1"):
            return
        with ExitStack() as mix:
            self.mix(mix, l, xin, xout1)
            kb.barrier()
        if self.stop_after is not None and self.stop_after[0] == l and self.stop_after[1] != "ffn":
            return
        with ExitStack() as ph:
            self.ffn(ph, l, xout1, xout2)
            kb.barrier()

    def mix(self, st, l, xin, xout1):
        kb = self.kb
        self.o_a = self.sbt(st, "o_a", [128, 4, S], BF16)
        with ExitStack() as ph:
            self.attention(ph, l)
            kb.barrier()
        self.dump("o_a%d" % l, self.o_a[:], [128, 4, S], "o_a", BF16)
        if self.stop_after == (l, "attn"):
            return
        self.o_c = self.sbt(st, "o_c", [128, 8, S], BF16)
        for g in range(2):
            with ExitStack() as ph:
                self.ssd(ph, l, g)
                kb.barrier()
        self.dump("o_c%d" % l, self.o_c[:], [128, 8, S], "o_c", BF16)
        if self.stop_after == (l, "ssd"):
            return
        self.o_b = self.sbt(st, "o_b", [128, 4, S], BF16)
        self.mergedT = self.sbt(st, "mergedT", [128, 8, S], BF16)
        self.wg_pre = self.sbt(st, "wgate", [128, 8, 3, 128], BF16)
        self.wb_pre = self.sbt(st, "wbr", [128, 16, 128], BF16)
        with ExitStack() as ph:
            self.pool_branch(ph, l)
            kb.barrier()
        self.dump("o_b%d" % l, self.o_b[:], [128, 4, S], "o_b", BF16)
        if self.stop_after == (l, "pool"):
            return
        self.wo = self.sbt(st, "wo", [128, 8, D], BF16)
        with ExitStack() as ph:
            self.merge(ph, l)
            kb.barrier()
        self.dump("merged%d" % l, self.mergedT[:], [128, 8, S], "merged", BF16)
        if self.stop_after == (l, "merge"):
            return
        with ExitStack() as ph:
            self.wout_phase(ph, l, xin, xout1)
            kb.barrier()
        self.dump("h2T%d" % l, self.hT[:], [128, 8, S], [("hT", b) for b in range(NB)], BF16)

    def ada(self, st, l):
        nc = self.nc
        kb = self.kb
        kb.mark("ada")
        self.modp = self.sbt(st, "modp", [128, 6, 8], F32)
        self.gbc = self.sbt(st, "gbc", [128, 2, D], F32)
        self.s1 = self.sbt(st, "s1", [128, 8], F32)
        self.s2 = self.sbt(st, "s2", [128, 8], F32)
        with ExitStack() as ph:
            slots = [self.sbt(ph, "adaw", [128, 8, 1024], BF16) for _ in range(2)]
            abc = self.sbt(ph, "adab_bc", [128, 2, D], F32)
            for gi, pc in enumerate((2, 5)):
                kb.dma("sp", abc[:, gi, :], self.d["ada_b"][l:l + 1, pc * 1024:(pc + 1) * 1024].partition_broadcast(128), writes=[("abc", gi)])
            for i in range(6):
                sl = slots[i % 2]
                key = ("adaw", i % 2)
                self.wload(sl[:], self.d["w_ada"][l, i].rearrange("p (k n) -> p k n", k=8), key)
                if i in (2, 5):
                    gi = 0 if i == 2 else 1
                    for half in range(2):
                        b = self.nb()
                        self.mm(self.ps[b], [(self.cbc[:, kc, :], sl[:, kc, half * 512:(half + 1) * 512]) for kc in range(8)], b, reads=[key, "cbc"])
                        self.dve(lambda b=b, gi=gi, half=half: nc.vector.tensor_add(out=self.gbc[:, gi, half * 512:(half + 1) * 512], in0=self.ps[b], in1=abc[:, gi, half * 512:(half + 1) * 512]),
                                 [("abc", gi)], [("ps", b), ("gbc", gi)])
                else:
                    b = self.nb()
                    fns = []
                    for j in range(8):
                        for kc in range(8):
                            fns.append(lambda j=j, kc=kc, b=b, sl=sl: nc.tensor.matmul(self.ps[b][:, j:j + 1], lhsT=sl[:, kc, j * 128:(j + 1) * 128], rhs=self.cactb[:, kc:kc + 1], start=(kc == 0), stop=(kc == 7)))
                    kb.group("pe", fns, reads=[key, "cact"], writes=[("ps", b)])
                    self.dve(lambda b=b, i=i: nc.vector.tensor_add(out=self.modp[:, i, :], in0=self.ps[b][:, 0:8], in1=self.vpc("ada_b", i * 8, 8)),
                             ["vp"], [("ps", b), ("modp", i)])
            self.dve(lambda: nc.vector.scalar_tensor_tensor(out=self.s1[:], in0=self.modp[:, 1, :], scalar=1.0, in1=self.vpc("norm1_w", 0, 8), op0=ALU.add, op1=ALU.mult), [("modp", 1), "vp"], ["s1"])
            self.dve(lambda: nc.vector.scalar_tensor_tensor(out=self.s2[:], in0=self.modp[:, 4, :], scalar=1.0, in1=self.vpc("norm2_w", 0, 8), op0=ALU.add, op1=ALU.mult), [("modp", 4), "vp"], ["s2"])
            self.b1 = self.modp[:, 0, :]
            self.b2 = self.modp[:, 3, :]
            self.dump("modp%d" % l, self.modp[:], [128, 6, 8], [("modp", i) for i in (0, 1, 3, 4)])
            self.dump("gbc%d" % l, self.gbc[:], [128, 2, D], [("gbc", 0), ("gbc", 1)])
            kb.barrier()

    def load_x(self, ph, xsrc, t):
        if not hasattr(self, "_xbufs") or self._xbufs_ph is not ph:
            self._xbufs = [self.sbt(ph, "xt", [128, D], F32) for _ in range(2)]
            self._xbufs_ph = ph
            self._xi = 0
        i = self._xi
        self._xi = (i + 1) % 2
        xt = self._xbufs[i]
        self.kb.dma("sp", xt[:], xsrc[t * 128:(t + 1) * 128, :], writes=[("xt", i)])
        return (xt, [("xt", i)])

    def norm_a(self, ph, xtk, t):
        nc = self.nc
        xt, xkeys = xtk
        if not hasattr(self, "_nb") or self._nb_ph is not ph:
            self._nb = dict(junk=self.sbt(ph, "njunk", [128, D], BF16),
                            ss=[self.sbt(ph, "nss", [128, 1], F32) for _ in range(2)],
                            xn=[self.sbt(ph, "nxn", [128, D], BF16) for _ in range(2)],
                            tmp=self.sbt(ph, "ntmp", [128, 4, 128], F32))
            self._nb_ph = ph
        i = t % 2
        junk = self._nb["junk"]
        ss = self._nb["ss"][i]
        xn = self._nb["xn"][i]
        self.act(junk[:], xt[:], AF.Square, reads=list(xkeys), writes=["njunk", ("nss", i)], accum=ss[:])
        self.rstd(ss[:], ss[:], 1.0 / D, EPS, [], ("nss", i))
        self.dve(lambda: nc.vector.tensor_scalar(out=xn[:], in0=xt[:], scalar1=ss[:, 0:1], scalar2=None, op0=ALU.mult), list(xkeys) + [("nss", i)], [("nxn", i)])

    def norm_b(self, s_ap, b_ap, t, skey=("s1",), bkey=(("modp", 0),), mixed=False):
        nc = self.nc
        i = t % 2
        xn = self._nb["xn"][i]
        self.kb.group("pe", [lambda kc=kc: nc.tensor.transpose(self.psb[:, kc * 128:(kc + 1) * 128], xn[:, kc * 128:(kc + 1) * 128], self.k16("ident")) for kc in range(8)],
                      reads=[("nxn", i), "c16"], writes=[("ps", 7)])
        nact = 4 if mixed else 8
        for kc in range(nact):
            self.act(self.hT[:, kc, t * 128:(t + 1) * 128], self.psb[:, kc * 128:(kc + 1) * 128], AF.Identity,
                     reads=list(skey) + list(bkey), writes=[("ps", 7), ("hT", t // 4)], scale=s_ap[:, kc:kc + 1], bias=b_ap[:, kc:kc + 1])
        if mixed:
            tmp = self._nb["tmp"]
            pin = self.psb[:, 512:1024].rearrange("p (c n) -> p c n", c=4)
            self.dve(lambda: nc.vector.tensor_tensor(out=tmp[:], in0=pin, in1=s_ap[:, 4:8].unsqueeze(2).to_broadcast([128, 4, 128]), op=ALU.mult),
                     list(skey), [("ps", 7), "ntmp"])
            self.dve(lambda: nc.vector.tensor_tensor(out=self.hT[:, 4:8, t * 128:(t + 1) * 128], in0=tmp[:], in1=b_ap[:, 4:8].unsqueeze(2).to_broadcast([128, 4, 128]), op=ALU.add),
                     list(bkey) + ["ntmp"], [("hT", t // 4)])

    def attention(self, ph, l):
        nc = self.nc
        kb = self.kb
        kb.mark("attention")
        wa = self.sbt(ph, "wattn", [128, 8, 832], BF16)
        wasrc = self.d["w_attn"][l].rearrange("p (k n) -> p k n", k=8)
        self.wload(wa[:, :, 384:640], wasrc[:, :, 384:640], ("wattn", "kv"))
        self.wload(wa[:, :, 640:832], wasrc[:, :, 640:832], ("wattn", "kr"))
        wqb = self.sbt(ph, "wqb", [128, 3, NH, 192], BF16)
        self._late_attn_loads = lambda: (
            self.kb.dma("pool", wa[:, :, 0:384], wasrc[:, :, 0:384], reads=[("ckvn", 0)], writes=[("wattn", "q")]),
            self.kb.dma("pool", wqb[:], self.d["w_qb"][l].rearrange("p (k h n) -> p k h n", k=3, h=NH), reads=[("ckvn", 0)], writes=["wqb"]))
        kT = self.sbt(ph, "kT", [128, NH, S], BF16)
        vext = self.sbt(ph, "vext", [128, NT, NH, 65], BF16)
        self.Ctab = self.sbt(ph, "Ctab", [128, S], F32)
        self.Stab = self.sbt(ph, "Stab", [128, S], F32)
        kb.dma("sp", self.Ctab[:], self.rope_c, writes=["tabc"])
        kb.dma("sp", self.Stab[:], self.rope_s, writes=["tabs"])
        self.dve(lambda: nc.vector.memset(vext[:], 1.0), [], ["vext"])
        ones16 = self.k16("ones")
        bd = self.k16("bd")

        def mkscr(stk, n):
            scr = [self.sbt(stk, "ascr", [128, 512], F32) for _ in range(n)]
            state = {"i": 0}

            def nscr():
                i = state["i"]
                state["i"] = (i + 1) % n
                return scr[i], ("ascr", i)
            return nscr

        with ExitStack() as sa:
            wkn = self.sbt(sa, "wkn", [128, 2, 512], BF16)
            wv = self.sbt(sa, "wv", [128, 2, 512], BF16)
            self.wload(wkn[:], self.d["w_kn"][l].rearrange("p (k n) -> p k n", k=2), "wkn")
            self.wload(wv[:], self.d["w_v"][l].rearrange("p (k n) -> p k n", k=2), "wv")
            ckvn = self.sbt(sa, "ckvn", [128, 2, S], BF16)
            ksq = [self.sbt(sa, "ksq", [64, 512], BF16) for _ in range(3)]
            krs = [self.sbt(sa, "krs", [64, 512], F32) for _ in range(3)]
            sqb = [self.sbt(sa, "asq", [128, 3, 512], BF16) for _ in range(2)]
            nscr = mkscr(sa, 4)
            for b in range(NB):
                bs = slice(b * 512, (b + 1) * 512)
                hk = ("hT", b)
                sq, sqk = sqb[b % 2], ("asq", b % 2)
                banks = []
                for c in range(2):
                    bk = self.nb()
                    banks.append(bk)
                    self.mm(self.ps[bk], [(wa[:, kc, 384 + c * 128:384 + (c + 1) * 128], self.hT[:, kc, bs]) for kc in range(8)], bk, reads=[("wattn", "kv"), hk])
                    self.act(sq[:, c, :], self.ps[bk], AF.Square, reads=[], writes=[("ps", bk), (sqk, c)])
                bss = self.nb()
                self.mm(self.ps[bss], [(ones16, sq[:, c, :]) for c in range(2)], bss, reads=["c16", (sqk, 0), (sqk, 1)])
                rs, rsk = nscr()
                self.rstd(rs[:], self.ps[bss], 1.0 / 256, EPS, [("ps", bss)], rsk)
                for c in range(2):
                    self.dve(lambda c=c, rs=rs, bk=banks[c]: nc.vector.scalar_tensor_tensor(out=ckvn[:, c, bs], in0=self.ps[bk], scalar=self.vpc("kv_a_norm", c), in1=rs[:], op0=ALU.mult, op1=ALU.mult),
                             [rsk, "vp"], [("ps", banks[c]), ("ckvn", b)])
                if b == 0:
                    self._late_attn_loads()
                bA = self.nb()
                self.mm(self.ps[bA][0:96, :], [(wa[:, kc, 640:736], self.hT[:, kc, bs]) for kc in range(8)], bA, reads=[("wattn", "kr"), hk])
                bB = self.nb()
                self.mm(self.ps[bB][0:96, :], [(wa[:, kc, 736:832], self.hT[:, kc, bs]) for kc in range(8)], bB, reads=[("wattn", "kr"), hk])
                self.act(sq[0:96, 2, :], self.ps[bA][0:96, :], AF.Square, reads=[], writes=[("ps", bA), (sqk, 2)])
                bm = self.nb()
                self.mm(self.ps[bm][0:96, :], [(bd[0:96, 0:96], sq[0:96, 2, :])], bm, reads=["c16", (sqk, 2)])
                rs, rsk = nscr()
                self.rstd(rs[0:96, :], self.ps[bm][0:96, :], 1.0, EPS, [("ps", bm)], rsk)
                t1, t1k = nscr()
                t2, t2k = nscr()
                self.dve(lambda t1=t1, bA=bA: nc.vector.scalar_tensor_tensor(out=t1[64:96, :], in0=self.ps[bA][64:96, :], scalar=self.vpc("kwA")[64:96, :], in1=self.Ctab[64:96, bs], op0=ALU.mult, op1=ALU.mult),
                         ["vp", "tabc"], [("ps", bA), t1k])
                self.dve(lambda t2=t2, bB=bB: nc.vector.scalar_tensor_tensor(out=t2[64:96, :], in0=self.ps[bB][64:96, :], scalar=self.vpc("kwB")[64:96, :], in1=self.Stab[64:96, bs], op0=ALU.mult, op1=ALU.mult),
                         ["vp", "tabs"], [("ps", bB), t2k])
                self.dve(lambda t1=t1, t2=t2: nc.vector.tensor_add(out=t1[64:96, :], in0=t1[64:96, :], in1=t2[64:96, :]), [t2k], [t1k])
                self.dve(lambda t1=t1, rs=rs: nc.vector.tensor_mul(out=t1[64:96, :], in0=t1[64:96, :], in1=rs[64:96, :]), [rsk], [t1k])
                self.dve(lambda t1=t1: nc.vector.tensor_copy(out=kT[64:96, :, bs], in_=t1[64:96, :].unsqueeze(1).to_broadcast([32, NH, 512])), [t1k], [("kTr", b)])
                def kchain(h, b=b, bs=bs):
                    def gen(slot):
                        bk, bm = slot, 3 + slot
                        sqh, sqhk = ksq[slot], ("ksq", slot)
                        rs, rsk = krs[slot], ("krs", slot)
                        self.mm(self.ps[bk][0:64, :], [(wkn[:, c, h * 64:(h + 1) * 64], ckvn[:, c, bs]) for c in range(2)], bk, reads=["wkn", ("ckvn", b)])
                        yield
                        self.act(sqh[0:64, :], self.ps[bk][0:64, :], AF.Square, reads=[], writes=[("ps", bk), sqhk])
                        yield
                        self.mm(self.ps[bm][0:64, :], [(bd[0:64, 0:64], sqh[0:64, :])], bm, reads=["c16", sqhk])
                        yield
                        self.act(rs[0:64, :], self.ps[bm][0:64, :], AF.Ln, reads=[], writes=[("ps", bm), rsk], bias=EPS, scale=1.0)
                        yield
                        self.act(rs[0:64, :], rs[0:64, :], AF.Exp, reads=[], writes=[rsk], scale=-0.5)
                        yield
                        self.dve(lambda: nc.vector.scalar_tensor_tensor(out=kT[0:64, h, bs], in0=self.ps[bk][0:64, :], scalar=self.vpc("kwA")[0:64, :], in1=rs[0:64, :], op0=ALU.mult, op1=ALU.mult),
                                 [rsk, "vp"], [("ps", bk), ("kTn", b, h)])
                        yield
                    return gen
                self.run_chains([kchain(h) for h in range(NH)], 3)
                for tt in range(4):
                    t = b * 4 + tt
                    bk = self.nb()
                    self.mm(self.ps[bk], [(ckvn[:, c, t * 128:(t + 1) * 128], wv[:, c, :]) for c in range(2)], bk, reads=["wv", ("ckvn", b)])
                    self.act(vext[:, t, :, 0:64], self.ps[bk].rearrange("p (h d) -> p h d", h=NH), AF.Identity, reads=[], writes=[("ps", bk), "vext"])
            kb.barrier()
        self.dump("kT%d" % l, kT[:], [128, NH, S], [], BF16)
        self.dump("vext%d" % l, vext[:], [128, NT, NH, 65], [], BF16)

        with ExitStack() as sq_:
            sqb = [self.sbt(sq_, "asq", [128, 3, 512], BF16)] * 2
            nscr = mkscr(sq_, 2)
            ql = self.sbt(sq_, "qln", [128, 3, 512], BF16)
            qsq = [self.sbt(sq_, "qsq", [128, 512], BF16) for _ in range(2)]
            qsc = [self.sbt(sq_, "qsc", [128, 512], F32) for _ in range(6)]
            qT = [self.sbt(sq_, "qT", [128, NH, 512], BF16) for _ in range(2)]
            Eb = [self.sbt(sq_, "E", [128, 512], BF16) for _ in range(3)]
            ot = self.sbt(sq_, "otok", [128, 4, 512], BF16)
            rcp = [self.sbt(sq_, "rcp", [128, 4], F32) for _ in range(2)]
            mask01 = self.k16("mask01")
            ei = 0
            for qb_ in range(NB):
                bs = slice(qb_ * 512, (qb_ + 1) * 512)
                hk = ("hT", qb_)
                qlk = "qln"
                qt_, qtk = qT[qb_ % 2], ("qT", qb_ % 2)
                sq, sqk = sqb[0], ("asq", 0)
                banks = []
                for c in range(3):
                    bk = self.nb()
                    banks.append(bk)
                    self.mm(self.ps[bk], [(wa[:, kc, c * 128:(c + 1) * 128], self.hT[:, kc, bs]) for kc in range(8)], bk, reads=[("wattn", "q"), hk])
                    self.act(sq[:, c, :], self.ps[bk], AF.Square, reads=[], writes=[("ps", bk), (sqk, c)])
                bss = self.nb()
                self.mm(self.ps[bss], [(ones16, sq[:, c, :]) for c in range(3)], bss, reads=["c16"] + [(sqk, c) for c in range(3)])
                rs, rsk = nscr()
                self.rstd(rs[:], self.ps[bss], 1.0 / 384, EPS, [("ps", bss)], rsk)
                for c in range(3):
                    self.dve(lambda c=c, rs=rs, bk=banks[c]: nc.vector.scalar_tensor_tensor(out=ql[:, c, :], in0=self.ps[bk], scalar=self.vpc("q_a_norm", c), in1=rs[:], op0=ALU.mult, op1=ALU.mult),
                             [rsk, "vp"], [("ps", banks[c]), (qlk, c)])
                def qchain(h, qb_=qb_, bs=bs, qt_=qt_, qtk=qtk):
                    def gen(slot):
                        bA, bB, bm = 3 * slot, 3 * slot + 1, 3 * slot + 2
                        sqh, sqhk = qsq[slot], ("qsq", slot)
                        rs, rsk = qsc[3 * slot], ("qsc", 3 * slot)
                        t1, t1k = qsc[3 * slot + 1], ("qsc", 3 * slot + 1)
                        t2, t2k = qsc[3 * slot + 2], ("qsc", 3 * slot + 2)
                        self.mm(self.ps[bA][0:96, :], [(wqb[:, c, h, 0:96], ql[:, c, :]) for c in range(3)], bA, reads=["wqb"] + [(qlk, c) for c in range(3)])
                        self.mm(self.ps[bB][0:96, :], [(wqb[:, c, h, 96:192], ql[:, c, :]) for c in range(3)], bB, reads=["wqb"] + [(qlk, c) for c in range(3)])
                        yield
                        self.act(sqh[0:96, :], self.ps[bA][0:96, :], AF.Square, reads=[], writes=[("ps", bA), sqhk])
                        yield
                        self.mm(self.ps[bm][0:96, :], [(bd[0:96, 0:96], sqh[0:96, :])], bm, reads=["c16", sqhk])
                        self.dve(lambda: nc.vector.scalar_tensor_tensor(out=t1[0:96, :], in0=self.ps[bA][0:96, :], scalar=self.vpc("qwA")[0:96, :], in1=self.Ctab[0:96, bs], op0=ALU.mult, op1=ALU.mult),
                                 ["vp", "tabc"], [("ps", bA), t1k])
                        yield
                        self.act(rs[0:96, :], self.ps[bm][0:96, :], AF.Ln, reads=[], writes=[("ps", bm), rsk], bias=EPS / (SM_SCALE ** 2), scale=1.0 / (SM_SCALE ** 2))
                        self.dve(lambda: nc.vector.scalar_tensor_tensor(out=t2[0:96, :], in0=self.ps[bB][0:96, :], scalar=self.vpc("qwB")[0:96, :], in1=self.Stab[0:96, bs], op0=ALU.mult, op1=ALU.mult),
                                 ["vp", "tabs"], [("ps", bB), t2k])
                        yield
                        self.act(rs[0:96, :], rs[0:96, :], AF.Exp, reads=[], writes=[rsk], scale=-0.5)
                        self.dve(lambda: nc.vector.tensor_add(out=t1[0:96, :], in0=t1[0:96, :], in1=t2[0:96, :]), [t2k], [t1k])
                        yield
                        self.dve(lambda: nc.vector.tensor_mul(out=qt_[0:96, h, :], in0=t1[0:96, :], in1=rs[0:96, :]), [rsk, t1k], [(qtk, h)])
                        yield
                    return gen
                self.run_chains([qchain(h) for h in range(NH)], 2)
                if qb_ == 0:
                    self.dump("qT%d" % l, qt_[:], [128, NH, 512], [(qtk, h) for h in range(NH)], BF16)
                otk = "otok"
                LA = 2
                for h in range(NH):
                    bo = 5 + (h % 2)
                    nkt = 4 * qb_ + 4
                    st = {"first": True}
                    pend = {}

                    def score(kt, h=h, qb_=qb_):
                        j0 = max(0, kt - 4 * qb_)
                        cs = slice(j0 * 128, 512)
                        bsT = self.nb5()
                        self.mm(self.ps[bsT][:, cs], [(kT[0:96, h, kt * 128:(kt + 1) * 128], qt_[0:96, h, cs])], bsT, reads=[(qtk, h)])
                        pend[kt] = (bsT, j0, cs)

                    def finish(kt, h=h, qb_=qb_, bo=bo, st=st):
                        nonlocal ei
                        bsT, j0, cs = pend.pop(kt)
                        E, Ek = Eb[ei % 3], ("E", ei % 3)
                        ei += 1
                        self.act(E[:, cs], self.ps[bsT][:, cs], AF.Exp, reads=[], writes=[("ps", bsT), Ek])
                        if kt >= 4 * qb_:
                            self.dve(lambda E=E, j0=j0: nc.vector.tensor_mul(out=E[:, j0 * 128:(j0 + 1) * 128], in0=E[:, j0 * 128:(j0 + 1) * 128], in1=mask01), ["c16"], [Ek])
                        fns = []
                        for j in range(j0, 4):
                            qtile = 4 * qb_ + j
                            st_flag = st["first"]
                            st["first"] = False
                            fns.append(lambda j=j, E=E, kt=kt, st_flag=st_flag, qtile=qtile: nc.tensor.matmul(
                                self.ps[bo][:, j * 65:(j + 1) * 65], lhsT=E[:, j * 128:(j + 1) * 128], rhs=vext[:, kt, h, :],
                                start=st_flag, stop=(kt == qtile), skip_group_check=True))
                        kb.group("pe", fns, reads=[Ek], writes=[("ps", bo)])
                    for kt in range(nkt):
                        score(kt)
                        if kt >= LA:
                            finish(kt - LA)
                    for kt in range(max(0, nkt - LA), nkt):
                        finish(kt)
                    rc, rck = rcp[h % 2], ("rcp", h % 2)
                    pview = self.ps[bo][:, 0:260].rearrange("p (j e) -> p j e", j=4)
                    self.dve(lambda rc=rc, pview=pview: nc.vector.reciprocal(out=rc[:].unsqueeze(2), in_=pview[:, :, 64:65]), [], [("ps", bo), rck])
                    self.dve(lambda rc=rc, pview=pview, h=h: nc.vector.tensor_tensor(out=ot[:, :, h * 64:(h + 1) * 64], in0=pview[:, :, 0:64], in1=rc[:].unsqueeze(2).to_broadcast([128, 4, 64]), op=ALU.mult),
                             [rck], [("ps", bo), (otk, h)])
                for j in range(4):
                    t = 4 * qb_ + j
                    kb.group("pe", [lambda c=c, j=j: nc.tensor.transpose(self.psb[:, c * 128:(c + 1) * 128], ot[:, j, c * 128:(c + 1) * 128], self.k16("ident")) for c in range(4)],
                             reads=[(otk, h) for h in range(NH)] + ["c16"], writes=[("ps", 7)])
                    self.dve(lambda t=t: nc.vector.tensor_copy(out=self.o_a[:, :, t * 128:(t + 1) * 128], in_=self.psb[:, 0:512].rearrange("p (c n) -> p c n", c=4)), [], [("ps", 7), "o_a"])
            kb.barrier()

    def pool_branch(self, ph, l):
        nc = self.nc
        kb = self.kb
        kb.mark("pool_branch")
        wu = self.sbt(ph, "wu", [128, 8, 512], BF16)
        wp = self.sbt(ph, "wpool", [128, 4, 128], BF16)
        self.wload(wu[:], self.d["w_u"][l].rearrange("p (k n) -> p k n", k=8), "wu")
        self.wload(wp[:], self.d["w_pool"][l].rearrange("p (g n) -> p g n", g=4), "wpool")
        self.wload(self.wg_pre[:], self.d["w_gate"][l, 0].rearrange("p (k x n) -> p k x n", k=8, x=3), ("wgate", 0))
        self.wload(self.wb_pre[:], self.d["w_br"][l, 0].rearrange("p (k n) -> p k n", k=16), ("wbr", 0))
        utok = self.sbt(ph, "utok", [128, NT, 512], BF16)
        pooled = [self.sbt(ph, "pooled", [128, 512], BF16) for _ in range(3)]
        for t in range(NT):
            bk = self.nb()
            self.mm(self.ps[bk], [(self.hT[:, kc, t * 128:(t + 1) * 128], wu[:, kc, :]) for kc in range(8)], bk, reads=["wu", ("hT", t // 4)])
            self.act(utok[:, t, :], self.ps[bk], AF.Identity, reads=[], writes=[("ps", bk), ("utok", t)])
        def pchain(b, g):
            def gen(slot):
                bk, b2 = 2 * slot, 2 * slot + 1
                pl, plk = pooled[slot], ("pooled", slot)
                fns = []
                for tt in range(4):
                    t = 4 * b + tt
                    cur = self.k16("bcur0" if t == 0 else "bcur", g)
                    o_ = self.ps[bk][:, tt * 128:(tt + 1) * 128]
                    if t == 0:
                        fns.append(lambda o_=o_, cur=cur, t=t: nc.tensor.matmul(o_, lhsT=utok[:, t, g * 128:(g + 1) * 128], rhs=cur, start=True, stop=True))
                    else:
                        fns.append(lambda o_=o_, cur=cur, t=t: nc.tensor.matmul(o_, lhsT=utok[:, t, g * 128:(g + 1) * 128], rhs=cur, start=True, stop=False))
                        fns.append(lambda o_=o_, t=t: nc.tensor.matmul(o_, lhsT=utok[:, t - 1, g * 128:(g + 1) * 128], rhs=self.k16("bprev", g), start=False, stop=True))
                kb.group("pe", fns, reads=["c16"] + [("utok", t) for t in range(max(0, 4 * b - 1), 4 * b + 4)], writes=[("ps", bk)])
                yield
                self.dve(lambda: nc.vector.tensor_copy(out=pl[:], in_=self.ps[bk]), [], [("ps", bk), plk])
                yield
                self.mm(self.ps[b2], [(wp[:, g, :], pl[:])], b2, reads=["wpool", plk])
                yield
                self.act(self.o_b[:, g, b * 512:(b + 1) * 512], self.ps[b2], AF.Identity, reads=["vp"], writes=[("ps", b2), "o_b"], scale=self.vpc("pool_scale", g))
                yield
            return gen
        self.run_chains([pchain(b, g) for b in range(NB) for g in range(4)], 3)

    def ssd(self, ph, l, g):
        nc = self.nc
        kb = self.kb
        kb.mark("ssd")
        ws = self.sbt(ph, "wssd", [128, 8, 1288], BF16)
        wssrc = self.d["w_ssd"][l, g].rearrange("p (k n) -> p k n", k=8)
        for j in range(6):
            self.wload(ws[:, :, j * 128:(j + 1) * 128], wssrc[:, :, j * 128:(j + 1) * 128], ("wssd", j))
        self.wload(ws[:, :, 768:1280], wssrc[:, :, 768:1280], ("wssd", "z"))
        self.wload(ws[:, :, 1280:1288], wssrc[:, :, 1280:1288], ("wssd", "dt"))
        sm = self.sbt(ph, "ssdsm", [128, 48], F32)
        kb.dma("sp", sm[:], self.d["ssd_small"][l:l + 1, :].partition_broadcast(128), writes=["ssdsm"])
        abc_ = self.sbt(ph, "ssda", [128, 8], F32)
        self.act(abc_[:], sm[:, 32 + 8 * g:32 + 8 * g + 8], AF.Exp, reads=["ssdsm"], writes=["ssda"])
        self.dve(lambda: nc.vector.tensor_scalar(out=abc_[:], in0=abc_[:], scalar1=-1.0, scalar2=None, op0=ALU.mult), [], ["ssda"])
        Dg = sm[:, 8 * g:8 * g + 8]
        dtb = sm[:, 16 + 8 * g:16 + 8 * g + 8]
        xbc = self.sbt(ph, "xbc", [128, 6, S], BF16)
        zs_all = self.sbt(ph, "zsall", [128, NT, 512], BF16)
        with ExitStack() as s1_:
            raw = self.sbt(s1_, "sraw", [128, 6, 3 + S], BF16)
            cacc = [self.sbt(s1_, "scacc", [128, 512], F32) for _ in range(2)]
            self.dve(lambda: nc.vector.memset(raw[:, :, 0:3], 0.0), [], ["rawpad"])
            dg5 = self.sbt(s1_, "sdiag5", [128, 4, 128], BF16)
            for k in range(4):
                self.dve(lambda k=k: nc.vector.tensor_scalar(out=dg5[:, k, :], in0=self.k16("ident"), scalar1=self.vpc("ssd_conv_w", k * 12 + 10 + g), scalar2=None, op0=ALU.mult),
                         ["c16", "vp"], ["sdiag5"])
            for b in range(NB):
                bs = slice(b * 512, (b + 1) * 512)
                for j in range(6):
                    bk = self.nb()
                    self.mm(self.ps[bk], [(ws[:, kc, j * 128:(j + 1) * 128], self.hT[:, kc, bs]) for kc in range(8)], bk, reads=[("wssd", j), ("hT", b)])
                    self.act(raw[:, j, 3 + b * 512:3 + (b + 1) * 512], self.ps[bk], AF.Identity, reads=[], writes=[("ps", bk), ("sraw", j, b)])
                for j in range(6):
                    ch = (4 * g + j) if j < 4 else (8 + g if j == 4 else 10 + g)
                    rd = [("sraw", j, b), "rawpad", "vp"] + ([("sraw", j, b - 1)] if b > 0 else [])
                    if j == 5:
                        bk = self.nb()
                        self.mm(self.ps[bk], [(dg5[:, k, :], raw[:, j, b * 512 + k:b * 512 + k + 512]) for k in range(4)], bk, reads=rd + ["sdiag5"])
                        self.act(xbc[:, j, bs], self.ps[bk], AF.Silu, reads=["vp"], writes=[("ps", bk), ("xbc", j, b)], bias=self.vpc("ssd_conv_b", ch))
                        continue
                    ca, ck = cacc[(b * 6 + j) % 2], ("scacc", (b * 6 + j) % 2)
                    rv = lambda k, j=j, b=b: raw[:, j, b * 512 + k:b * 512 + k + 512]
                    self.dve(lambda ca=ca, rv=rv, ch=ch: nc.vector.tensor_scalar(out=ca[:], in0=rv(0), scalar1=self.vpc("ssd_conv_w", 0 * 12 + ch), scalar2=self.vpc("ssd_conv_b", ch), op0=ALU.mult, op1=ALU.add), rd, [ck])
                    for k in range(1, 4):
                        self.dve(lambda ca=ca, rv=rv, ch=ch, k=k: nc.vector.scalar_tensor_tensor(out=ca[:], in0=rv(k), scalar=self.vpc("ssd_conv_w", k * 12 + ch), in1=ca[:], op0=ALU.mult, op1=ALU.add), rd, [ck])
                    self.act(xbc[:, j, bs], ca[:], AF.Silu, reads=[ck], writes=[("xbc", j, b)])
                for tt in range(4):
                    t = 4 * b + tt
                    bk = self.nb()
                    self.mm(self.ps[bk], [(self.hT[:, kc, t * 128:(t + 1) * 128], ws[:, kc, 768:1280]) for kc in range(8)], bk, reads=[("wssd", "z"), ("hT", b)])
                    self.act(zs_all[:, t, :], self.ps[bk], AF.Silu, reads=[], writes=[("ps", bk), ("zsall", t)])

            kb.barrier()
        self.dump("xbc%d_%d" % (l, g), xbc[:], [128, 6, S], [("xbc", j, b) for j in range(6) for b in range(NB)], BF16)
        U32 = self.k32("U")
        ones32 = self.k32("ones")
        ident16 = self.k16("ident")
        maskneg4 = self.c16[:, C16["maskneg4"][0]:C16["maskneg4"][0] + 512]
        ones40 = self.k16("ones")[0:40, :]
        sel16 = self.c16[0:64, C16["sel16"][0]:C16["sel16"][0] + 1024].rearrange("p (r m) -> p r m", r=8)
        prev = self.sbt(ph, "sprev", [128, 512], F32)
        prevb = self.sbt(ph, "sprevb", [128, 512], BF16)
        self.dve(lambda: nc.vector.memset(prev[:], 0.0), [], ["sprev"])
        self.dve(lambda: nc.vector.memset(prevb[:], 0.0), [], ["sprevb"])
        Dm = self.sbt(ph, "sDm", [128, 8, 128], BF16)
        for r in range(8):
            self.dve(lambda r=r: nc.vector.tensor_scalar(out=Dm[:, r, :], in0=ident16, scalar1=Dg[:, r:r + 1], scalar2=None, op0=ALU.mult), ["ssdsm", "c16"], ["sDm"])
        dt_all = self.sbt(ph, "sdtall", [128, NT, 8], F32)
        da_all = self.sbt(ph, "sdaall", [128, NT, 8], F32)
        dae_all = self.sbt(ph, "sdaeall", [128, NT, 2, 32], F32)
        acs_all = self.sbt(ph, "sacsall", [128, NT, 8], F32)
        ea_all = self.sbt(ph, "seaall", [128, NT, 8], F32)
        cd_all = self.sbt(ph, "scdall", [128, NT, 8], F32)
        dsd_all = self.sbt(ph, "sdsdall", [128, NT, 8], F32)
        hl_all = self.sbt(ph, "shlall", [64, NT, 128], BF16)
        nhl_all = self.sbt(ph, "snhlall", [64, NT, 128], BF16)
        hc_q = self.sbt(ph, "shcq", [64, 4, 128], BF16)
        self.dve(lambda: nc.vector.memset(dae_all[:], 0.0), [], ["sdae"])
        self.dve(lambda: nc.vector.memset(hl_all[:], 0.0), [], ["shl"])
        bdt = self.nb()
        fns = []
        for t in range(NT):
            for kc in range(8):
                fns.append(lambda t=t, kc=kc: nc.tensor.matmul(self.ps[bdt][:, t * 8:(t + 1) * 8], lhsT=self.hT[:, kc, t * 128:(t + 1) * 128], rhs=ws[:, kc, 1280:1288], start=(kc == 0), stop=(kc == 7)))
        kb.group("pe", fns, reads=[("wssd", "dt")] + [("hT", b) for b in range(NB)], writes=[("ps", bdt)])
        f2 = lambda a: a.rearrange("p t r -> p (t r)")
        self.dve(lambda: nc.vector.tensor_tensor(out=dt_all[:], in0=self.ps[bdt][:, 0:128].rearrange("p (t r) -> p t r", r=8), in1=dtb.unsqueeze(1).to_broadcast([128, NT, 8]), op=ALU.add), ["ssdsm"], [("ps", bdt), "sdt"])
        self.act(f2(dt_all[:]), f2(dt_all[:]), AF.Exp, reads=[], writes=["sdt"])
        self.act(f2(dt_all[:]), f2(dt_all[:]), AF.Ln, reads=[], writes=["sdt"], bias=1.0)
        self.dve(lambda: nc.vector.tensor_tensor(out=da_all[:], in0=dt_all[:], in1=abc_[:].unsqueeze(1).to_broadcast([128, NT, 8]), op=ALU.mult), ["sdt", "ssda"], ["sda"])
        for hh in range(2):
            self.dve(lambda hh=hh: nc.vector.tensor_copy(out=dae_all[:, :, hh, 0:8], in_=da_all[:]), ["sda"], ["sdae"])
        bcs = self.nb()
        kb.group("pe", [lambda: nc.tensor.matmul(self.ps[bcs][:, 0:128], lhsT=U32, rhs=f2(da_all[:]), start=True, stop=True),
                        lambda: nc.tensor.matmul(self.ps[bcs][:, 128:256], lhsT=ones32, rhs=f2(da_all[:]), start=True, stop=True)],
                 reads=["sda", "c32"], writes=[("ps", bcs)])
        self.act(f2(acs_all[:]), self.ps[bcs][:, 0:128], AF.Identity, reads=[], writes=[("ps", bcs), "sacs"])
        self.act(f2(ea_all[:]), self.ps[bcs][:, 0:128], AF.Exp, reads=[], writes=[("ps", bcs), "sea"])
        self.act(f2(cd_all[:]), self.ps[bcs][:, 128:256], AF.Exp, reads=[], writes=[("ps", bcs), "scd"])
        self.dve(lambda: nc.vector.tensor_sub(out=f2(dsd_all[:]), in0=self.ps[bcs][:, 128:256], in1=f2(acs_all[:])), ["sacs"], [("ps", bcs), "sdsd"])
        self.act(f2(dsd_all[:]), f2(dsd_all[:]), AF.Exp, reads=[], writes=["sdsd"])
        for q4 in range(4):
            bq = self.nb()
            fns = []
            for tt in range(4):
                t = 4 * q4 + tt
                lhs = dae_all[:, t].rearrange("p a b -> p (a b)")[:, 0:40]
                fns.append(lambda tt=tt, lhs=lhs, bq=bq: nc.tensor.matmul(self.ps[bq][0:40, tt * 128:(tt + 1) * 128], lhsT=lhs, rhs=U32, start=True, stop=True))
            kb.group("pe", fns, reads=["sdae", "c32"], writes=[("ps", bq)])
            tsl = slice(4 * q4, 4 * q4 + 4)
            pv = lambda lo, hi, bq=bq: self.ps[bq][lo:hi, :].rearrange("p (t m) -> p t m", t=4)
            self.act(hl_all[0:8, tsl, :], pv(0, 8), AF.Identity, reads=[], writes=[("ps", bq), "shl"])
            self.act(hc_q[32:40, :, :], pv(32, 40), AF.Identity, reads=[], writes=[("ps", bq), "shc"])
            self.dve(lambda tsl=tsl, pv=pv: nc.vector.tensor_sub(out=hl_all[32:40, tsl, :], in0=pv(32, 40), in1=hc_q[32:40, :, :]), ["shc"], [("ps", bq), "shl"])
        self.dve(lambda: nc.vector.tensor_scalar(out=nhl_all[0:40], in0=hl_all[0:40], scalar1=-1.0, scalar2=None, op0=ALU.mult), ["shl"], ["snhl"])

        R = 2

        def rot(name, shape, dt):
            return [self.sbt(ph, name, shape, dt) for _ in range(R)]
        one = lambda name, shape, dt: [self.sbt(ph, name, shape, dt)] * R
        bsel = one("sbsel", [64, 8, 128], BF16)
        xsb = one("sxsb", [128, 512], BF16)
        Eexp = one("sE", [128, 8, 128], BF16)
        Mt = Eexp
        cbT = one("scbT", [128, 128], BF16)
        xdt = one("sxdt", [128, 512], BF16)
        xdt2 = rot("sxdt2", [128, 512], BF16)
        Btok = rot("sBtok", [128, 128], BF16)
        yb = one("sy", [128, 512], F32)
        gt = one("sgt", [128, 512], BF16)
        junk = self.sbt(ph, "sjunk", [128, 512], BF16)
        ssq = one("sssq", [128, 1], F32)
        x3 = lambda a: a.rearrange("p (r d) -> p r d", r=8)
        bc8 = lambda a: a.unsqueeze(2).to_broadcast([128, 8, 64])
        nwbc = self.sbt(ph, "snwbc", [128, 512], F32)
        kb.dma("sp", nwbc[:], self.d["ssd_nw"][l:l + 1, 512 * g:512 * (g + 1)].partition_broadcast(128), writes=["snwbc"])
        psbA = self.ps[6].bitcast(BF16)
        poolA = {"i": 0}
        poolB = {"i": 0}

        def nbA():
            poolA["i"] ^= 1
            return poolA["i"]

        def nbB():
            poolB["i"] ^= 1
            return 2 + poolB["i"]

        SINGLE = {"sbsel", "sxsb", "sE", "sM", "scbT", "sxdt", "sy", "sgt", "sssq"}

        def stageA(t):
            i = t % R
            ts_ = slice(t * 128, (t + 1) * 128)
            b = t // 4
            K = lambda n: ("sE", 0) if n == "sM" else ((n, 0) if n in SINGLE else (n, i))
            by = 4 + i
            kb.group("pe", [lambda j=j: nc.tensor.transpose(psbA[:, j * 128:(j + 1) * 128], xbc[:, j, ts_], ident16) for j in range(5)],
                     reads=[("xbc", j, b) for j in range(5)] + ["c16"], writes=[("ps", 6)])
            bcb = nbA()
            self.mm(self.ps[bcb][:, 0:128], [(xbc[:, 4, ts_], xbc[:, 5, ts_])], bcb, reads=[("xbc", 4, b), ("xbc", 5, b)])
            yield
            self.dve(lambda: nc.vector.tensor_tensor(out=bsel[i][0:40], in0=sel16[0:40], in1=hl_all[0:40, t, :].unsqueeze(1).to_broadcast([40, 8, 128]), op=ALU.mult), ["shl", "c16"], [K("sbsel")])
            yield
            self.act(cbT[i][:], self.ps[bcb][:, 0:128], AF.Identity, reads=[], writes=[("ps", bcb), K("scbT")])
            self.act(xsb[i][:], psbA[:, 0:512], AF.Identity, reads=[], writes=[("ps", 6), K("sxsb")])
            self.act(Btok[i][:], psbA[:, 512:640], AF.Identity, reads=[], writes=[("ps", 6), K("sBtok")])
            yield
            bp = [nbA(), nbA()]
            for half in range(2):
                hsl = slice(half * 4, (half + 1) * 4)
                fns = [lambda hsl=hsl, half=half: nc.tensor.matmul(self.ps[bp[half]], lhsT=ones40, rhs=bsel[i][0:40, hsl, :].rearrange("p r m -> p (r m)"), start=True, stop=False),
                       lambda hsl=hsl, half=half: nc.tensor.matmul(self.ps[bp[half]], lhsT=nhl_all[0:40, t, :], rhs=sel16[0:40, hsl, :].rearrange("p r m -> p (r m)"), start=False, stop=False),
                       lambda half=half: nc.tensor.matmul(self.ps[bp[half]], lhsT=ident16, rhs=maskneg4, start=False, stop=True)]
                kb.group("pe", fns, reads=[K("sbsel"), "snhl", "c16"], writes=[("ps", bp[half])])
                yield
            self.dve(lambda: nc.vector.tensor_tensor(out=x3(xdt[i][:]), in0=x3(xsb[i][:]), in1=bc8(dt_all[:, t, :]), op=ALU.mult), [K("sxsb"), "sdt"], [K("sxdt")])
            yield
            for half in range(2):
                hsl = slice(half * 4, (half + 1) * 4)
                ek = ("sE", half)
                self.act(Eexp[i][:, hsl, :], self.ps[bp[half]].rearrange("p (r m) -> p r m", r=4), AF.Exp, reads=[], writes=[("ps", bp[half]), ek])
                yield
                self.dve(lambda hsl=hsl: nc.vector.tensor_mul(out=Mt[i][:, hsl, :], in0=Eexp[i][:, hsl, :], in1=cbT[i][:].unsqueeze(1).to_broadcast([128, 4, 128])), [K("scbT")], [ek])
                yield
                fns = []
                for r in range(half * 4, half * 4 + 4):
                    fns.append(lambda r=r: nc.tensor.matmul(self.ps[by][:, r * 64:(r + 1) * 64], lhsT=Mt[i][:, r, :], rhs=xdt[i][:, r * 64:(r + 1) * 64], start=True, stop=False))
                    fns.append(lambda r=r: nc.tensor.matmul(self.ps[by][:, r * 64:(r + 1) * 64], lhsT=Dm[:, r, :], rhs=xsb[i][:, r * 64:(r + 1) * 64], start=False, stop=True))
                kb.group("pe", fns, reads=[ek, K("sxdt"), K("sxsb"), "sDm"], writes=[("ps", by)])
                yield
            self.dve(lambda: nc.vector.tensor_tensor(out=x3(xdt2[i][:]), in0=x3(xdt[i][:]), in1=bc8(dsd_all[:, t, :]), op=ALU.mult), [K("sxdt"), "sdsd"], [K("sxdt2")])
            yield

        def stageB(t):
            i = t % R
            ts_ = slice(t * 128, (t + 1) * 128)
            b = t // 4
            K = lambda n: ("sE", 0) if n == "sM" else ((n, 0) if n in SINGLE else (n, i))
            by = 4 + i
            bo = nbB()
            self.mm(self.ps[bo], [(xbc[:, 5, ts_], prevb[:])], bo, reads=[("xbc", 5, b), "sprevb"])
            bst = nbB()
            self.mm(self.ps[bst], [(Btok[i][:], xdt2[i][:])], bst, reads=[K("sBtok"), K("sxdt2")])
            yield
            self.dve(lambda: nc.vector.tensor_tensor(out=x3(prev[:]), in0=x3(prev[:]), in1=bc8(cd_all[:, t, :]), op=ALU.mult), ["scd"], ["sprev"])
            yield
            self.dve(lambda: nc.vector.tensor_add(out=prev[:], in0=prev[:], in1=self.ps[bst]), [], [("ps", bst), "sprev"])
            yield
            self.dve(lambda: nc.vector.tensor_copy(out=prevb[:], in_=prev[:]), ["sprev"], ["sprevb"])
            yield
            self.dve(lambda: nc.vector.tensor_tensor(out=x3(yb[i][:]), in0=x3(self.ps[bo]), in1=bc8(ea_all[:, t, :]), op=ALU.mult), ["sea"], [("ps", bo), K("sy")])
            yield
            self.dve(lambda: nc.vector.tensor_add(out=yb[i][:], in0=yb[i][:], in1=self.ps[by]), [], [("ps", by), K("sy")])
            yield
            self.dve(lambda: nc.vector.tensor_mul(out=yb[i][:], in0=yb[i][:], in1=zs_all[:, t, :]), [], [K("sy")])
            yield
            self.act(junk[:], yb[i][:], AF.Square, reads=[K("sy")], writes=["sjunk", K("sssq")], accum=ssq[i][:])
            self.rstd(ssq[i][:], ssq[i][:], 1.0 / 512, EPS, [], K("sssq"))
            yield
            self.dve(lambda: nc.vector.scalar_tensor_tensor(out=gt[i][:], in0=yb[i][:], scalar=ssq[i][:, 0:1], in1=nwbc[:], op0=ALU.mult, op1=ALU.mult), [K("sy"), K("sssq"), "snwbc"], [K("sgt")])
            yield
            kb.group("pe", [lambda j=j: nc.tensor.transpose(self.psb[:, j * 128:(j + 1) * 128], gt[i][:, j * 128:(j + 1) * 128], ident16) for j in range(4)],
                     reads=[K("sgt"), "c16"], writes=[("ps", 7)])
            yield
            self.act(self.o_c[:, 4 * g:4 * g + 4, ts_], self.psb[:, 0:512].rearrange("p (c n) -> p c n", c=4), AF.Identity, reads=[], writes=[("ps", 7), "o_c"])
            yield

        def interleave(ga, gb, ra=1, rb=1):
            alive = [ga, gb]
            while alive:
                for g_, n_ in ((ga, ra), (gb, rb)):
                    if g_ in alive:
                        for _ in range(n_):
                            try:
                                next(g_)
                            except StopIteration:
                                alive.remove(g_)
                                break

        for _ in stageA(0):
            pass
        for t in range(NT):
            if t + 1 < NT:
                interleave(stageA(t + 1), stageB(t))
            else:
                for _ in stageB(t):
                    pass

    def merge(self, ph, l):
        nc = self.nc
        kb = self.kb
        kb.mark("merge")
        wg = [self.wg_pre, self.sbt(ph, "wgate", [128, 8, 3, 128], BF16)]
        wb = [self.wb_pre, self.sbt(ph, "wbr", [128, 16, 128], BF16)]
        sig = [self.sbt(ph, "msig", [128, 512], F32) for _ in range(6)]
        acc = [self.sbt(ph, "macc", [128, 512], F32) for _ in range(2)]
        srcs = [(self.o_a, 4, 0, "o_a"), (self.o_b, 4, 4, "o_b"), (self.o_c, 8, 8, "o_c")]
        wo = self.wo
        wosrc = self.d["w_o"][l].rearrange("p (k n) -> p k n", k=8)
        si = 0
        ai = 0
        for m in range(8):
            g_, gk = wg[m % 2], ("wgate", m % 2)
            b_, bk_ = wb[m % 2], ("wbr", m % 2)
            if m > 0:
                self.wload(g_[:], self.d["w_gate"][l, m].rearrange("p (k x n) -> p k x n", k=8, x=3), gk)
                self.wload(b_[:], self.d["w_br"][l, m].rearrange("p (k n) -> p k n", k=16), bk_)
            if m == 1:
                for half in range(2):
                    self.wload(wo[:, :, half * 512:(half + 1) * 512], wosrc[:, :, half * 512:(half + 1) * 512], ("wo", half))
            if m == 6:
                for half in range(2):
                    for kc in range(8):
                        self.dve(lambda kc=kc, half=half: nc.vector.tensor_tensor(out=wo[:, kc, half * 512:(half + 1) * 512], in0=wo[:, kc, half * 512:(half + 1) * 512], in1=self.gbc[:, 0, half * 512:(half + 1) * 512], op=ALU.mult),
                                 [("gbc", 0)], [("wo", half)])
            for b in range(NB):
                bs = slice(b * 512, (b + 1) * 512)
                ac, ack = acc[ai % 2], ("macc", ai % 2)
                ai += 1
                for x, (src, nch, off, skey) in enumerate(srcs):
                    bg = self.nb()
                    self.mm(self.ps[bg], [(g_[:, kc, x, :], self.hT[:, kc, bs]) for kc in range(8)], bg, reads=[gk, ("hT", b)])
                    sg, sgk = sig[si % 6], ("msig", si % 6)
                    si += 1
                    self.act(sg[:], self.ps[bg], AF.Sigmoid, reads=[], writes=[("ps", bg), sgk])
                    by = self.nb()
                    self.mm(self.ps[by], [(b_[:, off + c, :], src[:, c, bs]) for c in range(nch)], by, reads=[bk_, skey])
                    if x == 0:
                        self.dve(lambda ac=ac, sg=sg, by=by: nc.vector.tensor_mul(out=ac[:], in0=sg[:], in1=self.ps[by]), [sgk], [("ps", by), ack])
                    else:
                        self.dve(lambda sg=sg, by=by: nc.vector.tensor_mul(out=sg[:], in0=sg[:], in1=self.ps[by]), [], [("ps", by), sgk])
                        if x == 1:
                            self.dve(lambda ac=ac, sg=sg: nc.vector.tensor_add(out=ac[:], in0=ac[:], in1=sg[:]), [sgk], [ack])
                        else:
                            self.dve(lambda ac=ac, sg=sg, m=m, bs=bs: nc.vector.tensor_add(out=self.mergedT[:, m, bs], in0=ac[:], in1=sg[:]), [sgk, ack], ["merged"])

    def wout_phase(self, ph, l, xin, xout1):
        nc = self.nc
        kb = self.kb
        kb.mark("wout_phase")
        wo = self.wo
        xn = [self.sbt(ph, "xnew", [128, D], F32) for _ in range(2)]

        def compute(t):
            ts_ = slice(t * 128, (t + 1) * 128)
            xt, xkeys = self.load_x(ph, xin, t)
            xo, xok = xn[t % 2], ("xnew", t % 2)
            for half in range(2):
                hs = slice(half * 512, (half + 1) * 512)
                bk = self.nb()
                self.mm(self.ps[bk], [(self.mergedT[:, kc, ts_], wo[:, kc, hs]) for kc in range(8)], bk, reads=[("wo", half), "merged"])
                self.dve(lambda xo=xo, xt=xt, bk=bk, hs=hs: nc.vector.tensor_add(out=xo[:, hs], in0=self.ps[bk], in1=xt[:, hs]), list(xkeys), [("ps", bk), (xok, half)])
            kb.dma("sp", xout1[ts_, :], xo[:], reads=[(xok, 0), (xok, 1)], writes=[("xo1", t)])
            if t == 0:
                self.dump("xo0_%d" % l, xo[:], [128, D], [(xok, 0), (xok, 1)])
                self.dump("wo_%d" % l, wo[:], [128, 8, D], [("wo", 0), ("wo", 1)], BF16)
            self.norm_a(ph, (xo, [(xok, 0), (xok, 1)]), t)
        compute(0)
        for t in range(NT):
            if t + 1 < NT:
                compute(t + 1)
            self.norm_b(self.s2, self.b2, t, skey=("s2",), bkey=(("modp", 3),), mixed=True)

    def ffn(self, ph, l, xmid, xout2):
        nc = self.nc
        kb = self.kb
        kb.mark("ffn")
        actT = self.sbt(ph, "actT", [128, NJ, S], BF16)
        wd0 = self.sbt(ph, "wdn", [128, NJ, 512], BF16)
        wdsrc = self.d["w_dn"][l].rearrange("p (k n) -> p k n", k=NJ)
        with ExitStack() as up:
            wup = [self.sbt(up, "wup", [128, 8, 256], BF16) for _ in range(2)]
            raw = [self.sbt(up, "fraw", [128, 2, 2 + S], BF16) for _ in range(2)]
            dg = [self.sbt(up, "fdiag", [128, 3, 2, 128], BF16) for _ in range(2)]
            sg = [self.sbt(up, "fsil", [128, 512], F32) for _ in range(3)]
            vacc = [self.sbt(up, "fvacc", [128, 512], F32) for _ in range(2)]
            si = 0
            for p_ in range(2):
                self.dve(lambda p_=p_: nc.vector.memset(raw[p_][:, :, 0:2], 0.0), [], [("frawpad", p_)])
            def prep(j):
                p_ = j % 2
                self.wload(wup[p_][:], self.d["w_up"][l, j].rearrange("p (k n) -> p k n", k=8), ("wup", p_))
                if j == 2:
                    self.wload(wd0[:], wdsrc[:, :, 0:512], ("wdn", 0))
                for k in range(3):
                    self.dve(lambda k=k, p_=p_: nc.vector.tensor_scalar(out=dg[p_][:, k, 0, :], in0=self.k16("ident"), scalar1=self.vpc("ffn_conv_w", k * 44 + j), scalar2=None, op0=ALU.mult),
                             ["c16", "vp"], [("fdiag", p_)])

            def proj(step):
                j, b = divmod(step, NB)
                p_ = j % 2
                w_, wk = wup[p_], ("wup", p_)
                bs = slice(b * 512, (b + 1) * 512)
                for x in range(2):
                    bk = 2 * (step % 2) + x
                    self.mm(self.ps[bk], [(w_[:, kc, x * 128:(x + 1) * 128], self.hT[:, kc, bs]) for kc in range(8)], bk, reads=[wk, ("hT", b)])
                    if x == 0:
                        self.act(raw[p_][:, x, 2 + b * 512:2 + (b + 1) * 512], self.ps[bk], AF.Identity, reads=[], writes=[("ps", bk), ("fraw", p_, x, b)])
                    else:
                        self.dve(lambda p_=p_, x=x, b=b, bk=bk: nc.vector.tensor_copy(out=raw[p_][:, x, 2 + b * 512:2 + (b + 1) * 512], in_=self.ps[bk]), [], [("ps", bk), ("fraw", p_, x, b)])

            def conv(step):
                nonlocal si
                j, b = divmod(step, NB)
                p_ = j % 2
                bs = slice(b * 512, (b + 1) * 512)
                bk = 4 + (step % 2)
                rd = [("fdiag", p_), ("fraw", p_, 0, b), ("frawpad", p_)] + ([("fraw", p_, 0, b - 1)] if b > 0 else [])
                self.mm(self.ps[bk], [(dg[p_][:, k, 0, :], raw[p_][:, 0, b * 512 + k:b * 512 + k + 512]) for k in range(3)], bk, reads=rd)
                s_, sk = sg[si % 3], ("fsil", si % 3)
                va, vk = vacc[si % 2], ("fvacc", si % 2)
                si += 1
                self.act(s_[:], self.ps[bk], AF.Silu, reads=["vp"], writes=[("ps", bk), sk], bias=self.vpc("ffn_conv_b", j))
                rdv = [("fraw", p_, 1, b), ("frawpad", p_), "vp"] + ([("fraw", p_, 1, b - 1)] if b > 0 else [])
                rv = lambda k: raw[p_][:, 1, b * 512 + k:b * 512 + k + 512]
                chv = NJ + j
                self.dve(lambda: nc.vector.tensor_scalar(out=va[:], in0=rv(0), scalar1=self.vpc("ffn_conv_w", 0 * 44 + chv), scalar2=self.vpc("ffn_conv_b", chv), op0=ALU.mult, op1=ALU.add), rdv, [vk])
                self.dve(lambda: nc.vector.scalar_tensor_tensor(out=va[:], in0=rv(1), scalar=self.vpc("ffn_conv_w", 1 * 44 + chv), in1=va[:], op0=ALU.mult, op1=ALU.add), rdv, [vk])
                self.dve(lambda: nc.vector.scalar_tensor_tensor(out=va[:], in0=rv(2), scalar=self.vpc("ffn_conv_w", 2 * 44 + chv), in1=va[:], op0=ALU.mult, op1=ALU.add), rdv, [vk])
                self.dve(lambda: nc.vector.tensor_mul(out=actT[:, j, bs], in0=va[:], in1=s_[:]), [sk, vk], [("actT", b)])
            nsteps = NJ * NB
            prep(0)
            proj(0)
            for step in range(nsteps):
                if step + 1 < nsteps:
                    if (step + 1) % NB == 0:
                        prep((step + 1) // NB)
                    proj(step + 1)
                conv(step)
            kb.barrier()
        self.dump("actT%d" % l, actT[:], [128, NJ, S], [("actT", b) for b in range(NB)], BF16)
        with ExitStack() as dn:
            wd = [wd0, self.sbt(dn, "wdn", [128, NJ, 512], BF16)]
            xh = [self.sbt(dn, "xh", [128, 512], F32) for _ in range(2)]
            xn = [self.sbt(dn, "xnew2", [128, 512], F32) for _ in range(2)]
            wsrc = self.d["w_dn"][l].rearrange("p (k n) -> p k n", k=NJ)
            self.wload(wd[1][:], wsrc[:, :, 512:1024], ("wdn", 1))
            ci = 0
            for half in range(2):
                hs = slice(half * 512, (half + 1) * 512)
                for t in range(NT):
                    ts_ = slice(t * 128, (t + 1) * 128)
                    i = ci % 2
                    ci += 1
                    kb.dma("sp", xh[i][:], xmid[ts_, hs], reads=[("xo1", t)] if False else [], writes=[("xh", i)])
                    bk = self.nb()
                    self.mm(self.ps[bk], [(actT[:, j, ts_], wd[half][:, j, :]) for j in range(NJ)], bk, reads=[("wdn", half), ("actT", t // 4)])
                    self.dve(lambda i=i, bk=bk, hs=hs: nc.vector.tensor_mul(out=xn[i][:], in0=self.ps[bk], in1=self.gbc[:, 1, hs]), [("gbc", 1)], [("ps", bk), ("xnew2", i)])
                    self.dve(lambda i=i: nc.vector.tensor_add(out=xn[i][:], in0=xn[i][:], in1=xh[i][:]), [("xh", i)], [("xnew2", i)])
                    kb.dma("sp", xout2[ts_, hs], xn[i][:], reads=[("xnew2", i)], writes=[("xo2", t, half)])
            kb.barrier()


_CACHE = {}


def make_in_maps(inputs, n_cores=8):
    shared = host_layout(inputs)
    x = np.asarray(inputs["x"], np.float32)
    c = np.asarray(inputs["c"], np.float32)
    pos = np.asarray(inputs["positions"], np.int32)
    maps = []
    for b in range(n_cores):
        m = dict(shared)
        m["x"] = np.ascontiguousarray(x[b])
        m["cT"] = np.ascontiguousarray(c[b].reshape(8, 128).T)
        m["pos"] = np.ascontiguousarray(pos[b:b + 1])
        maps.append(m)
    return maps


def kernel(**inputs):
    maps = make_in_maps(inputs, 8)
    prog = Prog()
    nc = prog.build()
    res = run_bass_kernel_spmd(nc, maps, core_ids=list(range(8)))
    out = np.stack([np.asarray(res.results[b]["out"], np.float32).reshape(S, D) for b in range(8)], axis=0)
    return out
```
